# Optimizing a Trainium2 kernel written in Bass

```python
import jax, jax.numpy as jnp
from jax import lax
import numpy as np

D_MODEL = 2048
BATCH = 2
SEQ = 8192
DEPTH = 2

HEAD_DIM = 128
FOX_HEADS = 8
FOX_WIDTH = FOX_HEADS * HEAD_DIM
DSA_HEADS = 8
DSA_KV_HEADS = 2
DSA_WIDTH = DSA_HEADS * HEAD_DIM
DSA_KV_WIDTH = DSA_KV_HEADS * HEAD_DIM
IDX_HEADS = 16
IDX_DIM = 64
TOPK_MAX = 256
SSD_HEADS = 32
SSD_HEAD_DIM = 64
SSD_WIDTH = SSD_HEADS * SSD_HEAD_DIM
SSD_GROUPS = 4
SSD_STATE = 128
SSD_CHUNK = 128
CONV_WIDTH = 4
CONV_CH = SSD_WIDTH + 2 * SSD_GROUPS * SSD_STATE
N_BRANCH = 3
Q_BLOCK = 128
ROPE_THETA = 500000.0
ROPE_FRACTION = 4
EPS = 1e-6

SPLIT_SIZES = (
    FOX_WIDTH, FOX_WIDTH, FOX_WIDTH, FOX_HEADS, FOX_WIDTH,
    DSA_WIDTH, DSA_KV_WIDTH, DSA_KV_WIDTH,
    IDX_HEADS * IDX_DIM, IDX_DIM, IDX_HEADS, DSA_WIDTH,
    SSD_WIDTH, CONV_CH, SSD_HEADS,
    N_BRANCH * D_MODEL,
)
N_IN = sum(SPLIT_SIZES)

kernel_name = "fox_dsa_ssd_gated_hybrid"

F32 = jnp.float32


def rms_norm(x, w):
    xf = x.astype(F32)
    y = xf * lax.rsqrt(jnp.mean(xf * xf, axis=-1, keepdims=True) + EPS)
    return (y * w.astype(F32)).astype(x.dtype)


def partial_rope(x, positions):
    d = x.shape[-1]
    rd = d // ROPE_FRACTION
    half = rd // 2
    inv_freq = ROPE_THETA ** (-jnp.arange(half, dtype=F32) / half)
    ang = positions.astype(F32)[..., None] * inv_freq
    cos = jnp.cos(ang)[:, :, None, :]
    sin = jnp.sin(ang)[:, :, None, :]
    xr = x[..., :rd].astype(F32)
    x1, x2 = xr[..., :half], xr[..., half:]
    rot = jnp.concatenate([x1 * cos - x2 * sin, x2 * cos + x1 * sin], axis=-1)
    return jnp.concatenate([rot.astype(x.dtype), x[..., rd:]], axis=-1)


def fox_attention(q, k, v, log_f):
    B, S, H, d = q.shape
    nb = S // Q_BLOCK
    scale = d ** -0.5
    F = jnp.cumsum(log_f, axis=1)
    Fk = F.transpose(0, 2, 1)
    kpos = jnp.arange(S)
    qb = q.reshape(B, nb, Q_BLOCK, H, d).transpose(1, 0, 2, 3, 4)
    Fqb = F.reshape(B, nb, Q_BLOCK, H).transpose(1, 0, 3, 2)
    qposb = jnp.arange(S).reshape(nb, Q_BLOCK)

    def block(args):
        qi, Fq, qpos = args
        logits = jnp.einsum("bqhd,bkhd->bhqk", qi, k).astype(F32) * scale
        logits = logits + Fq[..., None] - Fk[:, :, None, :]
        logits = jnp.where(kpos[None, :] <= qpos[:, None], logits, -jnp.inf)
        p = jax.nn.softmax(logits, axis=-1).astype(v.dtype)
        return jnp.einsum("bhqk,bkhd->bqhd", p, v)

    out = lax.map(block, (qb, Fqb, qposb))
    return out.transpose(1, 0, 2, 3, 4).reshape(B, S, H, d)


def dsa_attention(q, k, v, q_idx, k_idx, w_idx):
    B, S, Hq, d = q.shape
    Hkv = k.shape[2]
    rep = Hq // Hkv
    topk = min(TOPK_MAX, S // 4)
    nb = S // Q_BLOCK
    scale = d ** -0.5
    idx_scale = (IDX_DIM ** -0.5) * (IDX_HEADS ** -0.5)
    kpos = jnp.arange(S)
    k_idx_f = k_idx.astype(F32)
    qb = q.reshape(B, nb, Q_BLOCK, Hkv, rep, d).transpose(1, 0, 2, 3, 4, 5)
    qib = q_idx.reshape(B, nb, Q_BLOCK, IDX_HEADS, IDX_DIM).transpose(1, 0, 2, 3, 4)
    wb = w_idx.reshape(B, nb, Q_BLOCK, IDX_HEADS).transpose(1, 0, 2, 3)
    qposb = jnp.arange(S).reshape(nb, Q_BLOCK)

    def block(args):
        qi, qI, wI, qpos = args
        s_h = jax.nn.relu(jnp.einsum("bqhe,bke->bqhk", qI.astype(F32), k_idx_f))
        score = jnp.einsum("bqh,bqhk->bqk", wI.astype(F32), s_h) * idx_scale
        score = jnp.where(kpos[None, None, :] <= qpos[None, :, None], score, -jnp.inf)
        _, sel = lax.top_k(score, topk)
        valid = sel <= qpos[None, :, None]
        flat = sel.reshape(B, Q_BLOCK * topk)[:, :, None, None]
        k_sel = jnp.take_along_axis(k, flat, axis=1).reshape(B, Q_BLOCK, topk, Hkv, d)
        v_sel = jnp.take_along_axis(v, flat, axis=1).reshape(B, Q_BLOCK, topk, Hkv, d)
        logits = jnp.einsum("bqgrd,bqkgd->bqgrk", qi, k_sel).astype(F32) * scale
        logits = jnp.where(valid[:, :, None, None, :], logits, -jnp.inf)
        p = jax.nn.softmax(logits, axis=-1).astype(v.dtype)
        return jnp.einsum("bqgrk,bqkgd->bqgrd", p, v_sel)

    out = lax.map(block, (qb, qib, wb, qposb))
    return out.transpose(1, 0, 2, 3, 4, 5).reshape(B, S, Hq, d)


def segsum(a):
    T = a.shape[-1]
    cs = jnp.cumsum(a, axis=-1)
    diff = cs[..., :, None] - cs[..., None, :]
    mask = jnp.tril(jnp.ones((T, T), dtype=bool))
    return jnp.where(mask, diff, -jnp.inf)


def ssd_scan(x, dt, a, b, c):
    Bsz, S, H, P = x.shape
    G, N = b.shape[2], b.shape[3]
    R = H // G
    L = SSD_CHUNK
    nc = S // L
    xd = (x.astype(F32) * dt[..., None]).reshape(Bsz, nc, L, G, R, P)
    da = (dt * a).reshape(Bsz, nc, L, G, R).transpose(0, 1, 3, 4, 2)
    bc = b.astype(F32).reshape(Bsz, nc, L, G, N)
    cc = c.astype(F32).reshape(Bsz, nc, L, G, N)
    a_cs = jnp.cumsum(da, axis=-1)
    decay_in = jnp.exp(segsum(da))
    cb = jnp.einsum("bclgn,bcsgn->bcgls", cc, bc)
    y_diag = jnp.einsum("bcgls,bcgrls,bcsgrp->bclgrp", cb, decay_in, xd)
    decay_to_end = jnp.exp(a_cs[..., -1:] - a_cs)
    chunk_states = jnp.einsum("bclgn,bcgrl,bclgrp->bcgrpn", bc, decay_to_end, xd)
    chunk_decay = jnp.exp(a_cs[..., -1])

    def step(state, inp):
        st_c, dec_c = inp
        return state * dec_c[..., None, None] + st_c, state

    init = jnp.zeros((Bsz, G, R, P, N), F32)
    _, prev = lax.scan(step, init, (chunk_states.transpose(1, 0, 2, 3, 4, 5),
                                    chunk_decay.transpose(1, 0, 2, 3)))
    prev = prev.transpose(1, 0, 2, 3, 4, 5)
    y_off = jnp.einsum("bclgn,bcgrpn,bcgrl->bclgrp", cc, prev, jnp.exp(a_cs))
    return (y_diag + y_off).reshape(Bsz, S, H, P)


def causal_depthwise_conv(x, w, b):
    K, C = w.shape
    out = lax.conv_general_dilated(
        x, w[:, None, :].astype(x.dtype), window_strides=(1,), padding=((K - 1, 0),),
        dimension_numbers=("NWC", "WIO", "NWC"), feature_group_count=C)
    return out + b.astype(x.dtype)


def hybrid_layer(x, c, positions, norm_w, w_ada, b_ada, w_in, b_fox_f, fox_q_norm, fox_k_norm,
                 dsa_q_norm, dsa_k_norm, conv_w, conv_b, dt_bias, a_log, d_skip, ssd_norm,
                 b_merge, w_o_fox, w_o_dsa, w_o_ssd, w_out):
    B, S, D = x.shape
    dtype = x.dtype
    mod = jnp.dot(jax.nn.silu(c), w_ada) + b_ada
    shift, scale, gate = jnp.split(mod, 3, axis=-1)
    u = rms_norm(x, norm_w) * (1.0 + scale[:, None, :]) + shift[:, None, :]

    proj = jnp.dot(u, w_in)
    split_points = [int(p) for p in np.cumsum(SPLIT_SIZES)[:-1]]
    (fq, fk, fv, ff, fg, dq, dk, dv, iq, ik, iw, dg,
     sz, sxbc, sdt, mg) = jnp.split(proj, split_points, axis=-1)

    fq = rms_norm(fq.reshape(B, S, FOX_HEADS, HEAD_DIM), fox_q_norm)
    fk = rms_norm(fk.reshape(B, S, FOX_HEADS, HEAD_DIM), fox_k_norm)
    fv = fv.reshape(B, S, FOX_HEADS, HEAD_DIM)
    log_f = jax.nn.log_sigmoid(ff.astype(F32) + b_fox_f.astype(F32))
    y_fox = fox_attention(fq, fk, fv, log_f).reshape(B, S, FOX_WIDTH) * jax.nn.silu(fg)

    dq = partial_rope(rms_norm(dq.reshape(B, S, DSA_HEADS, HEAD_DIM), dsa_q_norm), positions)
    dk = partial_rope(rms_norm(dk.reshape(B, S, DSA_KV_HEADS, HEAD_DIM), dsa_k_norm), positions)
    dv = dv.reshape(B, S, DSA_KV_HEADS, HEAD_DIM)
    iq = partial_rope(iq.reshape(B, S, IDX_HEADS, IDX_DIM), positions)
    ik = partial_rope(ik.reshape(B, S, 1, IDX_DIM), positions)[:, :, 0, :]
    y_dsa = dsa_attention(dq, dk, dv, iq, ik, iw).reshape(B, S, DSA_WIDTH) * jax.nn.silu(dg)

    xbc = jax.nn.silu(causal_depthwise_conv(sxbc, conv_w, conv_b))
    xs, bs, cs = jnp.split(xbc, [SSD_WIDTH, SSD_WIDTH + SSD_GROUPS * SSD_STATE], axis=-1)
    xs = xs.reshape(B, S, SSD_HEADS, SSD_HEAD_DIM)
    dt = jax.nn.softplus(sdt.astype(F32) + dt_bias.astype(F32))
    a = -jnp.exp(a_log.astype(F32))
    y = ssd_scan(xs, dt, a, bs.reshape(B, S, SSD_GROUPS, SSD_STATE),
                 cs.reshape(B, S, SSD_GROUPS, SSD_STATE))
    y = y + d_skip.astype(F32)[:, None] * xs.astype(F32)
    y = (y.reshape(B, S, SSD_WIDTH) * jax.nn.silu(sz.astype(F32))).astype(dtype)
    gsz = SSD_WIDTH // SSD_GROUPS
    y_ssd = rms_norm(y.reshape(B, S, SSD_GROUPS, gsz),
                     ssd_norm.reshape(SSD_GROUPS, gsz)).reshape(B, S, SSD_WIDTH)

    gates = jax.nn.sigmoid((mg.reshape(B, S, N_BRANCH, D) + b_merge).astype(F32)).astype(dtype)
    merged = (gates[:, :, 0] * jnp.dot(y_fox, w_o_fox)
              + gates[:, :, 1] * jnp.dot(y_dsa, w_o_dsa)
              + gates[:, :, 2] * jnp.dot(y_ssd, w_o_ssd))
    out = jnp.dot(merged, w_out)
    return x + gate[:, None, :] * out


def setup_inputs(seed: int = 0) -> dict:
    key = jax.random.key(seed)
    ks = jax.random.split(key, 24)

    def nrm(k, shape, s):
        return jax.random.normal(k, shape, F32) * s

    D = D_MODEL
    dt0 = jnp.exp(jax.random.uniform(ks[13], (DEPTH, SSD_HEADS), F32,
                                     minval=float(np.log(1e-3)), maxval=float(np.log(1e-1))))
    return {
        "x": nrm(ks[0], (BATCH, SEQ, D), 1.0),
        "c": nrm(ks[1], (BATCH, D), 1.0),
        "positions": jnp.tile(jnp.arange(SEQ, dtype=jnp.int32)[None, :], (BATCH, 1)),
        "norm_w": 1.0 + nrm(ks[2], (DEPTH, D), 0.02),
        "w_ada": nrm(ks[3], (DEPTH, D, 3 * D), 0.5 * D ** -0.5),
        "b_ada": nrm(ks[4], (DEPTH, 3 * D), 0.01),
        "w_in": nrm(ks[5], (DEPTH, D, N_IN), D ** -0.5),
        "b_fox_f": jax.random.uniform(ks[6], (DEPTH, FOX_HEADS), F32, minval=1.0, maxval=6.0),
        "fox_q_norm": 1.0 + nrm(ks[7], (DEPTH, HEAD_DIM), 0.02),
        "fox_k_norm": 1.0 + nrm(ks[8], (DEPTH, HEAD_DIM), 0.02),
        "dsa_q_norm": 1.0 + nrm(ks[9], (DEPTH, HEAD_DIM), 0.02),
        "dsa_k_norm": 1.0 + nrm(ks[10], (DEPTH, HEAD_DIM), 0.02),
        "conv_w": nrm(ks[11], (DEPTH, CONV_WIDTH, CONV_CH), CONV_WIDTH ** -0.5),
        "conv_b": nrm(ks[12], (DEPTH, CONV_CH), 0.01),
        "dt_bias": dt0 + jnp.log(-jnp.expm1(-dt0)),
        "a_log": jnp.log(jax.random.uniform(ks[14], (DEPTH, SSD_HEADS), F32, minval=1.0, maxval=16.0)),
        "d_skip": 1.0 + nrm(ks[15], (DEPTH, SSD_HEADS), 0.1),
        "ssd_norm": 1.0 + nrm(ks[16], (DEPTH, SSD_WIDTH), 0.02),
        "b_merge": nrm(ks[17], (DEPTH, N_BRANCH, D), 0.01),
        "w_o_fox": nrm(ks[18], (DEPTH, FOX_WIDTH, D), FOX_WIDTH ** -0.5),
        "w_o_dsa": nrm(ks[19], (DEPTH, DSA_WIDTH, D), DSA_WIDTH ** -0.5),
        "w_o_ssd": nrm(ks[20], (DEPTH, SSD_WIDTH, D), SSD_WIDTH ** -0.5),
        "w_out": nrm(ks[21], (DEPTH, D, D), D ** -0.5),
    }


def reference(x, c, positions, norm_w, w_ada, b_ada, w_in, b_fox_f, fox_q_norm, fox_k_norm,
              dsa_q_norm, dsa_k_norm, conv_w, conv_b, dt_bias, a_log, d_skip, ssd_norm,
              b_merge, w_o_fox, w_o_dsa, w_o_ssd, w_out):
    h = x
    for l in range(DEPTH):
        h = hybrid_layer(h, c, positions, norm_w[l], w_ada[l], b_ada[l], w_in[l], b_fox_f[l],
                         fox_q_norm[l], fox_k_norm[l], dsa_q_norm[l], dsa_k_norm[l],
                         conv_w[l], conv_b[l], dt_bias[l], a_log[l], d_skip[l], ssd_norm[l],
                         b_merge[l], w_o_fox[l], w_o_dsa[l], w_o_ssd[l], w_out[l])
    return h
```

```python
import math
import numpy as np
import ml_dtypes
from contextlib import ExitStack
import concourse.bass as bass
import concourse.mybir as mybir
from concourse.bass_utils import run_bass_kernel_spmd

BF = ml_dtypes.bfloat16


F32 = mybir.dt.float32
BF16 = mybir.dt.bfloat16
I32 = mybir.dt.int32
AF = mybir.ActivationFunctionType
ALU = mybir.AluOpType
AX = mybir.AxisListType

ENGS = ("pe", "act", "dve", "pool", "sp")
EPOCH = 30000


class Prog:
    def __init__(self, nc):
        self.nc = nc
        self.es = ExitStack()
        self.ops = {e: [] for e in ENGS}
        self.cnt = {e: 0 for e in ENGS}
        self.sems = {}
        self.seen = {e: {} for e in ENGS}
        self.bufs = {}
        self.dma_tot = {}
        self.nsem = 0

    def sb(self, name, shape, dt):
        return self.es.enter_context(self.nc.sbuf_tensor(name, list(shape), dt))

    def ps(self, name, shape, dt=F32):
        return self.es.enter_context(self.nc.psum_tensor(name, list(shape), dt))

    def sem(self, key):
        if key not in self.sems:
            self.nsem += 1
            self.sems[key] = self.es.enter_context(self.nc.semaphore("s%d" % self.nsem))
        return self.sems[key]

    def _need(self, waits, tok, eng):
        if tok is None:
            return
        k, v = tok
        if eng == "pe" and k[0] == "pe":
            return
        if self.seen[eng].get(k, 0) >= v:
            return
        if waits.get(k, 0) < v:
            waits[k] = v

    def _deps(self, eng, reads, writes):
        waits = {}
        for key in reads:
            st = self.bufs.get(key)
            if st is not None:
                self._need(waits, st[0], eng)
        for key in writes:
            st = self.bufs.get(key)
            if st is not None:
                self._need(waits, st[0], eng)
                for k, v in st[1].items():
                    self._need(waits, (k, v), eng)
        for k, v in waits.items():
            self.seen[eng][k] = v
        return waits

    def _record(self, tok, reads, writes):
        for key in reads:
            st = self.bufs.setdefault(key, [None, {}])
            if st[1].get(tok[0], 0) < tok[1]:
                st[1][tok[0]] = tok[1]
        for key in writes:
            self.bufs[key] = [tok, {}]

    def _tok(self, eng, n):
        ep = (n - 1) // EPOCH
        return ((eng, ep), n - ep * EPOCH)

    def op(self, eng, fn, reads=(), writes=(), track=True):
        waits = self._deps(eng, reads, writes)
        nxt = self.cnt[eng] + 1
        tok = self._tok(eng, nxt)
        if track:
            self.cnt[eng] = nxt
            inc = (tok[0], 1)
        else:
            inc = None
        self._record(tok, reads, writes)
        self.ops[eng].append((waits, fn, inc))

    def dma(self, eng, out, in_, semkey, reads=(), writes=()):
        waits = self._deps(eng, reads, writes)
        k = ("dma", semkey)
        self.dma_tot[k] = self.dma_tot.get(k, 0) + 16
        tok = (k, self.dma_tot[k])
        self._record(tok, reads, writes)
        self.ops[eng].append((waits, lambda e: e.dma_start(out=out, in_=in_), (k, 16)))

    def _emit_eng(self, name, e):
        for waits, fn, inc in self.ops[name]:
            for k, v in waits.items():
                e.wait_ge(self.sem(k), v)
            ins = fn(e)
            if inc is not None:
                ins.then_inc(self.sem(inc[0]), inc[1])

    def finish(self):
        waits = {}
        for k, v in self.dma_tot.items():
            waits[k] = v
        for e in ENGS:
            if e == "sp" or self.cnt[e] == 0:
                continue
            k, v = self._tok(e, self.cnt[e])
            waits[k] = v
        self.ops["sp"].append((waits, None, None))
        for e in ENGS:
            for waits_, fn, inc in self.ops[e]:
                for k in waits_:
                    self.sem(k)
                if inc is not None:
                    self.sem(inc[0])
        nc = self.nc
        with nc.Block() as block:
            @block.tensor
            def _(e):
                self._emit_eng("pe", e)

            @block.scalar
            def _(e):
                self._emit_eng("act", e)

            @block.vector
            def _(e):
                self._emit_eng("dve", e)

            @block.gpsimd
            def _(e):
                self._emit_eng("pool", e)

            @block.sync
            def _(e):
                for waits, fn, inc in self.ops["sp"]:
                    for k, v in waits.items():
                        e.wait_ge(self.sem(k), v)
                    if fn is not None:
                        ins = fn(e)
                        if inc is not None:
                            ins.then_inc(self.sem(inc[0]), inc[1])
        self.es.close()


D = 2048
T1 = 2048
KC = 16
NIN = 19064
NCOL = 4984
EPS = 1e-6
FAM = dict(fq=(0, 1024), fk=(1024, 1024), fv=(2048, 1024), ff=(3072, 8), fg=(3080, 1024),
           dq=(4104, 1024), dk=(5128, 256), dv=(5384, 256), iq=(5640, 1024), ik=(6664, 64),
           iw=(6728, 16), dg=(6744, 1024), sz=(7768, 2048), sxbc=(9816, 3072), sdt=(12888, 32),
           mg=(12920, 6144))
PERM_ORDER = ["fq", "fk", "fv", "fg", "dq", "dk", "dv", "iq", "dg", "sz", "sxbc", "mg", "ik", "iw", "ff", "sdt"]


def perm_cols():
    idx = []
    for f in PERM_ORDER:
        s, n = FAM[f]
        idx.extend(range(s, s + n))
    return np.array(idx, dtype=np.int64)


RP = dict(fqn=(0, 128), fkn=(128, 128), dqn=(256, 128), dkn=(384, 128), bff=(512, 8), dtb=(520, 32),
          if16=(552, 16), if8=(568, 8), if16b=(576, 16), if8b=(592, 8))
NRP = 600


def build_p1(stage=9, nchunks=None, NTG=4):
    nc = bass.Bass("TRN2", target_bir_lowering=False)
    dt = nc.dram_tensor
    TT = NTG * T1
    xT = dt("xT", [D, TT], F32, kind="ExternalInput").ap()
    cvec = dt("cvec", [128, KC], F32, kind="ExternalInput").ap()
    w_ada = dt("w_ada", [D, 3 * D], F32, kind="ExternalInput").ap()
    b_ada = dt("b_ada", [128, 48], F32, kind="ExternalInput").ap()
    norm_w = dt("norm_w", [128, KC], F32, kind="ExternalInput").ap()
    w_in = dt("w_in", [D, NCOL], F32, kind="ExternalInput").ap()
    rowp = dt("rowp", [1, NRP], F32, kind="ExternalInput").ap()
    b_merge = dt("b_merge", [1, 1536], F32, kind="ExternalInput").ap()
    pos = dt("pos", [128, 16 * NTG], I32, kind="ExternalInput").ap()

    o_fq = dt("o_fq", [TT, 256], BF16, kind="ExternalOutput").ap()
    o_fk = dt("o_fk", [TT, 256], BF16, kind="ExternalOutput").ap()
    o_fv = dt("o_fv", [TT, 256], BF16, kind="ExternalOutput").ap()
    o_fg = dt("o_fg", [TT, 256], BF16, kind="ExternalOutput").ap()
    o_dq = dt("o_dq", [TT, 256], BF16, kind="ExternalOutput").ap()
    o_dkv = dt("o_dkv", [TT, 256], BF16, kind="ExternalOutput").ap()
    o_iq = dt("o_iq", [TT, 256], BF16, kind="ExternalOutput").ap()
    o_dg = dt("o_dg", [TT, 256], BF16, kind="ExternalOutput").ap()
    o_sz = dt("o_sz", [TT, 512], BF16, kind="ExternalOutput").ap()
    o_xbc = dt("o_xbc", [TT, 768], BF16, kind="ExternalOutput").ap()
    o_mg = dt("o_mg", [TT, 1536], BF16, kind="ExternalOutput").ap()
    o_small = dt("o_small", [TT, 128], F32, kind="ExternalOutput").ap()
    o_gate = dt("o_gate", [128, KC], F32, kind="ExternalOutput").ap()

    p = Prog(nc)
    uT = p.sb("uT", [128, KC, T1], BF16)
    wst = [p.sb("wst%d" % i, [128, 4096], F32) for i in range(2)]
    wbf = [p.sb("wbf%d" % i, [128, KC, 512], BF16) for i in range(2)]
    stg = [p.sb("stg%d" % i, [128, 16, 512], BF16) for i in range(2)]
    rows = p.sb("rows", [128, NRP], F32)
    bmg = [p.sb("bmg%d" % i, [128, 512], F32) for i in range(2)]
    cs_in = wst[0][:, 0:768].rearrange("p (t c) -> p t c", t=16)
    cs_kf = wst[0][:, 768:1536].rearrange("p (t c) -> p t c", t=16)
    cs_r = wst[0][:, 1536:2304].rearrange("p (t c) -> p t c", t=16)
    cs_ki = wst[1][:, 0:768].bitcast(I32).rearrange("p (t c) -> p t c", t=16)
    cs = p.sb("cs", [128, 16, 48], F32)
    posi = p.sb("posi", [128, 16 * NTG], I32)
    posf = p.sb("posf", [128, 16 * NTG], F32)
    cv = p.sb("cv", [128, KC], F32)
    scv = p.sb("scv", [128, KC], F32)
    sig = p.sb("sig", [128, KC], F32)
    bad = p.sb("bad", [128, 48], F32)
    nw = p.sb("nw", [128, KC], F32)
    mod = p.sb("mod", [128, 48], F32)
    gvec = p.sb("gvec", [128, KC], F32)
    ones = p.sb("ones", [128, 128], F32)
    sq = [p.sb("sq%d" % i, [128, 512], F32) for i in range(2)]
    rstd = p.sb("rstd", [128, 512], F32)
    tmpu = [p.sb("tmpu%d" % i, [128, 512], F32) for i in range(2)]
    sqs = p.sb("sqs", [128, 512], F32)
    st4 = [p.sb("st4_%d" % i, [128, 8], F32) for i in range(2)]
    rt = [p.sb("rt%d" % i, [128, 6, 128], F32) for i in range(2)]
    nrm = [p.sb("nrm%d" % i, [128, 512], F32) for i in range(2)]
    smallo = p.sb("smallo", [128, 16, 128], F32)
    sm_t = p.sb("sm_t", [128, 64], F32)
    PS = [p.ps("ps%d" % i, [128, 512]) for i in range(8)]

    p.op("pool", lambda e: e.memset(ones[:], 1.0), writes=["ones"])
    p.dma("sp", rows[:], rowp.partition_broadcast(128), "rows", writes=["rows"])
    p.dma("sp", posi[:], pos, "posi", writes=["posi"])
    p.dma("sp", cv[:], cvec, "cv", writes=["cv"])
    p.dma("sp", bad[:], b_ada, "bad", writes=["bad"])
    p.dma("sp", nw[:], norm_w, "nw", writes=["nw"])
    p.op("pool", lambda e: e.memset(smallo[:], 0.0), writes=["smallo"])

    p.op("dve", lambda e: e.tensor_copy(out=posf[:], in_=posi[:]), reads=["posi"], writes=["posf"])

    def rope_tables(tg):
        for tt in range(16):
            p.op("dve", lambda e, tt=tt: e.tensor_scalar(out=cs_in[:, tt, :], in0=rows[:, 552:600],
                                                          scalar1=posf[:, tg * 16 + tt:tg * 16 + tt + 1], scalar2=None, op0=ALU.mult),
                 reads=["rows", "posf"], writes=["wst0"])
        p.op("dve", lambda e: e.tensor_scalar(out=cs_in[:, :, 24:48], in0=cs_in[:, :, 24:48], scalar1=math.pi / 2,
                                              scalar2=None, op0=ALU.add), reads=["wst0"], writes=["wst0"])
        p.op("dve", lambda e: e.tensor_scalar(out=cs_ki, in0=cs_in, scalar1=1.0 / (2 * math.pi), scalar2=None,
                                              op0=ALU.mult), reads=["wst0"], writes=["wst1"])
        p.op("dve", lambda e: e.tensor_copy(out=cs_kf, in_=cs_ki), reads=["wst1"], writes=["wst0"])
        p.op("dve", lambda e: e.scalar_tensor_tensor(out=cs_r, in0=cs_kf, scalar=-2 * math.pi, in1=cs_in,
                                                     op0=ALU.mult, op1=ALU.add), reads=["wst0", "wst0"], writes=["wst0"])
        p.op("dve", lambda e: e.tensor_scalar(out=cs_kf, in0=cs_r, scalar1=math.pi, scalar2=2 * math.pi,
                                              op0=ALU.is_gt, op1=ALU.mult), reads=["wst0"], writes=["wst0"])
        p.op("dve", lambda e: e.tensor_tensor(out=cs_r, in0=cs_r, in1=cs_kf, op=ALU.subtract),
             reads=["wst0", "wst0"], writes=["wst0"])
        p.op("dve", lambda e: e.tensor_scalar(out=cs_kf, in0=cs_r, scalar1=-math.pi, scalar2=2 * math.pi,
                                              op0=ALU.is_lt, op1=ALU.mult), reads=["wst0"], writes=["wst0"])
        p.op("dve", lambda e: e.tensor_tensor(out=cs_r, in0=cs_r, in1=cs_kf, op=ALU.add),
             reads=["wst0", "wst0"], writes=["wst0"])
        p.op("act", lambda e: e.activation(out=cs[:], in_=cs_r, func=AF.Sin), reads=["wst0"], writes=["cs"])


    p.op("act", lambda e: e.activation(out=scv[:], in_=cv[:], func=AF.Silu), reads=["cv"], writes=["scv"])
    w_ada_v = w_ada.rearrange("(kc p) n -> p kc n", p=128)
    mps = PS[0]
    for blk in range(24):
        buf = wst[blk % 2]
        key = "wst%d" % (blk % 2)
        bv = buf[:].rearrange("p (k n) -> p k n", k=16)
        p.dma("sp", bv, w_ada_v[:, :, blk * 256:(blk + 1) * 256], key, writes=[key])
        for jj in range(2):
            j = blk * 2 + jj
            for kc in range(KC):
                p.op("pe", lambda e, bv=bv, jj=jj, j=j, kc=kc: e.matmul(
                    mps[:, j:j + 1], lhsT=bv[:, kc, jj * 128:(jj + 1) * 128], rhs=scv[:, kc:kc + 1],
                    start=(kc == 0), stop=(kc == KC - 1)),
                     reads=[key, "scv"], writes=["ps0"], track=(kc == KC - 1))
    p.op("dve", lambda e: e.tensor_tensor(out=mod[:], in0=mps[:, 0:48], in1=bad[:], op=ALU.add),
         reads=["ps0", "bad"], writes=["mod"])
    p.op("dve", lambda e: e.scalar_tensor_tensor(out=gvec[:], in0=mod[:, 16:32], scalar=1.0, in1=nw[:],
                                                 op0=ALU.add, op1=ALU.mult), reads=["mod", "nw"], writes=["gvec"])
    p.dma("pool", o_gate, mod[:, 32:48], "o_gate", reads=["mod"])

    xT_v = xT.rearrange("(kc p) t -> p kc t", p=128)
    w_in_v = w_in.rearrange("(kc p) n -> p kc n", p=128)
    chunks = []
    c0 = 0

    def add(n, kind, oap, oc):
        nonlocal c0
        chunks.append((c0, n, kind, oap, oc))
        c0 += n
    add(256, "fq", o_fq, 0)
    add(256, "fk", o_fk, 0)
    add(256, "cast", o_fv, 0)
    add(256, "silu", o_fg, 0)
    add(256, "dq", o_dq, 0)
    add(256, "dkv", o_dkv, 0)
    add(256, "iq", o_iq, 0)
    add(256, "silu", o_dg, 0)
    add(512, "silu", o_sz, 0)
    add(512, "cast", o_xbc, 0)
    add(256, "cast", o_xbc, 512)
    for i in range(3): add(512, "mg", o_mg, i * 512)
    add(120, "small", o_small, 0)
    assert c0 == NCOL
    if nchunks is not None:
        chunks = chunks[:nchunks]
    state = dict(psi=3, st4i=0, rti=0, nrmi=0, ld=0)

    def compute_u(tg):
        for g in range(4):
            tsl = slice(g * 512, (g + 1) * 512)
            gsl = slice(tg * T1 + g * 512, tg * T1 + (g + 1) * 512)
            for hh in range(2):
                p.dma("sp", wst[hh][:].rearrange("p (k t) -> p k t", k=8), xT_v[:, hh * 8:(hh + 1) * 8, gsl],
                      "wst%d" % hh, writes=["wst%d" % hh])
            ssp = PS[1 + (g % 2)]
            sskey = "ps%d" % (1 + (g % 2))
            for kc in range(KC):
                xv = wst[kc // 8][:, (kc % 8) * 512:(kc % 8 + 1) * 512]
                xkey = "wst%d" % (kc // 8)
                s_ = sq[kc % 2]
                skey = "sq%d" % (kc % 2)
                p.op("act", lambda e, s_=s_, xv=xv: e.activation(out=s_[:], in_=xv, func=AF.Square),
                     reads=[xkey], writes=[skey])
                p.op("pe", lambda e, s_=s_, kc=kc, ssp=ssp: e.matmul(ssp[:], lhsT=ones[:], rhs=s_[:], start=(kc == 0),
                                                                    stop=(kc == KC - 1)),
                     reads=[skey, "ones"], writes=[sskey])
            p.op("dve", lambda e, ssp=ssp: e.tensor_scalar(out=rstd[:], in0=ssp[:], scalar1=1.0 / D, scalar2=EPS,
                                                           op0=ALU.mult, op1=ALU.add), reads=[sskey], writes=["rstd"])
            p.op("act", lambda e: e.activation(out=rstd[:], in_=rstd[:], func=AF.Ln), reads=["rstd"], writes=["rstd"])
            p.op("act", lambda e: e.activation(out=rstd[:], in_=rstd[:], func=AF.Exp, scale=-0.5), reads=["rstd"], writes=["rstd"])
            for kc in range(KC):
                xv = wst[kc // 8][:, (kc % 8) * 512:(kc % 8 + 1) * 512]
                xkey = "wst%d" % (kc // 8)
                tm = tmpu[kc % 2]
                tkey = "tmpu%d" % (kc % 2)
                p.op("dve", lambda e, tm=tm, xv=xv, kc=kc: e.scalar_tensor_tensor(
                    out=tm[:], in0=xv, scalar=gvec[:, kc:kc + 1], in1=rstd[:], op0=ALU.mult, op1=ALU.mult),
                     reads=[xkey, "gvec", "rstd"], writes=[tkey])
                p.op("act", lambda e, tm=tm, kc=kc, tsl=tsl: e.activation(
                    out=uT[:, kc, tsl], in_=tm[:], func=AF.Identity, bias=mod[:, kc:kc + 1], scale=1.0),
                     reads=[tkey, "mod"], writes=["uT"])

    def load_chunk(ci):
        col0, n, kind, oap, oc = chunks[ci]
        li = state["ld"]
        state["ld"] += 1
        wb = wbf[li % 2]
        wkey = "wbf%d" % (li % 2)
        for hh in range(2):
            skey = "wst%d" % hh
            sv = wst[hh][:, 0:8 * n].rearrange("p (k n) -> p k n", k=8)
            p.dma("sp", sv, w_in_v[:, hh * 8:(hh + 1) * 8, col0:col0 + n], skey, writes=[skey])
            eng = "pool" if hh == 0 else "dve"
            p.op(eng, lambda e, wb=wb, sv=sv, hh=hh, n=n: e.tensor_copy(out=wb[:, hh * 8:(hh + 1) * 8, 0:n], in_=sv),
                 reads=[skey], writes=[wkey + "h%d" % hh])
        if kind == "mg":
            bm = bmg[li % 2]
            bkey = "bmg%d" % (li % 2)
            p.dma("sp", bm[:], b_merge[:, oc:oc + 512].partition_broadcast(128), bkey, writes=[bkey])
        return li

    def rms_heads(ps, pkey, nh, hd, wcol, scale, dst, dkey):
        s4 = st4[state["st4i"] % 2]
        s4key = "st4_%d" % (state["st4i"] % 2)
        state["st4i"] += 1
        p.op("act", lambda e: e.activation(out=sqs[:, 0:nh * hd], in_=ps[:, 0:nh * hd], func=AF.Square),
             reads=[pkey], writes=["sqs"])
        p.op("dve", lambda e: e.tensor_reduce(out=s4[:, 0:nh], in_=sqs[:, 0:nh * hd].rearrange("p (h d) -> p h d", h=nh),
                                              axis=AX.X, op=ALU.add), reads=["sqs"], writes=[s4key])
        p.op("dve", lambda e: e.tensor_scalar(out=s4[:, 0:nh], in0=s4[:, 0:nh], scalar1=1.0 / hd, scalar2=EPS,
                                              op0=ALU.mult, op1=ALU.add), reads=[s4key], writes=[s4key])
        p.op("act", lambda e: e.activation(out=s4[:, 0:nh], in_=s4[:, 0:nh], func=AF.Ln), reads=[s4key], writes=[s4key])
        p.op("act", lambda e: e.activation(out=s4[:, 0:nh], in_=s4[:, 0:nh], func=AF.Exp, scale=-0.5,
                                           bias=math.log(scale)), reads=[s4key], writes=[s4key])
        for h in range(nh):
            p.op("dve", lambda e, h=h: e.scalar_tensor_tensor(
                out=dst[:, h * hd:(h + 1) * hd], in0=ps[:, h * hd:(h + 1) * hd], scalar=s4[:, h:h + 1],
                in1=rows[:, wcol:wcol + hd], op0=ALU.mult, op1=ALU.mult),
                 reads=[pkey, s4key, "rows"], writes=[dkey])

    def rope(src, skey, nh, hd, half, tt, dst, dkey, sin_off, cos_off):
        r = rt[state["rti"] % 2]
        rkey = "rt%d" % (state["rti"] % 2)
        state["rti"] += 1
        sv = src.rearrange("p (h d) -> p h d", h=nh)
        dv = dst.rearrange("p (h d) -> p h d", h=nh)
        x1 = sv[:, :, 0:half]
        x2 = sv[:, :, half:2 * half]
        sn = cs[:, tt, sin_off:sin_off + half].unsqueeze(1).broadcast_to([128, nh, half])
        cn = cs[:, tt, cos_off:cos_off + half].unsqueeze(1).broadcast_to([128, nh, half])
        W = nh * half

        def rv(i):
            return r[:, i, 0:W].rearrange("p (h d) -> p h d", h=nh)
        p.op("act", lambda e: e.activation(out=dst, in_=src, func=AF.Copy), reads=[skey], writes=[dkey])
        p.op("dve", lambda e: e.tensor_tensor(out=rv(0), in0=x1, in1=cn, op=ALU.mult), reads=[skey, "cs"], writes=[rkey])
        p.op("dve", lambda e: e.tensor_tensor(out=rv(1), in0=x2, in1=sn, op=ALU.mult), reads=[skey, "cs"], writes=[rkey])
        p.op("dve", lambda e: e.tensor_tensor(out=rv(2), in0=x2, in1=cn, op=ALU.mult), reads=[skey, "cs"], writes=[rkey])
        p.op("dve", lambda e: e.tensor_tensor(out=rv(3), in0=x1, in1=sn, op=ALU.mult), reads=[skey, "cs"], writes=[rkey])
        p.op("dve", lambda e: e.tensor_tensor(out=dv[:, :, 0:half], in0=rv(0), in1=rv(1), op=ALU.subtract),
             reads=[rkey, dkey], writes=[dkey])
        p.op("dve", lambda e: e.tensor_tensor(out=dv[:, :, half:2 * half], in0=rv(2), in1=rv(3), op=ALU.add),
             reads=[rkey, dkey], writes=[dkey])

    for tg in range(NTG):
        rope_tables(tg)
        if stage < 1:
            break
        compute_u(tg)
        if stage < 2:
            continue
        nxt = load_chunk(0)
        for ci in range(len(chunks)):
            col0, n, kind, oap, oc = chunks[ci]
            li = nxt
            if ci + 1 < len(chunks):
                nxt = load_chunk(ci + 1)
            wb = wbf[li % 2]
            wkey = "wbf%d" % (li % 2)
            sg = stg[li % 2]
            sgkey = "stg%d" % (li % 2)
            for tt in range(16):
                ps = PS[3 + state["psi"] % 5]
                pkey = "ps%d" % (3 + state["psi"] % 5)
                state["psi"] += 1
                for kc in range(KC):
                    p.op("pe", lambda e, ps=ps, kc=kc, tt=tt, wb=wb, n=n: e.matmul(
                        ps[:, 0:n], lhsT=uT[:, kc, tt * 128:(tt + 1) * 128], rhs=wb[:, kc, 0:n],
                        start=(kc == 0), stop=(kc == KC - 1)),
                         reads=["uT", wkey + "h%d" % (kc // 8)], writes=[pkey], track=(kc == KC - 1))
                dst = sg[:, tt, 0:n] if kind != "small" else None
                if kind == "cast":
                    p.op("act", lambda e, dst=dst, ps=ps, n=n: e.activation(out=dst, in_=ps[:, 0:n], func=AF.Copy),
                         reads=[pkey], writes=[sgkey])
                elif kind == "silu":
                    p.op("act", lambda e, dst=dst, ps=ps, n=n: e.activation(out=dst, in_=ps[:, 0:n], func=AF.Silu),
                         reads=[pkey], writes=[sgkey])
                elif kind == "mg":
                    bm = bmg[li % 2]
                    bkey = "bmg%d" % (li % 2)
                    tm = tmpu[tt % 2]
                    tkey = "tmpu%d" % (tt % 2)
                    p.op("dve", lambda e, tm=tm, ps=ps, bm=bm: e.tensor_tensor(out=tm[:], in0=ps[:], in1=bm[:], op=ALU.add),
                         reads=[pkey, bkey], writes=[tkey])
                    p.op("act", lambda e, dst=dst, tm=tm: e.activation(out=dst, in_=tm[:], func=AF.Sigmoid),
                         reads=[tkey], writes=[sgkey])
                elif kind == "fq":
                    rms_heads(ps, pkey, 2, 128, 0, 128 ** -0.5, dst, sgkey)
                elif kind == "fk":
                    rms_heads(ps, pkey, 2, 128, 128, 1.0, dst, sgkey)
                elif kind == "dq":
                    nm = nrm[state["nrmi"] % 2]
                    nkey = "nrm%d" % (state["nrmi"] % 2)
                    state["nrmi"] += 1
                    rms_heads(ps, pkey, 2, 128, 256, 128 ** -0.5, nm[:, 0:256], nkey)
                    rope(nm[:, 0:256], nkey, 2, 128, 16, tt, dst, sgkey, 0, 24)
                elif kind == "dkv":
                    nm = nrm[state["nrmi"] % 2]
                    nkey = "nrm%d" % (state["nrmi"] % 2)
                    state["nrmi"] += 1
                    rms_heads(ps, pkey, 1, 128, 384, 1.0, nm[:, 0:128], nkey)
                    rope(nm[:, 0:128], nkey, 1, 128, 16, tt, dst[:, 0:128], sgkey, 0, 24)
                    p.op("act", lambda e, dst=dst, ps=ps: e.activation(out=dst[:, 128:256], in_=ps[:, 128:256], func=AF.Copy),
                         reads=[pkey], writes=[sgkey])
                elif kind == "iq":
                    nm = nrm[state["nrmi"] % 2]
                    nkey = "nrm%d" % (state["nrmi"] % 2)
                    state["nrmi"] += 1
                    p.op("act", lambda e, nm=nm, ps=ps: e.activation(out=nm[:, 0:256], in_=ps[:, 0:256], func=AF.Copy),
                         reads=[pkey], writes=[nkey])
                    rope(nm[:, 0:256], nkey, 4, 64, 8, tt, dst, sgkey, 16, 40)
                elif kind == "small":
                    so = smallo[:, tt, :]
                    nm = nrm[state["nrmi"] % 2]
                    nkey = "nrm%d" % (state["nrmi"] % 2)
                    state["nrmi"] += 1
                    p.op("act", lambda e, nm=nm, ps=ps: e.activation(out=nm[:, 0:64], in_=ps[:, 0:64], func=AF.Copy),
                         reads=[pkey], writes=[nkey])
                    rope(nm[:, 0:64], nkey, 1, 64, 8, tt, so[:, 0:64], "smallo", 16, 40)
                    p.op("act", lambda e, so=so, ps=ps: e.activation(out=so[:, 64:80], in_=ps[:, 64:80], func=AF.Copy),
                         reads=[pkey], writes=["smallo"])
                    p.op("dve", lambda e, ps=ps: e.tensor_tensor(out=sm_t[:, 0:8], in0=ps[:, 80:88], in1=rows[:, 512:520], op=ALU.add),
                         reads=[pkey, "rows"], writes=["sm_t"])
                    p.op("act", lambda e: e.activation(out=sm_t[:, 8:16], in_=sm_t[:, 0:8], func=AF.Exp, scale=-1.0),
                         reads=["sm_t"], writes=["sm_t"])
                    p.op("act", lambda e: e.activation(out=sm_t[:, 16:24], in_=sm_t[:, 8:16], func=AF.Ln, bias=1.0, scale=1.0),
                         reads=["sm_t"], writes=["sm_t"])
                    p.op("dve", lambda e, so=so: e.tensor_scalar(out=so[:, 80:88], in0=sm_t[:, 16:24], scalar1=-1.0, scalar2=None, op0=ALU.mult),
                         reads=["sm_t"], writes=["smallo"])
                    p.op("dve", lambda e, ps=ps: e.tensor_tensor(out=sm_t[:, 24:56], in0=ps[:, 88:120], in1=rows[:, 520:552], op=ALU.add),
                         reads=[pkey, "rows"], writes=["sm_t"])
                    p.op("act", lambda e: e.activation(out=sm_t[:, 24:56], in_=sm_t[:, 24:56], func=AF.Exp),
                         reads=["sm_t"], writes=["sm_t"])
                    p.op("act", lambda e, so=so: e.activation(out=so[:, 88:120], in_=sm_t[:, 24:56], func=AF.Ln, bias=1.0, scale=1.0),
                         reads=["sm_t"], writes=["smallo"])
            rsl = slice(tg * T1, (tg + 1) * T1)
            if kind == "small":
                p.dma("pool", o_small[rsl, :].rearrange("(t p) c -> p t c", p=128), smallo[:], "smallo", reads=["smallo"])
            else:
                p.dma("pool", oap[rsl, :].rearrange("(t p) c -> p t c", p=128)[:, :, oc:oc + n], sg[:, :, 0:n], sgkey, reads=[sgkey])
    p.finish()
    return nc


S = 8192


def build_fox(NH=2, NQC=16, dbg=0):
    nc = bass.Bass("TRN2", target_bir_lowering=False)
    dt = nc.dram_tensor
    SQ = NQC * 512
    qT = dt("qT", [NH, 128, SQ], BF16, kind="ExternalInput").ap()
    kT = dt("kT", [NH, 128, SQ], BF16, kind="ExternalInput").ap()
    v = dt("v", [NH, 128, SQ // 128, 128], BF16, kind="ExternalInput").ap()
    logf = dt("logf", [NH, SQ], F32, kind="ExternalInput").ap()
    fgT = dt("fgT", [NH * 128, SQ], BF16, kind="ExternalInput").ap()
    tri = dt("tri", [128, 128], BF16, kind="ExternalInput").ap()
    yT = dt("yT", [NH * 128, SQ], BF16, kind="ExternalOutput").ap()

    p = Prog(nc)
    emit_fox(p, NH, NQC, qT, kT, v, logf, fgT, tri, yT, dbg)
    p.finish()
    return nc


def emit_fox(p, NH, NQC, qT, kT, v, logf, fgT, tri, yT, dbg=0):
    SQ = NQC * 512
    NKT = SQ // 128
    FC = min(2048, SQ)
    qsb = p.sb("f_q", [128, SQ], BF16)
    ksb = p.sb("f_k", [128, SQ], BF16)
    vsb = p.sb("f_v", [128, NKT, 128], BF16)
    Fp = p.sb("f_Fp", [96, SQ], BF16)
    F3 = p.sb("f_F3", [65, FC], F32)
    Ft = p.sb("f_Ft", [65, FC], BF16)
    Fr = p.sb("f_Fr", [65, FC], F32)
    lf = p.sb("f_lf", [65, FC], F32)
    one3 = p.sb("f_one3", [65, FC], F32)
    carry = p.sb("f_carry", [65, 1], F32)
    trisb = p.sb("f_tri", [128, 128], BF16)
    ones_bf = p.sb("f_ones", [128, 512], BF16)
    nones_bf = p.sb("f_nones", [128, 512], BF16)
    PT = [p.sb("f_pt%d" % i, [128, 512], BF16) for i in range(3)]
    tmpd = [p.sb("f_tmpd%d" % i, [128, 512], F32) for i in range(2)]
    rden = p.sb("f_rden", [128, 512], F32)
    ynum = p.sb("f_ynum", [128, 512], F32)
    fgs = [p.sb("f_fg%d" % i, [128, 512], BF16) for i in range(2)]
    yo = [p.sb("f_yo%d" % i, [128, 512], BF16) for i in range(2)]
    PSS = [p.ps("f_pss%d" % i, [128, 512]) for i in range(3)]
    PSN = [p.ps("f_psn%d" % i, [128, 512]) for i in range(2)]
    PSD = [p.ps("f_psd%d" % i, [128, 512]) for i in range(2)]

    p.dma("sp", trisb[:], tri, "f_tri", writes=["f_tri"])
    p.op("pool", lambda e: e.memset(ones_bf[:], 1.0), writes=["f_ones"])
    p.op("pool", lambda e: e.memset(nones_bf[:], -1.0), writes=["f_nones"])
    p.op("pool", lambda e: e.memset(one3[:], 1.0), writes=["f_one3"])
    p.op("pool", lambda e: e.memset(lf[:], 0.0), writes=["f_lf"])
    cnt = dict(s=0, pt=0, q=0)
    for h in range(NH):
        p.dma("sp", qsb[:], qT[h], "f_q", writes=["f_q"])
        p.dma("sp", ksb[:], kT[h], "f_k", writes=["f_k"])
        p.dma("sp", vsb[:], v[h], "f_v", writes=["f_v"])
        p.op("pool", lambda e: e.memset(Fp[:], 0.0), writes=["f_Fp"])
        p.op("pool", lambda e: e.memset(carry[:], 0.0), writes=["f_carry"])
        for c in range(SQ // FC if dbg != 2 else 0):
            csl = slice(c * FC, (c + 1) * FC)
            for r in (0, 32, 64):
                p.dma("sp", lf[r:r + 1, :], logf[h:h + 1, csl], "f_lf", writes=["f_lf"])
            p.op("dve", lambda e: e.tensor_tensor_scan(out=F3[:], data0=one3[:], data1=lf[:], initial=carry[:],
                                                       op0=ALU.mult, op1=ALU.add),
                 reads=["f_one3", "f_lf", "f_carry"], writes=["f_F3"])
            p.op("dve", lambda e: e.tensor_copy(out=carry[:], in_=F3[:, FC - 1:FC]), reads=["f_F3"], writes=["f_carry"])
            p.op("dve", lambda e: e.tensor_copy(out=Ft[:], in_=F3[:]), reads=["f_F3"], writes=["f_Ft"])
            p.op("dve", lambda e, csl=csl: e.tensor_copy(out=Fp[0:1, csl], in_=Ft[0:1, :]), reads=["f_Ft"], writes=["f_Fp"])
            p.op("dve", lambda e: e.tensor_tensor(out=Fr[:], in0=F3[:], in1=Ft[:], op=ALU.subtract),
                 reads=["f_F3", "f_Ft"], writes=["f_Fr"])
            p.op("dve", lambda e: e.tensor_copy(out=Ft[:], in_=Fr[:]), reads=["f_Fr"], writes=["f_Ft"])
            p.op("dve", lambda e, csl=csl: e.tensor_copy(out=Fp[32:33, csl], in_=Ft[32:33, :]), reads=["f_Ft"], writes=["f_Fp"])
            p.op("dve", lambda e: e.tensor_tensor(out=Fr[:], in0=Fr[:], in1=Ft[:], op=ALU.subtract),
                 reads=["f_Fr", "f_Ft"], writes=["f_Fr"])
            p.op("dve", lambda e, csl=csl: e.tensor_copy(out=Fp[64:65, csl], in_=Fr[64:65, :]), reads=["f_Fr"], writes=["f_Fp"])
        for qc in range(NQC if dbg != 1 else 0):
            q0 = qc * 512
            nkt = 4 * qc + 4
            psn = PSN[cnt["q"] % 2]
            psd = PSD[cnt["q"] % 2]
            nkey = "f_psn%d" % (cnt["q"] % 2)
            dkey = "f_psd%d" % (cnt["q"] % 2)
            fg = fgs[cnt["q"] % 2]
            fgkey = "f_fg%d" % (cnt["q"] % 2)
            yob = yo[cnt["q"] % 2]
            yokey = "f_yo%d" % (cnt["q"] % 2)
            cnt["q"] += 1
            p.dma("sp", fg[:], fgT[h * 128:(h + 1) * 128, q0:q0 + 512], fgkey, writes=[fgkey])
            for kt in range(nkt):
                j = kt - 4 * qc
                c0 = 128 * j if j > 0 else 0
                ncol = 512 - c0
                pss = PSS[cnt["s"] % 3]
                skey = "f_pss%d" % (cnt["s"] % 3)
                cnt["s"] += 1
                pt = PT[cnt["pt"] % 3]
                ptkey = "f_pt%d" % (cnt["pt"] % 3)
                cnt["pt"] += 1
                ksl = slice(kt * 128, (kt + 1) * 128)
                qsl = slice(q0 + c0, q0 + 512)
                p.op("pe", lambda e, pss=pss, ksl=ksl, qsl=qsl, ncol=ncol: e.matmul(
                    pss[:, 0:ncol], lhsT=ksb[:, ksl], rhs=qsb[:, qsl], start=True, stop=False),
                     reads=["f_k", "f_q"], writes=[skey], track=False)
                p.op("pe", lambda e, pss=pss, qsl=qsl, ncol=ncol: e.matmul(
                    pss[:, 0:ncol], lhsT=ones_bf[0:96, 0:128], rhs=Fp[:, qsl], start=False, stop=False),
                     reads=["f_ones", "f_Fp"], writes=[skey], track=False)
                p.op("pe", lambda e, pss=pss, ksl=ksl, ncol=ncol: e.matmul(
                    pss[:, 0:ncol], lhsT=Fp[:, ksl], rhs=nones_bf[0:96, 0:ncol], start=False, stop=True),
                     reads=["f_nones", "f_Fp"], writes=[skey])
                if j >= 0:
                    td = tmpd[kt % 2]
                    tdkey = "f_tmpd%d" % (kt % 2)
                    p.op("dve", lambda e, td=td, pss=pss, ncol=ncol: e.tensor_scalar(
                        out=td[:, 0:ncol], in0=pss[:, 0:ncol], scalar1=30.0, scalar2=None, op0=ALU.min),
                         reads=[skey], writes=[tdkey])
                    p.op("act", lambda e, td=td, pt=pt, ncol=ncol: e.activation(out=pt[:, 0:ncol], in_=td[:, 0:ncol], func=AF.Exp),
                         reads=[tdkey], writes=[ptkey])
                    p.op("pool", lambda e, pt=pt: e.tensor_tensor(out=pt[:, 0:128], in0=pt[:, 0:128], in1=trisb[:], op=ALU.mult),
                         reads=[ptkey, "f_tri"], writes=[ptkey])
                else:
                    p.op("act", lambda e, pss=pss, pt=pt: e.activation(out=pt[:], in_=pss[:], func=AF.Exp),
                         reads=[skey], writes=[ptkey])
                p.op("pe", lambda e, psn=psn, kt=kt, pt=pt, c0=c0, ncol=ncol, nkt=nkt: e.matmul(
                    psn[:, c0:512], lhsT=vsb[:, kt, :], rhs=pt[:, 0:ncol], start=(kt == 0), stop=(kt == nkt - 1)),
                     reads=["f_v", ptkey], writes=[nkey], track=False)
                p.op("pe", lambda e, psd=psd, kt=kt, pt=pt, c0=c0, ncol=ncol, nkt=nkt: e.matmul(
                    psd[:, c0:512], lhsT=ones_bf[:, 0:128], rhs=pt[:, 0:ncol], start=(kt == 0), stop=(kt == nkt - 1)),
                     reads=["f_ones", ptkey], writes=[dkey])
            p.op("dve", lambda e, psd=psd: e.reciprocal(out=rden[:], in_=psd[:]), reads=[dkey], writes=["f_rden"])
            p.op("dve", lambda e, psn=psn: e.tensor_tensor(out=ynum[:], in0=psn[:], in1=rden[:], op=ALU.mult),
                 reads=[nkey, "f_rden"], writes=["f_ynum"])
            p.op("pool", lambda e, yob=yob, fg=fg: e.tensor_tensor(out=yob[:], in0=ynum[:], in1=fg[:], op=ALU.mult),
                 reads=["f_ynum", fgkey], writes=[yokey])
            p.dma("pool", yT[h * 128:(h + 1) * 128, q0:q0 + 512], yob[:], yokey, reads=[yokey])


EPS = 1e-6


def build_ssd(NBLK=16):
    nc = bass.Bass("TRN2", target_bir_lowering=False)
    dt = nc.dram_tensor
    SQ = NBLK * 512
    xbcT = dt("xbcT", [128, 6, SQ], BF16, kind="ExternalInput").ap()
    convw = dt("convw", [128, 6, 4], F32, kind="ExternalInput").ap()
    convb = dt("convb", [128, 6], F32, kind="ExternalInput").ap()
    dtv = dt("dtv", [128, SQ // 128, 8], F32, kind="ExternalInput").ap()
    rowc = dt("rowc", [1, 16 + 512], F32, kind="ExternalInput").ap()
    sz = dt("sz", [128, SQ // 128, 512], BF16, kind="ExternalInput").ap()
    cst = dt("cst", [128, 3, 128], F32, kind="ExternalInput").ap()
    y = dt("y", [128, SQ // 128, 512], BF16, kind="ExternalOutput").ap()
    p = Prog(nc)
    emit_ssd(p, NBLK, xbcT, convw, convb, dtv, rowc, sz, cst, y)
    p.finish()
    return nc


def emit_ssd(p, NBLK, xbcT, convw, convb, dtv, rowc, sz, cst, y):
    SQ = NBLK * 512
    xr = [p.sb("s_xr%d" % i, [128, 6, 515], BF16) for i in range(2)]
    cv = p.sb("s_cv", [128, 6, 512], BF16)
    acc = [p.sb("s_acc%d" % i, [128, 512], F32) for i in range(2)]
    cw = p.sb("s_cw", [128, 6, 4], F32)
    cb = p.sb("s_cb", [128, 6], F32)
    dts = p.sb("s_dt", [128, SQ // 128, 8], F32)
    rows = p.sb("s_rows", [128, 528], F32)
    Arow = p.sb("s_A", [128, 8], F32)
    csts = p.sb("s_cst", [128, 3, 128], F32)
    ident = p.sb("s_ident", [128, 128], BF16)
    ones = p.sb("s_ones", [128, 128], F32)
    mnb = p.sb("s_mnb", [128, 8, 128], F32)
    szs = [p.sb("s_sz%d" % i, [128, 4, 512], BF16) for i in range(2)]
    xs = p.sb("s_xs", [128, 512], F32)
    xd = p.sb("s_xd", [128, 512], BF16)
    xdw = p.sb("s_xdw", [128, 512], BF16)
    Btok = p.sb("s_Btok", [128, 128], BF16)
    cbT = p.sb("s_cbT", [128, 128], F32)
    da = p.sb("s_da", [128, 8], F32)
    acs = p.sb("s_acs", [128, 8], F32)
    tot = p.sb("s_tot", [128, 8], F32)
    wl = p.sb("s_wl", [128, 8], F32)
    eacs = p.sb("s_eacs", [128, 8], F32)
    cd = p.sb("s_cd", [128, 8], F32)
    X = p.sb("s_X", [128, 8, 128], F32)
    dif = p.sb("s_dif", [128, 8, 128], F32)
    dec = p.sb("s_dec", [128, 8, 128], F32)
    MT = p.sb("s_MT", [128, 8, 128], BF16)
    Sst = p.sb("s_S", [128, 512], F32)
    Sbf = p.sb("s_Sbf", [128, 512], BF16)
    t1 = p.sb("s_t1", [128, 512], F32)
    t2 = p.sb("s_t2", [128, 512], F32)
    t3 = p.sb("s_t3", [128, 512], F32)
    ssq = p.sb("s_ssq", [128, 2], F32)
    junk = p.sb("s_junk", [128, 512], F32)
    yst = [p.sb("s_yst%d" % i, [128, 4, 512], BF16) for i in range(2)]
    P_xs = p.ps("s_pxs", [128, 512])
    P_b = p.ps("s_pb", [128, 512])
    P_a = p.ps("s_pa", [128, 512])
    P_d = [p.ps("s_pd%d" % i, [128, 512]) for i in range(2)]
    P_y = p.ps("s_py", [128, 512])
    P_o = p.ps("s_po", [128, 512])
    P_s = p.ps("s_psb", [128, 512])

    p.dma("sp", cw[:], convw, "s_cw", writes=["s_cw"])
    p.dma("sp", cb[:], convb, "s_cb", writes=["s_cb"])
    p.dma("sp", dts[:], dtv, "s_dt", writes=["s_dt"])
    p.dma("sp", rows[:], rowc.partition_broadcast(128), "s_rows", writes=["s_rows"])
    p.dma("sp", csts[:], cst, "s_cst", writes=["s_cst"])
    p.op("act", lambda e: e.activation(out=Arow[:], in_=rows[:, 0:8], func=AF.Exp), reads=["s_rows"], writes=["s_A"])
    p.op("dve", lambda e: e.tensor_scalar(out=Arow[:], in0=Arow[:], scalar1=-1.0, scalar2=None, op0=ALU.mult),
         reads=["s_A"], writes=["s_A"])
    p.op("dve", lambda e: e.tensor_copy(out=ident[:], in_=csts[:, 2, :]), reads=["s_cst"], writes=["s_ident"])
    p.op("pool", lambda e: e.memset(ones[:], 1.0), writes=["s_ones"])
    p.op("pool", lambda e: e.memset(Sst[:], 0.0), writes=["s_S"])
    p.op("pool", lambda e: e.memset(Sbf[:], 0.0), writes=["s_Sbf"])
    p.op("dve", lambda e: e.tensor_copy(out=mnb[:], in_=csts[:, 1, :].unsqueeze(1).broadcast_to([128, 8, 128])),
         reads=["s_cst"], writes=["s_mnb"])
    tri = csts[:, 0, :]
    xbc_v = xbcT
    sz_v = sz
    y_v = y
    Db = rows[:, 8:16].unsqueeze(2).broadcast_to([128, 8, 64])

    def v3(t):
        return t.rearrange("p (h d) -> p h d", h=8)

    for blk in range(NBLK):
        bi = blk % 2
        xk = "s_xr%d" % bi
        if blk == 0:
            p.op("pool", lambda e: e.memset(xr[0][:, :, 0:3], 0.0), writes=[xk])
            p.dma("sp", xr[0][:, :, 3:515], xbc_v[:, :, 0:512], xk, writes=[xk])
        else:
            p.dma("sp", xr[bi][:], xbc_v[:, :, blk * 512 - 3:blk * 512 + 512], xk, writes=[xk])
        szk = "s_sz%d" % bi
        p.dma("sp", szs[bi][:], sz_v[:, blk * 4:(blk + 1) * 4, :], szk, writes=[szk])
        for cc in range(6):
            a = acc[cc % 2]
            ak = "s_acc%d" % (cc % 2)
            p.op("dve", lambda e, a=a, cc=cc, bi=bi: e.tensor_scalar(out=a[:], in0=xr[bi][:, cc, 0:512], scalar1=cw[:, cc, 0:1],
                                                                  scalar2=None, op0=ALU.mult), reads=[xk, "s_cw"], writes=[ak])
            for k in range(1, 4):
                p.op("dve", lambda e, a=a, cc=cc, k=k, bi=bi: e.scalar_tensor_tensor(
                    out=a[:], in0=xr[bi][:, cc, k:k + 512], scalar=cw[:, cc, k:k + 1], in1=a[:], op0=ALU.mult, op1=ALU.add),
                     reads=[xk, "s_cw", ak], writes=[ak])
            p.op("act", lambda e, a=a, cc=cc: e.activation(out=cv[:, cc, :], in_=a[:], func=AF.Silu, bias=cb[:, cc:cc + 1], scale=1.0),
                 reads=[ak, "s_cb"], writes=["s_cv"])
        yk = "s_yst%d" % bi
        for j in range(4):
            c = blk * 4 + j
            jsl = slice(j * 128, (j + 1) * 128)
            for cc in range(4):
                p.op("pe", lambda e, cc=cc, jsl=jsl: e.matmul(P_xs[:, cc * 128:(cc + 1) * 128], lhsT=cv[:, cc, jsl], rhs=ident[:],
                                                             start=True, stop=True), reads=["s_cv", "s_ident"], writes=["s_pxs"],
                     track=(cc == 3))
            p.op("pe", lambda e, jsl=jsl: e.matmul(P_b[:, 0:128], lhsT=cv[:, 4, jsl], rhs=ident[:], start=True, stop=True),
                 reads=["s_cv", "s_ident"], writes=["s_pb"], track=False)
            p.op("pe", lambda e, jsl=jsl: e.matmul(P_b[:, 128:256], lhsT=cv[:, 4, jsl], rhs=cv[:, 5, jsl], start=True, stop=True),
                 reads=["s_cv"], writes=["s_pb"])
            p.op("act", lambda e: e.activation(out=xs[:], in_=P_xs[:], func=AF.Copy), reads=["s_pxs"], writes=["s_xs"])
            p.op("act", lambda e: e.activation(out=Btok[:], in_=P_b[:, 0:128], func=AF.Copy), reads=["s_pb"], writes=["s_Btok"])
            p.op("act", lambda e: e.activation(out=cbT[:], in_=P_b[:, 128:256], func=AF.Copy), reads=["s_pb"], writes=["s_cbT"])
            p.op("dve", lambda e, c=c: e.tensor_tensor(out=da[:], in0=dts[:, c, :], in1=Arow[:], op=ALU.mult),
                 reads=["s_dt", "s_A"], writes=["s_da"])
            p.op("pe", lambda e: e.matmul(P_a[:, 0:8], lhsT=tri, rhs=da[:], start=True, stop=True),
                 reads=["s_cst", "s_da"], writes=["s_pa"], track=False)
            p.op("pe", lambda e: e.matmul(P_a[:, 8:16], lhsT=ones[:], rhs=da[:], start=True, stop=True),
                 reads=["s_ones", "s_da"], writes=["s_pa"])
            p.op("dve", lambda e: e.tensor_copy(out=acs[:], in_=P_a[:, 0:8]), reads=["s_pa"], writes=["s_acs"])
            p.op("dve", lambda e: e.tensor_copy(out=tot[:], in_=P_a[:, 8:16]), reads=["s_pa"], writes=["s_tot"])
            p.op("dve", lambda e: e.tensor_tensor(out=wl[:], in0=tot[:], in1=acs[:], op=ALU.subtract),
                 reads=["s_tot", "s_acs"], writes=["s_wl"])
            p.op("act", lambda e: e.activation(out=wl[:], in_=wl[:], func=AF.Exp), reads=["s_wl"], writes=["s_wl"])
            p.op("act", lambda e: e.activation(out=eacs[:], in_=acs[:], func=AF.Exp), reads=["s_acs"], writes=["s_eacs"])
            p.op("act", lambda e: e.activation(out=cd[:], in_=tot[:], func=AF.Exp), reads=["s_tot"], writes=["s_cd"])
            p.op("dve", lambda e: e.tensor_tensor(out=X[:], in0=tri.unsqueeze(1).broadcast_to([128, 8, 128]),
                                                  in1=da[:].unsqueeze(2).broadcast_to([128, 8, 128]), op=ALU.mult),
                 reads=["s_cst", "s_da"], writes=["s_X"])
            for hh in range(2):
                pk = "s_pd%d" % hh
                p.op("pe", lambda e, hh=hh: e.matmul(P_d[hh][:], lhsT=ones[:], rhs=X[:, hh * 4:(hh + 1) * 4, :], start=True, stop=False),
                     reads=["s_ones", "s_X"], writes=[pk], track=False)
                p.op("pe", lambda e, hh=hh: e.matmul(P_d[hh][:], lhsT=csts[:, 2, :], rhs=mnb[:, hh * 4:(hh + 1) * 4, :], start=False, stop=True),
                     reads=["s_cst", "s_mnb"], writes=[pk])
                p.op("dve", lambda e, hh=hh: e.tensor_tensor(
                    out=dif[:, hh * 4:(hh + 1) * 4, :], in0=P_d[hh][:].rearrange("p (h l) -> p h l", h=4),
                    in1=acs[:, hh * 4:(hh + 1) * 4].unsqueeze(2).broadcast_to([128, 4, 128]), op=ALU.subtract),
                     reads=[pk, "s_acs"], writes=["s_dif"])
            p.op("act", lambda e: e.activation(out=dec[:], in_=dif[:], func=AF.Exp), reads=["s_dif"], writes=["s_dec"])
            p.op("dve", lambda e: e.tensor_tensor(out=MT[:], in0=dec[:], in1=cbT[:].unsqueeze(1).broadcast_to([128, 8, 128]), op=ALU.mult),
                 reads=["s_dec", "s_cbT"], writes=["s_MT"])
            p.op("pool", lambda e, c=c: e.tensor_tensor(out=v3(xd[:]), in0=v3(xs[:]),
                                                       in1=dts[:, c, :].unsqueeze(2).broadcast_to([128, 8, 64]), op=ALU.mult),
                 reads=["s_xs", "s_dt"], writes=["s_xd"])
            p.op("pool", lambda e: e.tensor_tensor(out=v3(xdw[:]), in0=v3(xd[:]), in1=wl[:].unsqueeze(2).broadcast_to([128, 8, 64]), op=ALU.mult),
                 reads=["s_xd", "s_wl"], writes=["s_xdw"])
            for h in range(8):
                p.op("pe", lambda e, h=h: e.matmul(P_y[:, h * 64:(h + 1) * 64], lhsT=MT[:, h, :], rhs=xd[:, h * 64:(h + 1) * 64],
                                                   start=True, stop=True), reads=["s_MT", "s_xd"], writes=["s_py"], track=(h == 7))
            p.op("pe", lambda e, jsl=jsl: e.matmul(P_o[:], lhsT=cv[:, 5, jsl], rhs=Sbf[:], start=True, stop=True),
                 reads=["s_cv", "s_Sbf"], writes=["s_po"])
            p.op("pe", lambda e: e.matmul(P_s[:], lhsT=Btok[:], rhs=xdw[:], start=True, stop=True),
                 reads=["s_Btok", "s_xdw"], writes=["s_psb"])
            p.op("dve", lambda e: e.tensor_tensor(out=v3(t1[:]), in0=v3(P_o[:]), in1=eacs[:].unsqueeze(2).broadcast_to([128, 8, 64]), op=ALU.mult),
                 reads=["s_po", "s_eacs"], writes=["s_t1"])
            p.op("dve", lambda e: e.tensor_tensor(out=t2[:], in0=P_y[:], in1=t1[:], op=ALU.add), reads=["s_py", "s_t1"], writes=["s_t2"])
            p.op("pool", lambda e: e.tensor_tensor(out=v3(t3[:]), in0=v3(xs[:]), in1=Db, op=ALU.mult), reads=["s_xs", "s_rows"], writes=["s_t3"])
            p.op("pool", lambda e: e.tensor_tensor(out=t2[:], in0=t2[:], in1=t3[:], op=ALU.add), reads=["s_t2", "s_t3"], writes=["s_t2"])
            p.op("dve", lambda e: e.tensor_tensor(out=v3(Sst[:]), in0=v3(Sst[:]), in1=cd[:].unsqueeze(2).broadcast_to([128, 8, 64]), op=ALU.mult),
                 reads=["s_S", "s_cd"], writes=["s_S"])
            p.op("dve", lambda e: e.tensor_tensor(out=Sst[:], in0=P_s[:], in1=Sst[:], op=ALU.add), reads=["s_psb", "s_S"], writes=["s_S"])
            p.op("act", lambda e: e.activation(out=Sbf[:], in_=Sst[:], func=AF.Copy), reads=["s_S"], writes=["s_Sbf"])
            p.op("dve", lambda e, j=j, bi=bi: e.tensor_tensor(out=t2[:], in0=t2[:], in1=szs[bi][:, j, :], op=ALU.mult),
                 reads=["s_t2", szk], writes=["s_t2"])
            p.op("pool", lambda e: e.memset(ssq[:], 0.0), writes=["s_ssq"])
            p.op("act", lambda e: e.activation(out=junk[:], in_=t2[:], func=AF.Square, accum_out=ssq[:, 0:1]),
                 reads=["s_t2"], writes=["s_junk", "s_ssq"])
            p.op("dve", lambda e: e.tensor_scalar(out=ssq[:, 1:2], in0=ssq[:, 0:1], scalar1=1.0 / 512, scalar2=EPS, op0=ALU.mult, op1=ALU.add),
                 reads=["s_ssq"], writes=["s_ssq"])
            p.op("act", lambda e: e.activation(out=ssq[:, 1:2], in_=ssq[:, 1:2], func=AF.Ln), reads=["s_ssq"], writes=["s_ssq"])
            p.op("act", lambda e: e.activation(out=ssq[:, 1:2], in_=ssq[:, 1:2], func=AF.Exp, scale=-0.5), reads=["s_ssq"], writes=["s_ssq"])
            p.op("dve", lambda e, j=j, bi=bi: e.scalar_tensor_tensor(out=yst[bi][:, j, :], in0=t2[:], scalar=ssq[:, 1:2], in1=rows[:, 16:528],
                                                                    op0=ALU.mult, op1=ALU.mult),
                 reads=["s_t2", "s_ssq", "s_rows"], writes=[yk])
        p.dma("pool", y_v[:, blk * 4:(blk + 1) * 4, :], yst[bi][:], yk, reads=[yk])


NITER = 18
TOPK = 256


def build_dsa(NI=8):
    nc = bass.Bass("TRN2", target_bir_lowering=False)
    dt = nc.dram_tensor
    NS = 2 * NI
    SK = NI * 1024
    dqT = dt("dqT", [NS, 128, 2, 512], BF16, kind="ExternalInput").ap()
    dgT = dt("dgT", [NS, 128, 2, 512], BF16, kind="ExternalInput").ap()
    dkT = dt("dkT", [128, 2, SK], BF16, kind="ExternalInput").ap()
    dv = dt("dv", [128, SK // 128, 256], BF16, kind="ExternalInput").ap()
    iqT = dt("iqT", [NS, 128, 8, 128], BF16, kind="ExternalInput").ap()
    ikT2 = dt("ikT2", [128, SK], BF16, kind="ExternalInput").ap()
    iw = dt("iw", [128, NS, 16], F32, kind="ExternalInput").ap()
    cmask = dt("cmask", [NS, 128, 1024], F32, kind="ExternalInput").ap()
    identd = dt("ident", [128, 128], BF16, kind="ExternalInput").ap()
    yT = dt("yT", [NS, 128, 2, 512], BF16, kind="ExternalOutput").ap()
    p = Prog(nc)
    emit_dsa(p, NI, dqT, dgT, dkT, dv, iqT, ikT2, iw, cmask, identd, yT)
    p.finish()
    return nc


def emit_dsa(p, NI, dqT, dgT, dkT, dv, iqT, ikT2, iw, cmask, identd, yT):
    NS = 2 * NI
    SK = NI * 1024
    ks = p.sb("d_k", [128, 2, SK], BF16)
    vs = p.sb("d_v", [128, SK // 128, 256], BF16)
    iks = p.sb("d_ik", [128, SK], BF16)
    iws = p.sb("d_iw", [128, NS, 16], F32)
    ident = p.sb("d_ident", [128, 128], BF16)
    ones = p.sb("d_ones", [128, 128], BF16)
    qs = [p.sb("d_q%d" % i, [128, 2, 512], BF16) for i in range(2)]
    gs = [p.sb("d_g%d" % i, [128, 2, 512], BF16) for i in range(2)]
    iqs = [p.sb("d_iq%d" % i, [128, 8, 128], BF16) for i in range(2)]
    cms = [p.sb("d_cm%d" % i, [128, 1024], F32) for i in range(2)]
    sc = p.sb("d_sc", [128, SK], F32)
    junk = p.sb("d_junk", [128, SK], BF16)
    maskq = p.sb("d_maskq", [128, SK], BF16)
    maskT = p.sb("d_maskT", [128, SK // 128, 128], BF16)
    R = [p.sb("d_R%d" % i, [128, 512], F32) for i in range(3)]
    st = p.sb("d_st", [128, 16], F32)
    PT = [p.sb("d_pt%d" % i, [128, 512], BF16) for i in range(3)]
    rden = p.sb("d_rden", [128, 512], F32)
    ynum = p.sb("d_ynum", [128, 512], F32)
    yo = [p.sb("d_yo%d" % i, [128, 2, 512], BF16) for i in range(2)]
    PI = [p.ps("d_pi%d" % i, [128, 512]) for i in range(2)]
    PTr = p.ps("d_ptr", [128, 512])
    PSS = [p.ps("d_pss%d" % i, [128, 512]) for i in range(2)]
    PSN = p.ps("d_psn", [128, 512])
    PSD = p.ps("d_psd", [128, 512])

    p.dma("sp", ks[:], dkT, "d_k", writes=["d_k"])
    p.dma("sp", vs[:], dv, "d_v", writes=["d_v"])
    p.dma("sp", iks[:], ikT2, "d_ik", writes=["d_ik"])
    p.dma("sp", iws[:], iw, "d_iw", writes=["d_iw"])
    p.dma("sp", ident[:], identd, "d_ident", writes=["d_ident"])
    p.op("pool", lambda e: e.memset(ones[:], 1.0), writes=["d_ones"])
    cnt = dict(r=0, pi=0, s=0, pt=0)

    def col(i):
        return st[:, i:i + 1]

    for sl in range(NS):
        i = sl // 2
        nk = 8 * (i + 1)
        NK = nk * 128
        b = sl % 2
        qk, gk, iqk, cmk, yok = "d_q%d" % b, "d_g%d" % b, "d_iq%d" % b, "d_cm%d" % b, "d_yo%d" % b
        p.dma("sp", qs[b][:], dqT[sl], qk, writes=[qk])
        p.dma("sp", gs[b][:], dgT[sl], gk, writes=[gk])
        p.dma("sp", iqs[b][:], iqT[sl], iqk, writes=[iqk])
        p.dma("sp", cms[b][:], cmask[sl], cmk, writes=[cmk])
        for kc in range(nk // 4):
            ksl = slice(kc * 512, (kc + 1) * 512)
            for h in range(16):
                pr, hf = h // 2, h % 2
                pi = PI[cnt["pi"] % 2]
                pik = "d_pi%d" % (cnt["pi"] % 2)
                cnt["pi"] += 1
                r = R[cnt["r"] % 3]
                rk = "d_R%d" % (cnt["r"] % 3)
                cnt["r"] += 1
                p.op("pe", lambda e, pi=pi, pr=pr, hf=hf, ksl=ksl, b=b: e.matmul(
                    pi[:], lhsT=iqs[b][hf * 64:(hf + 1) * 64, pr, :], rhs=iks[hf * 64:(hf + 1) * 64, ksl], start=True, stop=True),
                     reads=[iqk, "d_ik"], writes=[pik])
                p.op("act", lambda e, pi=pi, r=r: e.activation(out=r[:], in_=pi[:], func=AF.Relu), reads=[pik], writes=[rk])
                if h == 0:
                    p.op("dve", lambda e, r=r, ksl=ksl, sl=sl: e.tensor_scalar(out=sc[:, ksl], in0=r[:], scalar1=iws[:, sl, 0:1], scalar2=None,
                                                                            op0=ALU.mult), reads=[rk, "d_iw"], writes=["d_sc"])
                else:
                    p.op("dve", lambda e, r=r, ksl=ksl, sl=sl, h=h: e.scalar_tensor_tensor(
                        out=sc[:, ksl], in0=r[:], scalar=iws[:, sl, h:h + 1], in1=sc[:, ksl], op0=ALU.mult, op1=ALU.add),
                         reads=[rk, "d_iw", "d_sc"], writes=["d_sc"])
        p.op("dve", lambda e, NK=NK: e.tensor_reduce(out=col(8), in_=sc[:, 0:NK], axis=AX.X, op=ALU.max), reads=["d_sc"], writes=["d_st"])
        p.op("dve", lambda e, NK=NK: e.tensor_reduce(out=col(9), in_=sc[:, 0:NK], axis=AX.X, op=ALU.min), reads=["d_sc"], writes=["d_st"])
        p.op("dve", lambda e: e.tensor_scalar(out=col(0), in0=col(9), scalar1=-1.0, scalar2=None, op0=ALU.add), reads=["d_st"], writes=["d_st"])
        p.op("dve", lambda e: e.tensor_tensor(out=col(1), in0=col(8), in1=col(9), op=ALU.subtract), reads=["d_st"], writes=["d_st"])
        p.op("dve", lambda e: e.tensor_scalar(out=col(1), in0=col(1), scalar1=2.0, scalar2=None, op0=ALU.add), reads=["d_st"], writes=["d_st"])
        p.op("dve", lambda e, NK=NK, b=b: e.tensor_tensor(out=sc[:, NK - 1024:NK], in0=sc[:, NK - 1024:NK], in1=cms[b][:], op=ALU.add),
             reads=["d_sc", cmk], writes=["d_sc"])
        for it in range(NITER):
            cit = 0.5 ** (it + 1)
            p.op("dve", lambda e, cit=cit: e.tensor_scalar(out=col(4), in0=col(1), scalar1=cit, scalar2=None, op0=ALU.mult), reads=["d_st"], writes=["d_st"])
            p.op("dve", lambda e: e.tensor_tensor(out=col(2), in0=col(0), in1=col(4), op=ALU.add), reads=["d_st"], writes=["d_st"])
            p.op("dve", lambda e, NK=NK: e.tensor_scalar(out=junk[:, 0:NK], in0=sc[:, 0:NK], scalar1=col(2), scalar2=None, op0=ALU.is_ge,
                                                         op1=ALU.add, accum_out=col(3)), reads=["d_sc", "d_st"], writes=["d_junk", "d_st"])
            p.op("dve", lambda e: e.scalar_tensor_tensor(out=col(5), in0=col(3), scalar=TOPK - 0.5, in1=col(4), op0=ALU.is_gt, op1=ALU.mult),
                 reads=["d_st"], writes=["d_st"])
            p.op("dve", lambda e: e.tensor_tensor(out=col(0), in0=col(0), in1=col(5), op=ALU.add), reads=["d_st"], writes=["d_st"])
        p.op("dve", lambda e, NK=NK: e.tensor_scalar(out=maskq[:, 0:NK], in0=sc[:, 0:NK], scalar1=col(0), scalar2=None, op0=ALU.is_ge),
             reads=["d_sc", "d_st"], writes=["d_maskq"])
        for k4 in range(nk // 4):
            for jj in range(4):
                kt = k4 * 4 + jj
                p.op("pe", lambda e, kt=kt, jj=jj: e.matmul(PTr[:, jj * 128:(jj + 1) * 128], lhsT=maskq[:, kt * 128:(kt + 1) * 128], rhs=ident[:],
                                                           start=True, stop=True), reads=["d_maskq", "d_ident"], writes=["d_ptr"], track=(jj == 3))
            p.op("act", lambda e, k4=k4: e.activation(out=maskT[:, k4 * 4:(k4 + 1) * 4, :], in_=PTr[:].rearrange("p (a t) -> p a t", a=4), func=AF.Copy),
                 reads=["d_ptr"], writes=["d_maskT"])
        for gg in range(2):
            for kt in range(nk):
                pss = PSS[cnt["s"] % 2]
                sk = "d_pss%d" % (cnt["s"] % 2)
                cnt["s"] += 1
                pt = PT[cnt["pt"] % 3]
                ptk = "d_pt%d" % (cnt["pt"] % 3)
                cnt["pt"] += 1
                p.op("pe", lambda e, pss=pss, gg=gg, kt=kt, b=b: e.matmul(pss[:], lhsT=ks[:, gg, kt * 128:(kt + 1) * 128], rhs=qs[b][:, gg, :],
                                                                       start=True, stop=True), reads=["d_k", qk], writes=[sk])
                p.op("act", lambda e, pss=pss, pt=pt: e.activation(out=pt[:], in_=pss[:], func=AF.Exp), reads=[sk], writes=[ptk])
                p.op("pool", lambda e, pt=pt, kt=kt: e.tensor_tensor(
                    out=pt[:].rearrange("p (a t) -> p a t", a=4), in0=pt[:].rearrange("p (a t) -> p a t", a=4),
                    in1=maskT[:, kt, :].unsqueeze(1).broadcast_to([128, 4, 128]), op=ALU.mult), reads=[ptk, "d_maskT"], writes=[ptk])
                p.op("pe", lambda e, pt=pt, gg=gg, kt=kt, nk=nk: e.matmul(PSN[:], lhsT=vs[:, kt, gg * 128:(gg + 1) * 128], rhs=pt[:],
                                                                       start=(kt == 0), stop=(kt == nk - 1)), reads=["d_v", ptk], writes=["d_psn"], track=False)
                p.op("pe", lambda e, pt=pt, kt=kt, nk=nk: e.matmul(PSD[:], lhsT=ones[:], rhs=pt[:], start=(kt == 0), stop=(kt == nk - 1)),
                     reads=["d_ones", ptk], writes=["d_psd"])
            p.op("dve", lambda e: e.reciprocal(out=rden[:], in_=PSD[:]), reads=["d_psd"], writes=["d_rden"])
            p.op("dve", lambda e: e.tensor_tensor(out=ynum[:], in0=PSN[:], in1=rden[:], op=ALU.mult), reads=["d_psn", "d_rden"], writes=["d_ynum"])
            p.op("pool", lambda e, gg=gg, b=b: e.tensor_tensor(out=yo[b][:, gg, :], in0=ynum[:], in1=gs[b][:, gg, :], op=ALU.mult),
                 reads=["d_ynum", gk], writes=[yok])
        p.dma("pool", yT[sl], yo[b][:], yok, reads=[yok])


D = 2048
T3 = 2048


def build_p3(NG=4):
    nc = bass.Bass("TRN2", target_bir_lowering=False)
    dt = nc.dram_tensor
    TT = NG * 512
    yT = dt("yT", [NG, 128, 32, 512], BF16, kind="ExternalInput").ap()
    gT = dt("gT", [NG, 16, 128, 3, 512], BF16, kind="ExternalInput").ap()
    xT = dt("xT", [NG, 16, 128, 512], F32, kind="ExternalInput").ap()
    w_o = dt("w_o", [16, 128, 32, 128], F32, kind="ExternalInput").ap()
    w_out = dt("w_out", [16, 128, 16, 128], F32, kind="ExternalInput").ap()
    gate = dt("gate", [128, 16], F32, kind="ExternalInput").ap()
    x1T = dt("x1T", [NG, 16, 128, 512], F32, kind="ExternalOutput").ap()

    p = Prog(nc)
    ysb = p.sb("ysb", [128, 32, 512], BF16)
    gsb = [p.sb("gsb%d" % i, [128, 3, 512], BF16) for i in range(2)]
    msb = p.sb("msb", [128, 16, 512], BF16)
    wst = [p.sb("wst%d" % i, [128, 32, 128], F32) for i in range(2)]
    wbf = [p.sb("wbf%d" % i, [128, 32, 128], BF16) for i in range(2)]
    xsb = [p.sb("xsb%d" % i, [128, 512], F32) for i in range(2)]
    osb = [p.sb("osb%d" % i, [128, 512], F32) for i in range(2)]
    t0 = [p.sb("t0_%d" % i, [128, 512], F32) for i in range(2)]
    t1 = [p.sb("t1_%d" % i, [128, 512], F32) for i in range(2)]
    gt = p.sb("gt", [128, 16], F32)
    PS = [p.ps("ps%d" % i, [128, 512]) for i in range(8)]

    p.dma("sp", gt[:], gate, "gt", writes=["gt"])
    KB = [(0, 8), (8, 8), (16, 16)]
    cnt = 0
    for g in range(NG):
        tsl = slice(g * 512, (g + 1) * 512)
        p.dma("sp", ysb[:], yT[g], "ysb", writes=["ysb"])
        for nn in range(16):
            wi = cnt % 2
            cnt += 1
            p.dma("sp", wst[wi][:], w_o[nn], "wst%d" % wi, writes=["wst%d" % wi])
            p.op("pool", lambda e, wi=wi: e.tensor_copy(out=wbf[wi][:], in_=wst[wi][:]),
                 reads=["wst%d" % wi], writes=["wbf%d" % wi])
            p.dma("sp", gsb[wi][:], gT[g, nn], "gsb%d" % wi, writes=["gsb%d" % wi])
            pss = [PS[(nn % 2) * 3 + i] for i in range(3)]
            pkeys = ["ps%d" % ((nn % 2) * 3 + i) for i in range(3)]
            for i, (k0, nk) in enumerate(KB):
                for kk in range(nk):
                    kc = k0 + kk
                    p.op("pe", lambda e, i=i, kc=kc, kk=kk, nk=nk, wi=wi, pss=pss: e.matmul(
                        pss[i][:], lhsT=wbf[wi][:, kc, :], rhs=ysb[:, kc, :], start=(kk == 0), stop=(kk == nk - 1)),
                         reads=["wbf%d" % wi, "ysb"], writes=[pkeys[i]], track=(kk == nk - 1))
            a = t0[nn % 2]
            b = t1[nn % 2]
            ak = "t0_%d" % (nn % 2)
            bk = "t1_%d" % (nn % 2)
            p.op("dve", lambda e, a=a, wi=wi, pss=pss: e.tensor_tensor(out=a[:], in0=pss[0][:], in1=gsb[wi][:, 0, :], op=ALU.mult),
                 reads=[pkeys[0], "gsb%d" % wi], writes=[ak])
            p.op("dve", lambda e, b=b, wi=wi, pss=pss: e.tensor_tensor(out=b[:], in0=pss[1][:], in1=gsb[wi][:, 1, :], op=ALU.mult),
                 reads=[pkeys[1], "gsb%d" % wi], writes=[bk])
            p.op("pool", lambda e, a=a, b=b: e.tensor_tensor(out=a[:], in0=a[:], in1=b[:], op=ALU.add),
                 reads=[ak, bk], writes=[ak])
            p.op("dve", lambda e, b=b, wi=wi, pss=pss: e.tensor_tensor(out=b[:], in0=pss[2][:], in1=gsb[wi][:, 2, :], op=ALU.mult),
                 reads=[pkeys[2], "gsb%d" % wi], writes=[bk])
            p.op("pool", lambda e, a=a, b=b, nn=nn: e.tensor_tensor(out=msb[:, nn, :], in0=a[:], in1=b[:], op=ALU.add),
                 reads=[ak, bk], writes=["msb"])
        for mc in range(16):
            wi = cnt % 2
            cnt += 1
            p.dma("sp", wst[wi][:, 0:16, :], w_out[mc], "wst%d" % wi, writes=["wst%d" % wi])
            p.op("pool", lambda e, wi=wi: e.tensor_copy(out=wbf[wi][:, 0:16, :], in_=wst[wi][:, 0:16, :]),
                 reads=["wst%d" % wi], writes=["wbf%d" % wi])
            xi = mc % 2
            p.dma("sp", xsb[xi][:], xT[g, mc], "xsb%d" % xi, writes=["xsb%d" % xi])
            ps = PS[6 + mc % 2]
            pk = "ps%d" % (6 + mc % 2)
            for kc in range(16):
                p.op("pe", lambda e, kc=kc, wi=wi, ps=ps: e.matmul(ps[:], lhsT=wbf[wi][:, kc, :], rhs=msb[:, kc, :],
                                                                 start=(kc == 0), stop=(kc == 15)),
                     reads=["wbf%d" % wi, "msb"], writes=[pk], track=(kc == 15))
            p.op("dve", lambda e, xi=xi, ps=ps, mc=mc: e.scalar_tensor_tensor(
                out=osb[xi][:], in0=ps[:], scalar=gt[:, mc:mc + 1], in1=xsb[xi][:], op0=ALU.mult, op1=ALU.add),
                 reads=[pk, "gt", "xsb%d" % xi], writes=["osb%d" % xi])
            p.dma("pool", x1T[g, mc], osb[xi][:], "osb%d" % xi, reads=["osb%d" % xi])
    p.finish()
    return nc


def fm(v, n):
    return np.ascontiguousarray(v.reshape(n, 128).T)

def core_cols(g):
    idx = []
    def r(f, off, n):
        s, _ = FAM[f]
        idx.extend(range(s + off, s + off + n))
    r("fq", 256 * g, 256); r("fk", 256 * g, 256); r("fv", 256 * g, 256); r("fg", 256 * g, 256)
    r("dq", 256 * g, 256); r("dk", 128 * (g % 2), 128); r("dv", 128 * (g % 2), 128)
    r("iq", 256 * g, 256); r("dg", 256 * g, 256); r("sz", 512 * g, 512); r("sxbc", 768 * g, 768)
    r("mg", 1536 * g, 1536); r("ik", 0, 64); r("iw", 0, 16); r("ff", 0, 8); r("sdt", 0, 32)
    return np.array(idx, dtype=np.int64)

def rowp_for(inp, l):
    theta = 500000.0
    if16 = (theta ** (-np.arange(16, dtype=np.float32) / 16)).astype(np.float32)
    if8 = (theta ** (-np.arange(8, dtype=np.float32) / 8)).astype(np.float32)
    rowp = np.zeros((1, NRP), np.float32)
    rowp[0, 0:128] = inp["fox_q_norm"][l]; rowp[0, 128:256] = inp["fox_k_norm"][l]
    rowp[0, 256:384] = inp["dsa_q_norm"][l]; rowp[0, 384:512] = inp["dsa_k_norm"][l]
    rowp[0, 512:520] = inp["b_fox_f"][l]; rowp[0, 520:552] = inp["dt_bias"][l]
    rowp[0, 552:568] = if16; rowp[0, 568:576] = if8; rowp[0, 576:592] = if16; rowp[0, 592:600] = if8
    return rowp

def p1_maps(xT_b, inp, l, ntg=4):
    rowp = rowp_for(inp, l)
    maps = []
    for core in range(8):
        b, g = core // 4, core % 4
        cols = core_cols(g)
        ntok = ntg * T1
        posc = inp["positions"][b, :ntok].astype(np.int32)
        maps.append(dict(
            xT=np.ascontiguousarray(xT_b[b][:, :ntok]), cvec=fm(inp["c"][b], 16), w_ada=inp["w_ada"][l],
            b_ada=fm(inp["b_ada"][l], 48), norm_w=fm(inp["norm_w"][l], 16),
            w_in=np.ascontiguousarray(inp["w_in"][l][:, cols]), rowp=rowp,
            b_merge=np.ascontiguousarray(inp["b_merge"][l].reshape(1, 3 * D)[:, g * 1536:(g + 1) * 1536]),
            pos=np.ascontiguousarray(posc.reshape(ntok // 128, 128).T)))
    return maps


def p3_maps_one(yT, gT, xT, w_o, w_out, gate):
    TT = yT.shape[1]; NG = TT // 512
    y4 = np.ascontiguousarray(yT.reshape(32, 128, NG, 512).transpose(2, 1, 0, 3)).astype(BF)
    g4 = np.ascontiguousarray(gT.reshape(3, 16, 128, NG, 512).transpose(3, 1, 2, 0, 4)).astype(BF)
    x4 = np.ascontiguousarray(xT.reshape(16, 128, NG, 512).transpose(2, 0, 1, 3)).astype(np.float32)
    return dict(yT=y4, gT=g4, xT=x4, w_o=w_o, w_out=w_out, gate=gate)

def p3_weights(inp, l):
    w_o = np.concatenate([inp["w_o_fox"][l], inp["w_o_dsa"][l], inp["w_o_ssd"][l]], 0)
    w_o4 = np.ascontiguousarray(w_o.reshape(32, 128, 16, 128).transpose(2, 1, 0, 3))
    w_out4 = np.ascontiguousarray(inp["w_out"][l].reshape(16, 128, 16, 128).transpose(2, 1, 0, 3))
    return w_o4, w_out4

def p3_unpack(x1):
    NG = x1.shape[0]
    return np.ascontiguousarray(x1.transpose(1, 2, 0, 3).reshape(2048, NG * 512))

def fox_v_layout(v_tok, NH):
    SQ = v_tok.shape[0]
    return np.ascontiguousarray(v_tok.reshape(SQ // 128, 128, NH, 128).transpose(2, 1, 0, 3))


_NC_CACHE = {}


def _get_nc(name, builder):
    if name not in _NC_CACHE:
        _NC_CACHE[name] = builder()
    return _NC_CACHE[name]


_LAUNCH_LOG = []


def _run(nc, maps, tag=""):
    res = run_bass_kernel_spmd(nc, maps, core_ids=list(range(8)))
    et = getattr(res, "exec_time_ns", None)
    _LAUNCH_LOG.append((tag, et))
    print("[kernel] launch %s exec_time_ns=%s" % (tag, et), flush=True)
    return res.results


def _f32(a):
    return np.asarray(a).astype(np.float32)


def ssd_maps(xbc_tok, conv_w, conv_b, dt_tok, a_log, d_skip, ssd_norm_g, sz_tok):
    SQ = xbc_tok.shape[0]
    xbcT = np.ascontiguousarray(xbc_tok.T.reshape(6, 128, SQ).transpose(1, 0, 2)).astype(BF)
    convw = np.ascontiguousarray(conv_w.T.reshape(6, 128, 4).transpose(1, 0, 2)).astype(np.float32)
    convb = np.ascontiguousarray(conv_b.reshape(6, 128).T).astype(np.float32)
    dtv = np.ascontiguousarray(dt_tok.reshape(SQ // 128, 128, 8).transpose(1, 0, 2)).astype(np.float32)
    rowc = np.concatenate([a_log, d_skip, ssd_norm_g]).reshape(1, -1).astype(np.float32)
    szm = np.ascontiguousarray(sz_tok.reshape(SQ // 128, 128, 512).transpose(1, 0, 2)).astype(BF)
    tri = np.triu(np.ones((128, 128), np.float32))
    cst = np.ascontiguousarray(np.stack([tri, (tri - 1) * 30000.0, np.eye(128, dtype=np.float32)], 1)).astype(np.float32)
    return dict(xbcT=xbcT, convw=convw, convb=convb, dtv=dtv, rowc=rowc, sz=szm, cst=cst)


def dsa_maps(j, NI, dq, dk, dv, iq, ik, iw, dg):
    NS = 2 * NI
    SK = NI * 1024
    qts = []
    for i in range(NI):
        qts += [8 * i + j, 8 * i + 7 - j]

    def qlay(a, qt):
        t = a[qt * 128:(qt + 1) * 128].reshape(128, 2, 4, 128)
        return np.ascontiguousarray(t.transpose(3, 1, 2, 0).reshape(128, 2, 512))
    dqT = np.stack([qlay(dq, qt) for qt in qts]).astype(BF)
    dgT = np.stack([qlay(dg, qt) for qt in qts]).astype(BF)
    dkT = np.ascontiguousarray(dk[:SK].transpose(2, 1, 0)).astype(BF)
    dvm = np.ascontiguousarray(dv[:SK].reshape(SK // 128, 128, 256).transpose(1, 0, 2)).astype(BF)

    def iqlay(qt):
        t = iq[qt * 128:(qt + 1) * 128].reshape(128, 8, 2, 64)
        return np.ascontiguousarray(t.transpose(2, 3, 1, 0).reshape(128, 8, 128))
    iqT = np.stack([iqlay(qt) for qt in qts]).astype(BF)
    ikT = ik[:SK].T
    ikT2 = np.ascontiguousarray(np.concatenate([ikT, ikT], 0)).astype(BF)
    iwm = np.ascontiguousarray(np.stack([iw[qt * 128:(qt + 1) * 128] for qt in qts], 1)).astype(np.float32)
    cm = np.zeros((NS, 128, 1024), np.float32)
    for sl, qt in enumerate(qts):
        i = sl // 2
        s = 8 * i * 128 + np.arange(1024)[None, :]
        t = qt * 128 + np.arange(128)[:, None]
        cm[sl] = np.where(s <= t, 0.0, -1e30)
    return dict(dqT=dqT, dgT=dgT, dkT=dkT, dv=dvm, iqT=iqT, ikT2=ikT2, iw=iwm, cmask=cm,
                ident=np.eye(128, dtype=np.float32).astype(BF)), qts


def run_layer(xT_b, inp, l):
    S = 8192
    nc1 = _get_nc("p1", lambda: build_p1(9, None, 4))
    rA = _run(nc1, p1_maps(xT_b, inp, l, ntg=4), "P1")
    tri_bf = np.triu(np.ones((128, 128), np.float32)).astype(BF)
    mapsB = []
    for core in range(8):
        b, g = core // 4, core % 4
        r = rA[core]
        small = _f32(rA[b * 4]["o_small"])
        q = np.asarray(r["o_fq"]).reshape(S, 2, 128)
        k = np.asarray(r["o_fk"]).reshape(S, 2, 128)
        mapsB.append(dict(
            qT=np.ascontiguousarray(q.transpose(1, 2, 0)), kT=np.ascontiguousarray(k.transpose(1, 2, 0)),
            v=fox_v_layout(np.asarray(r["o_fv"]), 2), logf=np.ascontiguousarray(small[:, 80 + 2 * g:82 + 2 * g].T),
            fgT=np.ascontiguousarray(np.asarray(r["o_fg"]).T), tri=tri_bf))
    ncB = _get_nc("fox", lambda: build_fox(2, 16))
    rB = _run(ncB, mapsB, "FoX")
    per_b = []
    for b in range(2):
        rs = [rA[b * 4 + g] for g in range(4)]
        small = _f32(rs[0]["o_small"])
        d = dict(
            dq=np.concatenate([_f32(r["o_dq"]).reshape(S, 2, 128) for r in rs], 1),
            dk=np.stack([_f32(rs[0]["o_dkv"])[:, 0:128], _f32(rs[1]["o_dkv"])[:, 0:128]], 1),
            dv=np.stack([_f32(rs[0]["o_dkv"])[:, 128:256], _f32(rs[1]["o_dkv"])[:, 128:256]], 1),
            iq=np.concatenate([_f32(r["o_iq"]).reshape(S, 4, 64) for r in rs], 1),
            dg=np.concatenate([_f32(r["o_dg"]).reshape(S, 2, 128) for r in rs], 1),
            sz=np.concatenate([_f32(r["o_sz"]) for r in rs], 1),
            xbc=np.concatenate([_f32(r["o_xbc"]) for r in rs], 1),
            mg=np.concatenate([np.asarray(r["o_mg"]) for r in rs], 1),
            ik=small[:, 0:64], iw=small[:, 64:80], dt=small[:, 88:120], gate=_f32(rs[0]["o_gate"]))
        per_b.append(d)
    mapsC = []
    for core in range(8):
        b, g = core // 4, core % 4
        d = per_b[b]
        ch = np.concatenate([np.arange(512 * g, 512 * g + 512), 2048 + np.arange(128 * g, 128 * g + 128),
                             2560 + np.arange(128 * g, 128 * g + 128)])
        mapsC.append(ssd_maps(d["xbc"][:, ch], inp["conv_w"][l][:, ch], inp["conv_b"][l][ch], d["dt"][:, 8 * g:8 * g + 8],
                              inp["a_log"][l][8 * g:8 * g + 8], inp["d_skip"][l][8 * g:8 * g + 8],
                              inp["ssd_norm"][l][512 * g:512 * g + 512], d["sz"][:, 512 * g:512 * g + 512]))
    ncC = _get_nc("ssd", lambda: build_ssd(16))
    rC = _run(ncC, mapsC, "SSD")
    mapsD, qtsD = [], []
    for core in range(8):
        b, j = core // 4, core % 4
        d = per_b[b]
        m, qts = dsa_maps(j, 8, d["dq"], d["dk"], d["dv"], d["iq"], d["ik"], d["iw"], d["dg"])
        mapsD.append(m)
        qtsD.append(qts)
    ncD = _get_nc("dsa", lambda: build_dsa(8))
    rD = _run(ncD, mapsD, "DSA")
    yT_b = []
    for b in range(2):
        yT = np.zeros((4096, S), dtype=BF)
        for g in range(4):
            yT[256 * g:256 * g + 256] = np.asarray(rB[b * 4 + g]["yT"])
            ys = np.asarray(rC[b * 4 + g]["y"]).transpose(1, 0, 2).reshape(S, 512)
            yT[2048 + 512 * g:2048 + 512 * g + 512] = ys.T
            yd = np.asarray(rD[b * 4 + g]["yT"])
            for sl, qt in enumerate(qtsD[b * 4 + g]):
                blk = yd[sl].reshape(128, 2, 4, 128).transpose(1, 2, 0, 3).reshape(1024, 128)
                yT[1024:2048, qt * 128:(qt + 1) * 128] = blk
        yT_b.append(yT)
    w_o4, w_out4 = p3_weights(inp, l)
    mapsE = []
    for core in range(8):
        b, q = core // 4, core % 4
        tsl = slice(q * 2048, (q + 1) * 2048)
        mapsE.append(p3_maps_one(np.ascontiguousarray(yT_b[b][:, tsl]), np.ascontiguousarray(per_b[b]["mg"][tsl].T),
                                 np.ascontiguousarray(xT_b[b][:, tsl]), w_o4, w_out4, per_b[b]["gate"]))
    ncE = _get_nc("p3", lambda: build_p3(4))
    rE = _run(ncE, mapsE, "P3")
    new = []
    for b in range(2):
        new.append(np.concatenate([p3_unpack(np.asarray(rE[b * 4 + q]["x1T"])) for q in range(4)], 1))
    return new


def kernel(**inputs):
    inp = {k: np.asarray(v) for k, v in inputs.items()}
    xT_b = [np.ascontiguousarray(inp["x"][b].T).astype(np.float32) for b in range(2)]
    for l in range(2):
        xT_b = run_layer(xT_b, inp, l)
    out = np.stack([np.ascontiguousarray(xT_b[b].T) for b in range(2)]).astype(np.float32)
    return out
```

```python
import math
import numpy as np
import ml_dtypes
from contextlib import ExitStack
import concourse.bass as bass
import concourse.mybir as mybir
from concourse.bass_utils import run_bass_kernel_spmd

BF = ml_dtypes.bfloat16


F32 = mybir.dt.float32
BF16 = mybir.dt.bfloat16
I32 = mybir.dt.int32
AF = mybir.ActivationFunctionType
ALU = mybir.AluOpType
AX = mybir.AxisListType

ENGS = ("pe", "act", "dve", "pool", "sp")
EPOCH = 30000


class Prog:
    def __init__(self, nc):
        self.nc = nc
        self.es = ExitStack()
        self.ops = {e: [] for e in ENGS}
        self.cnt = {e: 0 for e in ENGS}
        self.sems = {}
        self.seen = {e: {} for e in ENGS}
        self.bufs = {}
        self.dma_tot = {}
        self.nsem = 0

    def sb(self, name, shape, dt):
        return self.es.enter_context(self.nc.sbuf_tensor(name, list(shape), dt))

    def ps(self, name, shape, dt=F32):
        return self.es.enter_context(self.nc.psum_tensor(name, list(shape), dt))

    def sem(self, key):
        if key not in self.sems:
            self.nsem += 1
            self.sems[key] = self.es.enter_context(self.nc.semaphore("s%d" % self.nsem))
        return self.sems[key]

    def _need(self, waits, tok, eng):
        if tok is None:
            return
        k, v = tok
        if eng == "pe" and k[0] == "pe":
            return
        if self.seen[eng].get(k, 0) >= v:
            return
        if waits.get(k, 0) < v:
            waits[k] = v

    def _deps(self, eng, reads, writes):
        waits = {}
        for key in reads:
            st = self.bufs.get(key)
            if st is not None:
                self._need(waits, st[0], eng)
        for key in writes:
            st = self.bufs.get(key)
            if st is not None:
                self._need(waits, st[0], eng)
                for k, v in st[1].items():
                    self._need(waits, (k, v), eng)
        for k, v in waits.items():
            self.seen[eng][k] = v
        return waits

    def _record(self, tok, reads, writes):
        for key in reads:
            st = self.bufs.setdefault(key, [None, {}])
            if st[1].get(tok[0], 0) < tok[1]:
                st[1][tok[0]] = tok[1]
        for key in writes:
            self.bufs[key] = [tok, {}]

    def _tok(self, eng, n):
        ep = (n - 1) // EPOCH
        return ((eng, ep), n - ep * EPOCH)

    def op(self, eng, fn, reads=(), writes=(), track=True):
        waits = self._deps(eng, reads, writes)
        nxt = self.cnt[eng] + 1
        tok = self._tok(eng, nxt)
        if track:
            self.cnt[eng] = nxt
            inc = (tok[0], 1)
        else:
            inc = None
        self._record(tok, reads, writes)
        self.ops[eng].append((waits, fn, inc))

    def dma(self, eng, out, in_, semkey, reads=(), writes=()):
        waits = self._deps(eng, reads, writes)
        k = ("dma", semkey)
        self.dma_tot[k] = self.dma_tot.get(k, 0) + 16
        tok = (k, self.dma_tot[k])
        self._record(tok, reads, writes)
        self.ops[eng].append((waits, lambda e: e.dma_start(out=out, in_=in_), (k, 16)))

    def _emit_eng(self, name, e):
        for waits, fn, inc in self.ops[name]:
            for k, v in waits.items():
                e.wait_ge(self.sem(k), v)
            ins = fn(e)
            if inc is not None:
                ins.then_inc(self.sem(inc[0]), inc[1])

    def finish(self):
        waits = {}
        for k, v in self.dma_tot.items():
            waits[k] = v
        for e in ENGS:
            if e == "sp" or self.cnt[e] == 0:
                continue
            k, v = self._tok(e, self.cnt[e])
            waits[k] = v
        self.ops["sp"].append((waits, None, None))
        for e in ENGS:
            for waits_, fn, inc in self.ops[e]:
                for k in waits_:
                    self.sem(k)
                if inc is not None:
                    self.sem(inc[0])
        nc = self.nc
        with nc.Block() as block:
            @block.tensor
            def _(e):
                self._emit_eng("pe", e)

            @block.scalar
            def _(e):
                self._emit_eng("act", e)

            @block.vector
            def _(e):
                self._emit_eng("dve", e)

            @block.gpsimd
            def _(e):
                self._emit_eng("pool", e)

            @block.sync
            def _(e):
                for waits, fn, inc in self.ops["sp"]:
                    for k, v in waits.items():
                        e.wait_ge(self.sem(k), v)
                    if fn is not None:
                        ins = fn(e)
                        if inc is not None:
                            ins.then_inc(self.sem(inc[0]), inc[1])
        self.es.close()


D = 2048
T1 = 2048
KC = 16
NIN = 19064
NCOL = 4984
EPS = 1e-6
FAM = dict(fq=(0, 1024), fk=(1024, 1024), fv=(2048, 1024), ff=(3072, 8), fg=(3080, 1024),
           dq=(4104, 1024), dk=(5128, 256), dv=(5384, 256), iq=(5640, 1024), ik=(6664, 64),
           iw=(6728, 16), dg=(6744, 1024), sz=(7768, 2048), sxbc=(9816, 3072), sdt=(12888, 32),
           mg=(12920, 6144))
PERM_ORDER = ["fq", "fk", "fv", "fg", "dq", "dk", "dv", "iq", "dg", "sz", "sxbc", "mg", "ik", "iw", "ff", "sdt"]


def perm_cols():
    idx = []
    for f in PERM_ORDER:
        s, n = FAM[f]
        idx.extend(range(s, s + n))
    return np.array(idx, dtype=np.int64)


RP = dict(fqn=(0, 128), fkn=(128, 128), dqn=(256, 128), dkn=(384, 128), bff=(512, 8), dtb=(520, 32),
          if16=(552, 16), if8=(568, 8), if16b=(576, 16), if8b=(592, 8))
NRP = 600


def build_p1(stage=9, nchunks=None, NTG=4):
    nc = bass.Bass("TRN2", target_bir_lowering=False)
    dt = nc.dram_tensor
    TT = NTG * T1
    xT = dt("xT", [D, TT], F32, kind="ExternalInput").ap()
    cvec = dt("cvec", [128, KC], F32, kind="ExternalInput").ap()
    w_ada = dt("w_ada", [D, 3 * D], F32, kind="ExternalInput").ap()
    b_ada = dt("b_ada", [128, 48], F32, kind="ExternalInput").ap()
    norm_w = dt("norm_w", [128, KC], F32, kind="ExternalInput").ap()
    w_in = dt("w_in", [D, NCOL], F32, kind="ExternalInput").ap()
    rowp = dt("rowp", [1, NRP], F32, kind="ExternalInput").ap()
    b_merge = dt("b_merge", [1, 1536], F32, kind="ExternalInput").ap()
    pos = dt("pos", [128, 16 * NTG], I32, kind="ExternalInput").ap()

    o_fq = dt("o_fq", [TT, 256], BF16, kind="ExternalOutput").ap()
    o_fk = dt("o_fk", [TT, 256], BF16, kind="ExternalOutput").ap()
    o_fv = dt("o_fv", [TT, 256], BF16, kind="ExternalOutput").ap()
    o_fg = dt("o_fg", [TT, 256], BF16, kind="ExternalOutput").ap()
    o_dq = dt("o_dq", [TT, 256], BF16, kind="ExternalOutput").ap()
    o_dkv = dt("o_dkv", [TT, 256], BF16, kind="ExternalOutput").ap()
    o_iq = dt("o_iq", [TT, 256], BF16, kind="ExternalOutput").ap()
    o_dg = dt("o_dg", [TT, 256], BF16, kind="ExternalOutput").ap()
    o_sz = dt("o_sz", [TT, 512], BF16, kind="ExternalOutput").ap()
    o_xbc = dt("o_xbc", [TT, 768], BF16, kind="ExternalOutput").ap()
    o_mg = dt("o_mg", [TT, 1536], BF16, kind="ExternalOutput").ap()
    o_small = dt("o_small", [TT, 128], F32, kind="ExternalOutput").ap()
    o_gate = dt("o_gate", [128, KC], F32, kind="ExternalOutput").ap()

    p = Prog(nc)
    uT = p.sb("uT", [128, KC, T1], BF16)
    wst = [p.sb("wst%d" % i, [128, 4096], F32) for i in range(2)]
    wbf = [p.sb("wbf%d" % i, [128, KC, 512], BF16) for i in range(2)]
    stg = [p.sb("stg%d" % i, [128, 16, 512], BF16) for i in range(2)]
    rows = p.sb("rows", [128, NRP], F32)
    bmg = [p.sb("bmg%d" % i, [128, 512], F32) for i in range(2)]
    cs_in = wst[0][:, 0:768].rearrange("p (t c) -> p t c", t=16)
    cs_kf = wst[0][:, 768:1536].rearrange("p (t c) -> p t c", t=16)
    cs_r = wst[0][:, 1536:2304].rearrange("p (t c) -> p t c", t=16)
    cs_ki = wst[1][:, 0:768].bitcast(I32).rearrange("p (t c) -> p t c", t=16)
    cs = p.sb("cs", [128, 16, 48], F32)
    posi = p.sb("posi", [128, 16 * NTG], I32)
    posf = p.sb("posf", [128, 16 * NTG], F32)
    cv = p.sb("cv", [128, KC], F32)
    scv = p.sb("scv", [128, KC], F32)
    sig = p.sb("sig", [128, KC], F32)
    bad = p.sb("bad", [128, 48], F32)
    nw = p.sb("nw", [128, KC], F32)
    mod = p.sb("mod", [128, 48], F32)
    gvec = p.sb("gvec", [128, KC], F32)
    ones = p.sb("ones", [128, 128], F32)
    sq = [p.sb("sq%d" % i, [128, 512], F32) for i in range(2)]
    rstd = p.sb("rstd", [128, 512], F32)
    tmpu = [p.sb("tmpu%d" % i, [128, 512], F32) for i in range(2)]
    sqs = p.sb("sqs", [128, 512], F32)
    st4 = [p.sb("st4_%d" % i, [128, 8], F32) for i in range(2)]
    rt = [p.sb("rt%d" % i, [128, 6, 128], F32) for i in range(2)]
    nrm = [p.sb("nrm%d" % i, [128, 512], F32) for i in range(2)]
    smallo = p.sb("smallo", [128, 16, 128], F32)
    sm_t = p.sb("sm_t", [128, 64], F32)
    PS = [p.ps("ps%d" % i, [128, 512]) for i in range(8)]

    p.op("pool", lambda e: e.memset(ones[:], 1.0), writes=["ones"])
    p.dma("sp", rows[:], rowp.partition_broadcast(128), "rows", writes=["rows"])
    p.dma("sp", posi[:], pos, "posi", writes=["posi"])
    p.dma("sp", cv[:], cvec, "cv", writes=["cv"])
    p.dma("sp", bad[:], b_ada, "bad", writes=["bad"])
    p.dma("sp", nw[:], norm_w, "nw", writes=["nw"])
    p.op("pool", lambda e: e.memset(smallo[:], 0.0), writes=["smallo"])

    p.op("dve", lambda e: e.tensor_copy(out=posf[:], in_=posi[:]), reads=["posi"], writes=["posf"])

    def rope_tables(tg):
        for tt in range(16):
            p.op("dve", lambda e, tt=tt: e.tensor_scalar(out=cs_in[:, tt, :], in0=rows[:, 552:600],
                                                          scalar1=posf[:, tg * 16 + tt:tg * 16 + tt + 1], scalar2=None, op0=ALU.mult),
                 reads=["rows", "posf"], writes=["wst0"])
        p.op("dve", lambda e: e.tensor_scalar(out=cs_in[:, :, 24:48], in0=cs_in[:, :, 24:48], scalar1=math.pi / 2,
                                              scalar2=None, op0=ALU.add), reads=["wst0"], writes=["wst0"])
        p.op("dve", lambda e: e.tensor_scalar(out=cs_ki, in0=cs_in, scalar1=1.0 / (2 * math.pi), scalar2=None,
                                              op0=ALU.mult), reads=["wst0"], writes=["wst1"])
        p.op("dve", lambda e: e.tensor_copy(out=cs_kf, in_=cs_ki), reads=["wst1"], writes=["wst0"])
        p.op("dve", lambda e: e.scalar_tensor_tensor(out=cs_r, in0=cs_kf, scalar=-2 * math.pi, in1=cs_in,
                                                     op0=ALU.mult, op1=ALU.add), reads=["wst0", "wst0"], writes=["wst0"])
        p.op("dve", lambda e: e.tensor_scalar(out=cs_kf, in0=cs_r, scalar1=math.pi, scalar2=2 * math.pi,
                                              op0=ALU.is_gt, op1=ALU.mult), reads=["wst0"], writes=["wst0"])
        p.op("dve", lambda e: e.tensor_tensor(out=cs_r, in0=cs_r, in1=cs_kf, op=ALU.subtract),
             reads=["wst0", "wst0"], writes=["wst0"])
        p.op("dve", lambda e: e.tensor_scalar(out=cs_kf, in0=cs_r, scalar1=-math.pi, scalar2=2 * math.pi,
                                              op0=ALU.is_lt, op1=ALU.mult), reads=["wst0"], writes=["wst0"])
        p.op("dve", lambda e: e.tensor_tensor(out=cs_r, in0=cs_r, in1=cs_kf, op=ALU.add),
             reads=["wst0", "wst0"], writes=["wst0"])
        p.op("act", lambda e: e.activation(out=cs[:], in_=cs_r, func=AF.Sin), reads=["wst0"], writes=["cs"])


    p.op("act", lambda e: e.activation(out=scv[:], in_=cv[:], func=AF.Silu), reads=["cv"], writes=["scv"])
    w_ada_v = w_ada.rearrange("(kc p) n -> p kc n", p=128)
    mps = PS[0]
    for blk in range(24):
        buf = wst[blk % 2]
        key = "wst%d" % (blk % 2)
        bv = buf[:].rearrange("p (k n) -> p k n", k=16)
        p.dma("sp", bv, w_ada_v[:, :, blk * 256:(blk + 1) * 256], key, writes=[key])
        for jj in range(2):
            j = blk * 2 + jj
            for kc in range(KC):
                p.op("pe", lambda e, bv=bv, jj=jj, j=j, kc=kc: e.matmul(
                    mps[:, j:j + 1], lhsT=bv[:, kc, jj * 128:(jj + 1) * 128], rhs=scv[:, kc:kc + 1],
                    start=(kc == 0), stop=(kc == KC - 1)),
                     reads=[key, "scv"], writes=["ps0"], track=(kc == KC - 1))
    p.op("dve", lambda e: e.tensor_tensor(out=mod[:], in0=mps[:, 0:48], in1=bad[:], op=ALU.add),
         reads=["ps0", "bad"], writes=["mod"])
    p.op("dve", lambda e: e.scalar_tensor_tensor(out=gvec[:], in0=mod[:, 16:32], scalar=1.0, in1=nw[:],
                                                 op0=ALU.add, op1=ALU.mult), reads=["mod", "nw"], writes=["gvec"])
    p.dma("pool", o_gate, mod[:, 32:48], "o_gate", reads=["mod"])

    xT_v = xT.rearrange("(kc p) t -> p kc t", p=128)
    w_in_v = w_in.rearrange("(kc p) n -> p kc n", p=128)
    chunks = []
    c0 = 0

    def add(n, kind, oap, oc):
        nonlocal c0
        chunks.append((c0, n, kind, oap, oc))
        c0 += n
    add(256, "fq", o_fq, 0)
    add(256, "fk", o_fk, 0)
    add(256, "cast", o_fv, 0)
    add(256, "silu", o_fg, 0)
    add(256, "dq", o_dq, 0)
    add(256, "dkv", o_dkv, 0)
    add(256, "iq", o_iq, 0)
    add(256, "silu", o_dg, 0)
    add(512, "silu", o_sz, 0)
    add(512, "cast", o_xbc, 0)
    add(256, "cast", o_xbc, 512)
    for i in range(3): add(512, "mg", o_mg, i * 512)
    add(120, "small", o_small, 0)
    assert c0 == NCOL
    if nchunks is not None:
        chunks = chunks[:nchunks]
    state = dict(psi=3, st4i=0, rti=0, nrmi=0, ld=0)

    def compute_u(tg):
        for g in range(4):
            tsl = slice(g * 512, (g + 1) * 512)
            gsl = slice(tg * T1 + g * 512, tg * T1 + (g + 1) * 512)
            for hh in range(2):
                p.dma("sp", wst[hh][:].rearrange("p (k t) -> p k t", k=8), xT_v[:, hh * 8:(hh + 1) * 8, gsl],
                      "wst%d" % hh, writes=["wst%d" % hh])
            ssp = PS[1 + (g % 2)]
            sskey = "ps%d" % (1 + (g % 2))
            for kc in range(KC):
                xv = wst[kc // 8][:, (kc % 8) * 512:(kc % 8 + 1) * 512]
                xkey = "wst%d" % (kc // 8)
                s_ = sq[kc % 2]
                skey = "sq%d" % (kc % 2)
                p.op("act", lambda e, s_=s_, xv=xv: e.activation(out=s_[:], in_=xv, func=AF.Square),
                     reads=[xkey], writes=[skey])
                p.op("pe", lambda e, s_=s_, kc=kc, ssp=ssp: e.matmul(ssp[:], lhsT=ones[:], rhs=s_[:], start=(kc == 0),
                                                                    stop=(kc == KC - 1)),
                     reads=[skey, "ones"], writes=[sskey])
            p.op("dve", lambda e, ssp=ssp: e.tensor_scalar(out=rstd[:], in0=ssp[:], scalar1=1.0 / D, scalar2=EPS,
                                                           op0=ALU.mult, op1=ALU.add), reads=[sskey], writes=["rstd"])
            p.op("act", lambda e: e.activation(out=rstd[:], in_=rstd[:], func=AF.Ln), reads=["rstd"], writes=["rstd"])
            p.op("act", lambda e: e.activation(out=rstd[:], in_=rstd[:], func=AF.Exp, scale=-0.5), reads=["rstd"], writes=["rstd"])
            for kc in range(KC):
                xv = wst[kc // 8][:, (kc % 8) * 512:(kc % 8 + 1) * 512]
                xkey = "wst%d" % (kc // 8)
                tm = tmpu[kc % 2]
                tkey = "tmpu%d" % (kc % 2)
                p.op("dve", lambda e, tm=tm, xv=xv, kc=kc: e.scalar_tensor_tensor(
                    out=tm[:], in0=xv, scalar=gvec[:, kc:kc + 1], in1=rstd[:], op0=ALU.mult, op1=ALU.mult),
                     reads=[xkey, "gvec", "rstd"], writes=[tkey])
                p.op("act", lambda e, tm=tm, kc=kc, tsl=tsl: e.activation(
                    out=uT[:, kc, tsl], in_=tm[:], func=AF.Identity, bias=mod[:, kc:kc + 1], scale=1.0),
                     reads=[tkey, "mod"], writes=["uT"])

    def load_chunk(ci):
        col0, n, kind, oap, oc = chunks[ci]
        li = state["ld"]
        state["ld"] += 1
        wb = wbf[li % 2]
        wkey = "wbf%d" % (li % 2)
        for hh in range(2):
            skey = "wst%d" % hh
            sv = wst[hh][:, 0:8 * n].rearrange("p (k n) -> p k n", k=8)
            p.dma("sp", sv, w_in_v[:, hh * 8:(hh + 1) * 8, col0:col0 + n], skey, writes=[skey])
            eng = "pool" if hh == 0 else "dve"
            p.op(eng, lambda e, wb=wb, sv=sv, hh=hh, n=n: e.tensor_copy(out=wb[:, hh * 8:(hh + 1) * 8, 0:n], in_=sv),
                 reads=[skey], writes=[wkey + "h%d" % hh])
        if kind == "mg":
            bm = bmg[li % 2]
            bkey = "bmg%d" % (li % 2)
            p.dma("sp", bm[:], b_merge[:, oc:oc + 512].partition_broadcast(128), bkey, writes=[bkey])
        return li

    def rms_heads(ps, pkey, nh, hd, wcol, scale, dst, dkey):
        s4 = st4[state["st4i"] % 2]
        s4key = "st4_%d" % (state["st4i"] % 2)
        state["st4i"] += 1
        p.op("act", lambda e: e.activation(out=sqs[:, 0:nh * hd], in_=ps[:, 0:nh * hd], func=AF.Square),
             reads=[pkey], writes=["sqs"])
        p.op("dve", lambda e: e.tensor_reduce(out=s4[:, 0:nh], in_=sqs[:, 0:nh * hd].rearrange("p (h d) -> p h d", h=nh),
                                              axis=AX.X, op=ALU.add), reads=["sqs"], writes=[s4key])
        p.op("dve", lambda e: e.tensor_scalar(out=s4[:, 0:nh], in0=s4[:, 0:nh], scalar1=1.0 / hd, scalar2=EPS,
                                              op0=ALU.mult, op1=ALU.add), reads=[s4key], writes=[s4key])
        p.op("act", lambda e: e.activation(out=s4[:, 0:nh], in_=s4[:, 0:nh], func=AF.Ln), reads=[s4key], writes=[s4key])
        p.op("act", lambda e: e.activation(out=s4[:, 0:nh], in_=s4[:, 0:nh], func=AF.Exp, scale=-0.5,
                                           bias=math.log(scale)), reads=[s4key], writes=[s4key])
        for h in range(nh):
            p.op("dve", lambda e, h=h: e.scalar_tensor_tensor(
                out=dst[:, h * hd:(h + 1) * hd], in0=ps[:, h * hd:(h + 1) * hd], scalar=s4[:, h:h + 1],
                in1=rows[:, wcol:wcol + hd], op0=ALU.mult, op1=ALU.mult),
                 reads=[pkey, s4key, "rows"], writes=[dkey])

    def rope(src, skey, nh, hd, half, tt, dst, dkey, sin_off, cos_off):
        r = rt[state["rti"] % 2]
        rkey = "rt%d" % (state["rti"] % 2)
        state["rti"] += 1
        sv = src.rearrange("p (h d) -> p h d", h=nh)
        dv = dst.rearrange("p (h d) -> p h d", h=nh)
        x1 = sv[:, :, 0:half]
        x2 = sv[:, :, half:2 * half]
        sn = cs[:, tt, sin_off:sin_off + half].unsqueeze(1).broadcast_to([128, nh, half])
        cn = cs[:, tt, cos_off:cos_off + half].unsqueeze(1).broadcast_to([128, nh, half])
        W = nh * half

        def rv(i):
            return r[:, i, 0:W].rearrange("p (h d) -> p h d", h=nh)
        p.op("act", lambda e: e.activation(out=dst, in_=src, func=AF.Copy), reads=[skey], writes=[dkey])
        p.op("dve", lambda e: e.tensor_tensor(out=rv(0), in0=x1, in1=cn, op=ALU.mult), reads=[skey, "cs"], writes=[rkey])
        p.op("dve", lambda e: e.tensor_tensor(out=rv(1), in0=x2, in1=sn, op=ALU.mult), reads=[skey, "cs"], writes=[rkey])
        p.op("dve", lambda e: e.tensor_tensor(out=rv(2), in0=x2, in1=cn, op=ALU.mult), reads=[skey, "cs"], writes=[rkey])
        p.op("dve", lambda e: e.tensor_tensor(out=rv(3), in0=x1, in1=sn, op=ALU.mult), reads=[skey, "cs"], writes=[rkey])
        p.op("dve", lambda e: e.tensor_tensor(out=dv[:, :, 0:half], in0=rv(0), in1=rv(1), op=ALU.subtract),
             reads=[rkey, dkey], writes=[dkey])
        p.op("dve", lambda e: e.tensor_tensor(out=dv[:, :, half:2 * half], in0=rv(2), in1=rv(3), op=ALU.add),
             reads=[rkey, dkey], writes=[dkey])

    for tg in range(NTG):
        rope_tables(tg)
        if stage < 1:
            break
        compute_u(tg)
        if stage < 2:
            continue
        nxt = load_chunk(0)
        for ci in range(len(chunks)):
            col0, n, kind, oap, oc = chunks[ci]
            li = nxt
            if ci + 1 < len(chunks):
                nxt = load_chunk(ci + 1)
            wb = wbf[li % 2]
            wkey = "wbf%d" % (li % 2)
            sg = stg[li % 2]
            sgkey = "stg%d" % (li % 2)
            for tt in range(16):
                ps = PS[3 + state["psi"] % 5]
                pkey = "ps%d" % (3 + state["psi"] % 5)
                state["psi"] += 1
                for kc in range(KC):
                    p.op("pe", lambda e, ps=ps, kc=kc, tt=tt, wb=wb, n=n: e.matmul(
                        ps[:, 0:n], lhsT=uT[:, kc, tt * 128:(tt + 1) * 128], rhs=wb[:, kc, 0:n],
                        start=(kc == 0), stop=(kc == KC - 1)),
                         reads=["uT", wkey + "h%d" % (kc // 8)], writes=[pkey], track=(kc == KC - 1))
                dst = sg[:, tt, 0:n] if kind != "small" else None
                if kind == "cast":
                    p.op("act", lambda e, dst=dst, ps=ps, n=n: e.activation(out=dst, in_=ps[:, 0:n], func=AF.Copy),
                         reads=[pkey], writes=[sgkey])
                elif kind == "silu":
                    p.op("act", lambda e, dst=dst, ps=ps, n=n: e.activation(out=dst, in_=ps[:, 0:n], func=AF.Silu),
                         reads=[pkey], writes=[sgkey])
                elif kind == "mg":
                    bm = bmg[li % 2]
                    bkey = "bmg%d" % (li % 2)
                    tm = tmpu[tt % 2]
                    tkey = "tmpu%d" % (tt % 2)
                    p.op("dve", lambda e, tm=tm, ps=ps, bm=bm: e.tensor_tensor(out=tm[:], in0=ps[:], in1=bm[:], op=ALU.add),
                         reads=[pkey, bkey], writes=[tkey])
                    p.op("act", lambda e, dst=dst, tm=tm: e.activation(out=dst, in_=tm[:], func=AF.Sigmoid),
                         reads=[tkey], writes=[sgkey])
                elif kind == "fq":
                    rms_heads(ps, pkey, 2, 128, 0, 128 ** -0.5, dst, sgkey)
                elif kind == "fk":
                    rms_heads(ps, pkey, 2, 128, 128, 1.0, dst, sgkey)
                elif kind == "dq":
                    nm = nrm[state["nrmi"] % 2]
                    nkey = "nrm%d" % (state["nrmi"] % 2)
                    state["nrmi"] += 1
                    rms_heads(ps, pkey, 2, 128, 256, 128 ** -0.5, nm[:, 0:256], nkey)
                    rope(nm[:, 0:256], nkey, 2, 128, 16, tt, dst, sgkey, 0, 24)
                elif kind == "dkv":
                    nm = nrm[state["nrmi"] % 2]
                    nkey = "nrm%d" % (state["nrmi"] % 2)
                    state["nrmi"] += 1
                    rms_heads(ps, pkey, 1, 128, 384, 1.0, nm[:, 0:128], nkey)
                    rope(nm[:, 0:128], nkey, 1, 128, 16, tt, dst[:, 0:128], sgkey, 0, 24)
                    p.op("act", lambda e, dst=dst, ps=ps: e.activation(out=dst[:, 128:256], in_=ps[:, 128:256], func=AF.Copy),
                         reads=[pkey], writes=[sgkey])
                elif kind == "iq":
                    nm = nrm[state["nrmi"] % 2]
                    nkey = "nrm%d" % (state["nrmi"] % 2)
                    state["nrmi"] += 1
                    p.op("act", lambda e, nm=nm, ps=ps: e.activation(out=nm[:, 0:256], in_=ps[:, 0:256], func=AF.Copy),
                         reads=[pkey], writes=[nkey])
                    rope(nm[:, 0:256], nkey, 4, 64, 8, tt, dst, sgkey, 16, 40)
                elif kind == "small":
                    so = smallo[:, tt, :]
                    nm = nrm[state["nrmi"] % 2]
                    nkey = "nrm%d" % (state["nrmi"] % 2)
                    state["nrmi"] += 1
                    p.op("act", lambda e, nm=nm, ps=ps: e.activation(out=nm[:, 0:64], in_=ps[:, 0:64], func=AF.Copy),
                         reads=[pkey], writes=[nkey])
                    rope(nm[:, 0:64], nkey, 1, 64, 8, tt, so[:, 0:64], "smallo", 16, 40)
                    p.op("act", lambda e, so=so, ps=ps: e.activation(out=so[:, 64:80], in_=ps[:, 64:80], func=AF.Copy),
                         reads=[pkey], writes=["smallo"])
                    p.op("dve", lambda e, ps=ps: e.tensor_tensor(out=sm_t[:, 0:8], in0=ps[:, 80:88], in1=rows[:, 512:520], op=ALU.add),
                         reads=[pkey, "rows"], writes=["sm_t"])
                    p.op("act", lambda e: e.activation(out=sm_t[:, 8:16], in_=sm_t[:, 0:8], func=AF.Exp, scale=-1.0),
                         reads=["sm_t"], writes=["sm_t"])
                    p.op("act", lambda e: e.activation(out=sm_t[:, 16:24], in_=sm_t[:, 8:16], func=AF.Ln, bias=1.0, scale=1.0),
                         reads=["sm_t"], writes=["sm_t"])
                    p.op("dve", lambda e, so=so: e.tensor_scalar(out=so[:, 80:88], in0=sm_t[:, 16:24], scalar1=-1.0, scalar2=None, op0=ALU.mult),
                         reads=["sm_t"], writes=["smallo"])
                    p.op("dve", lambda e, ps=ps: e.tensor_tensor(out=sm_t[:, 24:56], in0=ps[:, 88:120], in1=rows[:, 520:552], op=ALU.add),
                         reads=[pkey, "rows"], writes=["sm_t"])
                    p.op("act", lambda e: e.activation(out=sm_t[:, 24:56], in_=sm_t[:, 24:56], func=AF.Exp),
                         reads=["sm_t"], writes=["sm_t"])
                    p.op("act", lambda e, so=so: e.activation(out=so[:, 88:120], in_=sm_t[:, 24:56], func=AF.Ln, bias=1.0, scale=1.0),
                         reads=["sm_t"], writes=["smallo"])
            rsl = slice(tg * T1, (tg + 1) * T1)
            if kind == "small":
                p.dma("pool", o_small[rsl, :].rearrange("(t p) c -> p t c", p=128), smallo[:], "smallo", reads=["smallo"])
            else:
                p.dma("pool", oap[rsl, :].rearrange("(t p) c -> p t c", p=128)[:, :, oc:oc + n], sg[:, :, 0:n], sgkey, reads=[sgkey])
    p.finish()
    return nc


S = 8192


def build_fox(NH=2, NQC=16, dbg=0):
    nc = bass.Bass("TRN2", target_bir_lowering=False)
    dt = nc.dram_tensor
    SQ = NQC * 512
    qT = dt("qT", [NH, 128, SQ], BF16, kind="ExternalInput").ap()
    kT = dt("kT", [NH, 128, SQ], BF16, kind="ExternalInput").ap()
    v = dt("v", [NH, 128, SQ // 128, 128], BF16, kind="ExternalInput").ap()
    logf = dt("logf", [NH, SQ], F32, kind="ExternalInput").ap()
    fgT = dt("fgT", [NH * 128, SQ], BF16, kind="ExternalInput").ap()
    tri = dt("tri", [128, 128], BF16, kind="ExternalInput").ap()
    yT = dt("yT", [NH * 128, SQ], BF16, kind="ExternalOutput").ap()

    p = Prog(nc)
    emit_fox(p, NH, NQC, qT, kT, v, logf, fgT, tri, yT, dbg)
    p.finish()
    return nc


def emit_fox(p, NH, NQC, qT, kT, v, logf, fgT, tri, yT, dbg=0):
    SQ = NQC * 512
    NKT = SQ // 128
    FC = min(2048, SQ)
    qsb = p.sb("f_q", [128, SQ], BF16)
    ksb = p.sb("f_k", [128, SQ], BF16)
    vsb = p.sb("f_v", [128, NKT, 128], BF16)
    Fp = p.sb("f_Fp", [96, SQ], BF16)
    F3 = p.sb("f_F3", [65, FC], F32)
    Ft = p.sb("f_Ft", [65, FC], BF16)
    Fr = p.sb("f_Fr", [65, FC], F32)
    lf = p.sb("f_lf", [65, FC], F32)
    one3 = p.sb("f_one3", [65, FC], F32)
    carry = p.sb("f_carry", [65, 1], F32)
    trisb = p.sb("f_tri", [128, 128], BF16)
    ones_bf = p.sb("f_ones", [128, 512], BF16)
    nones_bf = p.sb("f_nones", [128, 512], BF16)
    PT = [p.sb("f_pt%d" % i, [128, 512], BF16) for i in range(3)]
    tmpd = [p.sb("f_tmpd%d" % i, [128, 512], F32) for i in range(2)]
    rden = p.sb("f_rden", [128, 512], F32)
    ynum = p.sb("f_ynum", [128, 512], F32)
    fgs = [p.sb("f_fg%d" % i, [128, 512], BF16) for i in range(2)]
    yo = [p.sb("f_yo%d" % i, [128, 512], BF16) for i in range(2)]
    PSS = [p.ps("f_pss%d" % i, [128, 512]) for i in range(3)]
    PSN = [p.ps("f_psn%d" % i, [128, 512]) for i in range(2)]
    PSD = [p.ps("f_psd%d" % i, [128, 512]) for i in range(2)]

    p.dma("sp", trisb[:], tri, "f_tri", writes=["f_tri"])
    p.op("pool", lambda e: e.memset(ones_bf[:], 1.0), writes=["f_ones"])
    p.op("pool", lambda e: e.memset(nones_bf[:], -1.0), writes=["f_nones"])
    p.op("pool", lambda e: e.memset(one3[:], 1.0), writes=["f_one3"])
    p.op("pool", lambda e: e.memset(lf[:], 0.0), writes=["f_lf"])
    cnt = dict(s=0, pt=0, q=0)
    for h in range(NH):
        p.dma("sp", qsb[:], qT[h], "f_q", writes=["f_q"])
        p.dma("sp", ksb[:], kT[h], "f_k", writes=["f_k"])
        p.dma("sp", vsb[:], v[h], "f_v", writes=["f_v"])
        p.op("pool", lambda e: e.memset(Fp[:], 0.0), writes=["f_Fp"])
        p.op("pool", lambda e: e.memset(carry[:], 0.0), writes=["f_carry"])
        for c in range(SQ // FC if dbg != 2 else 0):
            csl = slice(c * FC, (c + 1) * FC)
            for r in (0, 32, 64):
                p.dma("sp", lf[r:r + 1, :], logf[h:h + 1, csl], "f_lf", writes=["f_lf"])
            p.op("dve", lambda e: e.tensor_tensor_scan(out=F3[:], data0=one3[:], data1=lf[:], initial=carry[:],
                                                       op0=ALU.mult, op1=ALU.add),
                 reads=["f_one3", "f_lf", "f_carry"], writes=["f_F3"])
            p.op("dve", lambda e: e.tensor_copy(out=carry[:], in_=F3[:, FC - 1:FC]), reads=["f_F3"], writes=["f_carry"])
            p.op("dve", lambda e: e.tensor_copy(out=Ft[:], in_=F3[:]), reads=["f_F3"], writes=["f_Ft"])
            p.op("dve", lambda e, csl=csl: e.tensor_copy(out=Fp[0:1, csl], in_=Ft[0:1, :]), reads=["f_Ft"], writes=["f_Fp"])
            p.op("dve", lambda e: e.tensor_tensor(out=Fr[:], in0=F3[:], in1=Ft[:], op=ALU.subtract),
                 reads=["f_F3", "f_Ft"], writes=["f_Fr"])
            p.op("dve", lambda e: e.tensor_copy(out=Ft[:], in_=Fr[:]), reads=["f_Fr"], writes=["f_Ft"])
            p.op("dve", lambda e, csl=csl: e.tensor_copy(out=Fp[32:33, csl], in_=Ft[32:33, :]), reads=["f_Ft"], writes=["f_Fp"])
            p.op("dve", lambda e: e.tensor_tensor(out=Fr[:], in0=Fr[:], in1=Ft[:], op=ALU.subtract),
                 reads=["f_Fr", "f_Ft"], writes=["f_Fr"])
            p.op("dve", lambda e, csl=csl: e.tensor_copy(out=Fp[64:65, csl], in_=Fr[64:65, :]), reads=["f_Fr"], writes=["f_Fp"])
        tiles = []
        for qc in range(NQC if dbg != 1 else 0):
            for kt in range(4 * qc + 4):
                tiles.append((qc, kt))
        info = {}

        def s_stage(i):
            qc, kt = tiles[i]
            q0 = qc * 512
            j = kt - 4 * qc
            c0 = 128 * j if j > 0 else 0
            ncol = 512 - c0
            if kt == 0:
                qi = cnt["q"]
                cnt["q"] += 1
                info[("q", qc)] = qi
                fg = fgs[qi % 2]
                p.dma("sp", fg[:], fgT[h * 128:(h + 1) * 128, q0:q0 + 512], "f_fg%d" % (qi % 2), writes=["f_fg%d" % (qi % 2)])
            pss = PSS[cnt["s"] % 3]
            skey = "f_pss%d" % (cnt["s"] % 3)
            cnt["s"] += 1
            info[i] = (pss, skey, c0, ncol, j)
            ksl = slice(kt * 128, (kt + 1) * 128)
            qsl = slice(q0 + c0, q0 + 512)
            p.op("pe", lambda e: e.matmul(pss[:, 0:ncol], lhsT=ksb[:, ksl], rhs=qsb[:, qsl], start=True, stop=False),
                 reads=["f_k", "f_q"], writes=[skey], track=False)
            p.op("pe", lambda e: e.matmul(pss[:, 0:ncol], lhsT=ones_bf[0:96, 0:128], rhs=Fp[:, qsl], start=False, stop=False),
                 reads=["f_ones", "f_Fp"], writes=[skey], track=False)
            p.op("pe", lambda e: e.matmul(pss[:, 0:ncol], lhsT=Fp[:, ksl], rhs=nones_bf[0:96, 0:ncol], start=False, stop=True),
                 reads=["f_nones", "f_Fp"], writes=[skey])

        def e_stage(i):
            qc, kt = tiles[i]
            pss, skey, c0, ncol, j = info[i]
            pt = PT[cnt["pt"] % 3]
            ptkey = "f_pt%d" % (cnt["pt"] % 3)
            cnt["pt"] += 1
            info[("pt", i)] = (pt, ptkey)
            if j >= 0:
                td = tmpd[kt % 2]
                tdkey = "f_tmpd%d" % (kt % 2)
                p.op("dve", lambda e: e.tensor_scalar(out=td[:, 0:ncol], in0=pss[:, 0:ncol], scalar1=30.0, scalar2=None, op0=ALU.min),
                     reads=[skey], writes=[tdkey])
                p.op("act", lambda e: e.activation(out=pt[:, 0:ncol], in_=td[:, 0:ncol], func=AF.Exp), reads=[tdkey], writes=[ptkey])
                p.op("dve", lambda e: e.tensor_tensor(out=pt[:, 0:128], in0=pt[:, 0:128], in1=trisb[:], op=ALU.mult),
                     reads=[ptkey, "f_tri"], writes=[ptkey])
            else:
                p.op("act", lambda e: e.activation(out=pt[:], in_=pss[:], func=AF.Exp), reads=[skey], writes=[ptkey])

        def pv_stage(i):
            qc, kt = tiles[i]
            q0 = qc * 512
            nkt = 4 * qc + 4
            pss, skey, c0, ncol, j = info[i]
            pt, ptkey = info[("pt", i)]
            qi = info[("q", qc)]
            psn, psd = PSN[qi % 2], PSD[qi % 2]
            nkey, dkey = "f_psn%d" % (qi % 2), "f_psd%d" % (qi % 2)
            p.op("pe", lambda e: e.matmul(psn[:, c0:512], lhsT=vsb[:, kt, :], rhs=pt[:, 0:ncol], start=(kt == 0), stop=(kt == nkt - 1)),
                 reads=["f_v", ptkey], writes=[nkey], track=False)
            p.op("pe", lambda e: e.matmul(psd[:, c0:512], lhsT=ones_bf[:, 0:128], rhs=pt[:, 0:ncol], start=(kt == 0), stop=(kt == nkt - 1)),
                 reads=["f_ones", ptkey], writes=[dkey])
            if kt == nkt - 1:
                fg, fgkey = fgs[qi % 2], "f_fg%d" % (qi % 2)
                yob, yokey = yo[qi % 2], "f_yo%d" % (qi % 2)
                p.op("dve", lambda e: e.reciprocal(out=rden[:], in_=psd[:]), reads=[dkey], writes=["f_rden"])
                p.op("dve", lambda e: e.tensor_tensor(out=ynum[:], in0=psn[:], in1=rden[:], op=ALU.mult),
                     reads=[nkey, "f_rden"], writes=["f_ynum"])
                p.op("pool", lambda e: e.tensor_tensor(out=yob[:], in0=ynum[:], in1=fg[:], op=ALU.mult),
                     reads=["f_ynum", fgkey], writes=[yokey])
                p.dma("pool", yT[h * 128:(h + 1) * 128, q0:q0 + 512], yob[:], yokey, reads=[yokey])

        if tiles:
            s_stage(0)
        for i in range(len(tiles)):
            if i + 1 < len(tiles):
                s_stage(i + 1)
            e_stage(i)
            pv_stage(i)


EPS = 1e-6


def build_ssd(NBLK=16):
    nc = bass.Bass("TRN2", target_bir_lowering=False)
    dt = nc.dram_tensor
    SQ = NBLK * 512
    xbcT = dt("xbcT", [128, 6, SQ], BF16, kind="ExternalInput").ap()
    convw = dt("convw", [128, 6, 4], F32, kind="ExternalInput").ap()
    convb = dt("convb", [128, 6], F32, kind="ExternalInput").ap()
    dtv = dt("dtv", [128, SQ // 128, 8], F32, kind="ExternalInput").ap()
    rowc = dt("rowc", [1, 16 + 512], F32, kind="ExternalInput").ap()
    sz = dt("sz", [128, SQ // 128, 512], BF16, kind="ExternalInput").ap()
    cst = dt("cst", [128, 3, 128], F32, kind="ExternalInput").ap()
    y = dt("y", [128, SQ // 128, 512], BF16, kind="ExternalOutput").ap()
    p = Prog(nc)
    emit_ssd(p, NBLK, xbcT, convw, convb, dtv, rowc, sz, cst, y)
    p.finish()
    return nc


def emit_ssd(p, NBLK, xbcT, convw, convb, dtv, rowc, sz, cst, y):
    SQ = NBLK * 512
    xr = [p.sb("s_xr%d" % i, [128, 6, 515], BF16) for i in range(2)]
    cv = p.sb("s_cv", [128, 6, 512], BF16)
    acc = [p.sb("s_acc%d" % i, [128, 512], F32) for i in range(2)]
    cw = p.sb("s_cw", [128, 6, 4], F32)
    cb = p.sb("s_cb", [128, 6], F32)
    dts = p.sb("s_dt", [128, SQ // 128, 8], F32)
    rows = p.sb("s_rows", [128, 528], F32)
    Arow = p.sb("s_A", [128, 8], F32)
    csts = p.sb("s_cst", [128, 3, 128], F32)
    ident = p.sb("s_ident", [128, 128], BF16)
    ones = p.sb("s_ones", [128, 128], F32)
    mnb = p.sb("s_mnb", [128, 8, 128], F32)
    szs = [p.sb("s_sz%d" % i, [128, 4, 512], BF16) for i in range(2)]
    xs = p.sb("s_xs", [128, 512], F32)
    xd = p.sb("s_xd", [128, 512], BF16)
    xdw = p.sb("s_xdw", [128, 512], BF16)
    Btok = p.sb("s_Btok", [128, 128], BF16)
    cbT = p.sb("s_cbT", [128, 128], F32)
    da = p.sb("s_da", [128, 8], F32)
    acs = p.sb("s_acs", [128, 8], F32)
    tot = p.sb("s_tot", [128, 8], F32)
    wl = p.sb("s_wl", [128, 8], F32)
    eacs = p.sb("s_eacs", [128, 8], F32)
    cd = p.sb("s_cd", [128, 8], F32)
    X = p.sb("s_X", [128, 8, 128], F32)
    dif = p.sb("s_dif", [128, 8, 128], F32)
    dec = p.sb("s_dec", [128, 8, 128], F32)
    MT = p.sb("s_MT", [128, 8, 128], BF16)
    Sst = p.sb("s_S", [128, 512], F32)
    Sbf = p.sb("s_Sbf", [128, 512], BF16)
    t1 = p.sb("s_t1", [128, 512], F32)
    t2 = p.sb("s_t2", [128, 512], F32)
    t3 = p.sb("s_t3", [128, 512], F32)
    ssq = p.sb("s_ssq", [128, 2], F32)
    junk = p.sb("s_junk", [128, 512], F32)
    yst = [p.sb("s_yst%d" % i, [128, 4, 512], BF16) for i in range(2)]
    P_xs = p.ps("s_pxs", [128, 512])
    P_b = p.ps("s_pb", [128, 512])
    P_a = p.ps("s_pa", [128, 512])
    P_d = [p.ps("s_pd%d" % i, [128, 512]) for i in range(2)]
    P_y = p.ps("s_py", [128, 512])
    P_o = p.ps("s_po", [128, 512])
    P_s = p.ps("s_psb", [128, 512])

    p.dma("sp", cw[:], convw, "s_cw", writes=["s_cw"])
    p.dma("sp", cb[:], convb, "s_cb", writes=["s_cb"])
    p.dma("sp", dts[:], dtv, "s_dt", writes=["s_dt"])
    p.dma("sp", rows[:], rowc.partition_broadcast(128), "s_rows", writes=["s_rows"])
    p.dma("sp", csts[:], cst, "s_cst", writes=["s_cst"])
    p.op("act", lambda e: e.activation(out=Arow[:], in_=rows[:, 0:8], func=AF.Exp), reads=["s_rows"], writes=["s_A"])
    p.op("dve", lambda e: e.tensor_scalar(out=Arow[:], in0=Arow[:], scalar1=-1.0, scalar2=None, op0=ALU.mult),
         reads=["s_A"], writes=["s_A"])
    p.op("dve", lambda e: e.tensor_copy(out=ident[:], in_=csts[:, 2, :]), reads=["s_cst"], writes=["s_ident"])
    p.op("pool", lambda e: e.memset(ones[:], 1.0), writes=["s_ones"])
    p.op("pool", lambda e: e.memset(Sst[:], 0.0), writes=["s_S"])
    p.op("pool", lambda e: e.memset(Sbf[:], 0.0), writes=["s_Sbf"])
    p.op("dve", lambda e: e.tensor_copy(out=mnb[:], in_=csts[:, 1, :].unsqueeze(1).broadcast_to([128, 8, 128])),
         reads=["s_cst"], writes=["s_mnb"])
    tri = csts[:, 0, :]
    xbc_v = xbcT
    sz_v = sz
    y_v = y
    Db = rows[:, 8:16].unsqueeze(2).broadcast_to([128, 8, 64])

    def v3(t):
        return t.rearrange("p (h d) -> p h d", h=8)

    for blk in range(NBLK):
        bi = blk % 2
        xk = "s_xr%d" % bi
        if blk == 0:
            p.op("pool", lambda e: e.memset(xr[0][:, :, 0:3], 0.0), writes=[xk])
            p.dma("sp", xr[0][:, :, 3:515], xbc_v[:, :, 0:512], xk, writes=[xk])
        else:
            p.dma("sp", xr[bi][:], xbc_v[:, :, blk * 512 - 3:blk * 512 + 512], xk, writes=[xk])
        szk = "s_sz%d" % bi
        p.dma("sp", szs[bi][:], sz_v[:, blk * 4:(blk + 1) * 4, :], szk, writes=[szk])
        for cc in range(6):
            a = acc[cc % 2]
            ak = "s_acc%d" % (cc % 2)
            p.op("dve", lambda e, a=a, cc=cc, bi=bi: e.tensor_scalar(out=a[:], in0=xr[bi][:, cc, 0:512], scalar1=cw[:, cc, 0:1],
                                                                  scalar2=None, op0=ALU.mult), reads=[xk, "s_cw"], writes=[ak])
            for k in range(1, 4):
                p.op("dve", lambda e, a=a, cc=cc, k=k, bi=bi: e.scalar_tensor_tensor(
                    out=a[:], in0=xr[bi][:, cc, k:k + 512], scalar=cw[:, cc, k:k + 1], in1=a[:], op0=ALU.mult, op1=ALU.add),
                     reads=[xk, "s_cw", ak], writes=[ak])
            p.op("act", lambda e, a=a, cc=cc: e.activation(out=cv[:, cc, :], in_=a[:], func=AF.Silu, bias=cb[:, cc:cc + 1], scale=1.0),
                 reads=[ak, "s_cb"], writes=["s_cv"])
        yk = "s_yst%d" % bi
        for j in range(4):
            c = blk * 4 + j
            jsl = slice(j * 128, (j + 1) * 128)
            for cc in range(4):
                p.op("pe", lambda e, cc=cc, jsl=jsl: e.matmul(P_xs[:, cc * 128:(cc + 1) * 128], lhsT=cv[:, cc, jsl], rhs=ident[:],
                                                             start=True, stop=True), reads=["s_cv", "s_ident"], writes=["s_pxs"],
                     track=(cc == 3))
            p.op("pe", lambda e, jsl=jsl: e.matmul(P_b[:, 0:128], lhsT=cv[:, 4, jsl], rhs=ident[:], start=True, stop=True),
                 reads=["s_cv", "s_ident"], writes=["s_pb"], track=False)
            p.op("pe", lambda e, jsl=jsl: e.matmul(P_b[:, 128:256], lhsT=cv[:, 4, jsl], rhs=cv[:, 5, jsl], start=True, stop=True),
                 reads=["s_cv"], writes=["s_pb"])
            p.op("act", lambda e: e.activation(out=xs[:], in_=P_xs[:], func=AF.Copy), reads=["s_pxs"], writes=["s_xs"])
            p.op("act", lambda e: e.activation(out=Btok[:], in_=P_b[:, 0:128], func=AF.Copy), reads=["s_pb"], writes=["s_Btok"])
            p.op("act", lambda e: e.activation(out=cbT[:], in_=P_b[:, 128:256], func=AF.Copy), reads=["s_pb"], writes=["s_cbT"])
            p.op("dve", lambda e, c=c: e.tensor_tensor(out=da[:], in0=dts[:, c, :], in1=Arow[:], op=ALU.mult),
                 reads=["s_dt", "s_A"], writes=["s_da"])
            p.op("pe", lambda e: e.matmul(P_a[:, 0:8], lhsT=tri, rhs=da[:], start=True, stop=True),
                 reads=["s_cst", "s_da"], writes=["s_pa"], track=False)
            p.op("pe", lambda e: e.matmul(P_a[:, 8:16], lhsT=ones[:], rhs=da[:], start=True, stop=True),
                 reads=["s_ones", "s_da"], writes=["s_pa"])
            p.op("dve", lambda e: e.tensor_copy(out=acs[:], in_=P_a[:, 0:8]), reads=["s_pa"], writes=["s_acs"])
            p.op("dve", lambda e: e.tensor_copy(out=tot[:], in_=P_a[:, 8:16]), reads=["s_pa"], writes=["s_tot"])
            p.op("dve", lambda e: e.tensor_tensor(out=wl[:], in0=tot[:], in1=acs[:], op=ALU.subtract),
                 reads=["s_tot", "s_acs"], writes=["s_wl"])
            p.op("act", lambda e: e.activation(out=wl[:], in_=wl[:], func=AF.Exp), reads=["s_wl"], writes=["s_wl"])
            p.op("act", lambda e: e.activation(out=eacs[:], in_=acs[:], func=AF.Exp), reads=["s_acs"], writes=["s_eacs"])
            p.op("act", lambda e: e.activation(out=cd[:], in_=tot[:], func=AF.Exp), reads=["s_tot"], writes=["s_cd"])
            p.op("dve", lambda e: e.tensor_tensor(out=X[:], in0=tri.unsqueeze(1).broadcast_to([128, 8, 128]),
                                                  in1=da[:].unsqueeze(2).broadcast_to([128, 8, 128]), op=ALU.mult),
                 reads=["s_cst", "s_da"], writes=["s_X"])
            for hh in range(2):
                pk = "s_pd%d" % hh
                p.op("pe", lambda e, hh=hh: e.matmul(P_d[hh][:], lhsT=ones[:], rhs=X[:, hh * 4:(hh + 1) * 4, :], start=True, stop=False),
                     reads=["s_ones", "s_X"], writes=[pk], track=False)
                p.op("pe", lambda e, hh=hh: e.matmul(P_d[hh][:], lhsT=csts[:, 2, :], rhs=mnb[:, hh * 4:(hh + 1) * 4, :], start=False, stop=True),
                     reads=["s_cst", "s_mnb"], writes=[pk])
                p.op("dve", lambda e, hh=hh: e.tensor_tensor(
                    out=dif[:, hh * 4:(hh + 1) * 4, :], in0=P_d[hh][:].rearrange("p (h l) -> p h l", h=4),
                    in1=acs[:, hh * 4:(hh + 1) * 4].unsqueeze(2).broadcast_to([128, 4, 128]), op=ALU.subtract),
                     reads=[pk, "s_acs"], writes=["s_dif"])
            p.op("act", lambda e: e.activation(out=dec[:], in_=dif[:], func=AF.Exp), reads=["s_dif"], writes=["s_dec"])
            p.op("dve", lambda e: e.tensor_tensor(out=MT[:], in0=dec[:], in1=cbT[:].unsqueeze(1).broadcast_to([128, 8, 128]), op=ALU.mult),
                 reads=["s_dec", "s_cbT"], writes=["s_MT"])
            p.op("pool", lambda e, c=c: e.tensor_tensor(out=v3(xd[:]), in0=v3(xs[:]),
                                                       in1=dts[:, c, :].unsqueeze(2).broadcast_to([128, 8, 64]), op=ALU.mult),
                 reads=["s_xs", "s_dt"], writes=["s_xd"])
            p.op("pool", lambda e: e.tensor_tensor(out=v3(xdw[:]), in0=v3(xd[:]), in1=wl[:].unsqueeze(2).broadcast_to([128, 8, 64]), op=ALU.mult),
                 reads=["s_xd", "s_wl"], writes=["s_xdw"])
            for h in range(8):
                p.op("pe", lambda e, h=h: e.matmul(P_y[:, h * 64:(h + 1) * 64], lhsT=MT[:, h, :], rhs=xd[:, h * 64:(h + 1) * 64],
                                                   start=True, stop=True), reads=["s_MT", "s_xd"], writes=["s_py"], track=(h == 7))
            p.op("pe", lambda e, jsl=jsl: e.matmul(P_o[:], lhsT=cv[:, 5, jsl], rhs=Sbf[:], start=True, stop=True),
                 reads=["s_cv", "s_Sbf"], writes=["s_po"])
            p.op("pe", lambda e: e.matmul(P_s[:], lhsT=Btok[:], rhs=xdw[:], start=True, stop=True),
                 reads=["s_Btok", "s_xdw"], writes=["s_psb"])
            p.op("dve", lambda e: e.tensor_tensor(out=v3(t1[:]), in0=v3(P_o[:]), in1=eacs[:].unsqueeze(2).broadcast_to([128, 8, 64]), op=ALU.mult),
                 reads=["s_po", "s_eacs"], writes=["s_t1"])
            p.op("dve", lambda e: e.tensor_tensor(out=t2[:], in0=P_y[:], in1=t1[:], op=ALU.add), reads=["s_py", "s_t1"], writes=["s_t2"])
            p.op("pool", lambda e: e.tensor_tensor(out=v3(t3[:]), in0=v3(xs[:]), in1=Db, op=ALU.mult), reads=["s_xs", "s_rows"], writes=["s_t3"])
            p.op("pool", lambda e: e.tensor_tensor(out=t2[:], in0=t2[:], in1=t3[:], op=ALU.add), reads=["s_t2", "s_t3"], writes=["s_t2"])
            p.op("dve", lambda e: e.tensor_tensor(out=v3(Sst[:]), in0=v3(Sst[:]), in1=cd[:].unsqueeze(2).broadcast_to([128, 8, 64]), op=ALU.mult),
                 reads=["s_S", "s_cd"], writes=["s_S"])
            p.op("dve", lambda e: e.tensor_tensor(out=Sst[:], in0=P_s[:], in1=Sst[:], op=ALU.add), reads=["s_psb", "s_S"], writes=["s_S"])
            p.op("act", lambda e: e.activation(out=Sbf[:], in_=Sst[:], func=AF.Copy), reads=["s_S"], writes=["s_Sbf"])
            p.op("dve", lambda e, j=j, bi=bi: e.tensor_tensor(out=t2[:], in0=t2[:], in1=szs[bi][:, j, :], op=ALU.mult),
                 reads=["s_t2", szk], writes=["s_t2"])
            p.op("pool", lambda e: e.memset(ssq[:], 0.0), writes=["s_ssq"])
            p.op("act", lambda e: e.activation(out=junk[:], in_=t2[:], func=AF.Square, accum_out=ssq[:, 0:1]),
                 reads=["s_t2"], writes=["s_junk", "s_ssq"])
            p.op("dve", lambda e: e.tensor_scalar(out=ssq[:, 1:2], in0=ssq[:, 0:1], scalar1=1.0 / 512, scalar2=EPS, op0=ALU.mult, op1=ALU.add),
                 reads=["s_ssq"], writes=["s_ssq"])
            p.op("act", lambda e: e.activation(out=ssq[:, 1:2], in_=ssq[:, 1:2], func=AF.Ln), reads=["s_ssq"], writes=["s_ssq"])
            p.op("act", lambda e: e.activation(out=ssq[:, 1:2], in_=ssq[:, 1:2], func=AF.Exp, scale=-0.5), reads=["s_ssq"], writes=["s_ssq"])
            p.op("dve", lambda e, j=j, bi=bi: e.scalar_tensor_tensor(out=yst[bi][:, j, :], in0=t2[:], scalar=ssq[:, 1:2], in1=rows[:, 16:528],
                                                                    op0=ALU.mult, op1=ALU.mult),
                 reads=["s_t2", "s_ssq", "s_rows"], writes=[yk])
        p.dma("pool", y_v[:, blk * 4:(blk + 1) * 4, :], yst[bi][:], yk, reads=[yk])


NITER = 18
TOPK = 256


def build_dsa(NI=8):
    nc = bass.Bass("TRN2", target_bir_lowering=False)
    dt = nc.dram_tensor
    NS = 2 * NI
    SK = NI * 1024
    dqT = dt("dqT", [NS, 128, 2, 512], BF16, kind="ExternalInput").ap()
    dgT = dt("dgT", [NS, 128, 2, 512], BF16, kind="ExternalInput").ap()
    dkT = dt("dkT", [128, 2, SK], BF16, kind="ExternalInput").ap()
    dv = dt("dv", [128, SK // 128, 256], BF16, kind="ExternalInput").ap()
    iqT = dt("iqT", [NS, 128, 8, 128], BF16, kind="ExternalInput").ap()
    ikT2 = dt("ikT2", [128, SK], BF16, kind="ExternalInput").ap()
    iw = dt("iw", [128, NS, 16], F32, kind="ExternalInput").ap()
    cmask = dt("cmask", [NS, 128, 1024], F32, kind="ExternalInput").ap()
    identd = dt("ident", [128, 128], BF16, kind="ExternalInput").ap()
    yT = dt("yT", [NS, 128, 2, 512], BF16, kind="ExternalOutput").ap()
    p = Prog(nc)
    emit_dsa(p, NI, dqT, dgT, dkT, dv, iqT, ikT2, iw, cmask, identd, yT)
    p.finish()
    return nc


def emit_dsa(p, NI, dqT, dgT, dkT, dv, iqT, ikT2, iw, cmask, identd, yT):
    NS = 2 * NI
    SK = NI * 1024
    ks = p.sb("d_k", [128, 2, SK], BF16)
    vs = p.sb("d_v", [128, SK // 128, 256], BF16)
    iks = p.sb("d_ik", [128, SK], BF16)
    iws = p.sb("d_iw", [128, NS, 16], F32)
    ident = p.sb("d_ident", [128, 128], BF16)
    ones = p.sb("d_ones", [128, 128], BF16)
    qs = [p.sb("d_q%d" % i, [128, 2, 512], BF16) for i in range(2)]
    gs = [p.sb("d_g%d" % i, [128, 2, 512], BF16) for i in range(2)]
    iqs = [p.sb("d_iq%d" % i, [128, 8, 128], BF16) for i in range(2)]
    cms = [p.sb("d_cm%d" % i, [128, 1024], F32) for i in range(2)]
    sc = p.sb("d_sc", [128, SK], F32)
    junk = p.sb("d_junk", [128, SK], BF16)
    maskq = p.sb("d_maskq", [128, SK], BF16)
    maskT = p.sb("d_maskT", [128, SK // 128, 128], BF16)
    R = [p.sb("d_R%d" % i, [128, 512], F32) for i in range(3)]
    st = p.sb("d_st", [128, 16], F32)
    PT = [p.sb("d_pt%d" % i, [128, 512], BF16) for i in range(3)]
    rden = p.sb("d_rden", [128, 512], F32)
    ynum = p.sb("d_ynum", [128, 512], F32)
    yo = [p.sb("d_yo%d" % i, [128, 2, 512], BF16) for i in range(2)]
    PI = [p.ps("d_pi%d" % i, [128, 512]) for i in range(2)]
    PTr = p.ps("d_ptr", [128, 512])
    PSS = [p.ps("d_pss%d" % i, [128, 512]) for i in range(2)]
    PSN = p.ps("d_psn", [128, 512])
    PSD = p.ps("d_psd", [128, 512])

    p.dma("sp", ks[:], dkT, "d_k", writes=["d_k"])
    p.dma("sp", vs[:], dv, "d_v", writes=["d_v"])
    p.dma("sp", iks[:], ikT2, "d_ik", writes=["d_ik"])
    p.dma("sp", iws[:], iw, "d_iw", writes=["d_iw"])
    p.dma("sp", ident[:], identd, "d_ident", writes=["d_ident"])
    p.op("pool", lambda e: e.memset(ones[:], 1.0), writes=["d_ones"])
    cnt = dict(r=0, pi=0, s=0, pt=0)

    def col(i):
        return st[:, i:i + 1]

    for sl in range(NS):
        i = sl // 2
        nk = 8 * (i + 1)
        NK = nk * 128
        b = sl % 2
        qk, gk, iqk, cmk, yok = "d_q%d" % b, "d_g%d" % b, "d_iq%d" % b, "d_cm%d" % b, "d_yo%d" % b
        p.dma("sp", qs[b][:], dqT[sl], qk, writes=[qk])
        p.dma("sp", gs[b][:], dgT[sl], gk, writes=[gk])
        p.dma("sp", iqs[b][:], iqT[sl], iqk, writes=[iqk])
        p.dma("sp", cms[b][:], cmask[sl], cmk, writes=[cmk])
        for kc in range(nk // 4):
            ksl = slice(kc * 512, (kc + 1) * 512)
            for h in range(16):
                pr, hf = h // 2, h % 2
                pi = PI[cnt["pi"] % 2]
                pik = "d_pi%d" % (cnt["pi"] % 2)
                cnt["pi"] += 1
                r = R[cnt["r"] % 3]
                rk = "d_R%d" % (cnt["r"] % 3)
                cnt["r"] += 1
                p.op("pe", lambda e, pi=pi, pr=pr, hf=hf, ksl=ksl, b=b: e.matmul(
                    pi[:], lhsT=iqs[b][hf * 64:(hf + 1) * 64, pr, :], rhs=iks[hf * 64:(hf + 1) * 64, ksl], start=True, stop=True),
                     reads=[iqk, "d_ik"], writes=[pik])
                p.op("act", lambda e, pi=pi, r=r: e.activation(out=r[:], in_=pi[:], func=AF.Relu), reads=[pik], writes=[rk])
                if h == 0:
                    p.op("dve", lambda e, r=r, ksl=ksl, sl=sl: e.tensor_scalar(out=sc[:, ksl], in0=r[:], scalar1=iws[:, sl, 0:1], scalar2=None,
                                                                            op0=ALU.mult), reads=[rk, "d_iw"], writes=["d_sc"])
                else:
                    p.op("dve", lambda e, r=r, ksl=ksl, sl=sl, h=h: e.scalar_tensor_tensor(
                        out=sc[:, ksl], in0=r[:], scalar=iws[:, sl, h:h + 1], in1=sc[:, ksl], op0=ALU.mult, op1=ALU.add),
                         reads=[rk, "d_iw", "d_sc"], writes=["d_sc"])
        p.op("dve", lambda e, NK=NK: e.tensor_reduce(out=col(8), in_=sc[:, 0:NK], axis=AX.X, op=ALU.max), reads=["d_sc"], writes=["d_st"])
        p.op("dve", lambda e, NK=NK: e.tensor_reduce(out=col(9), in_=sc[:, 0:NK], axis=AX.X, op=ALU.min), reads=["d_sc"], writes=["d_st"])
        p.op("dve", lambda e: e.tensor_scalar(out=col(0), in0=col(9), scalar1=-1.0, scalar2=None, op0=ALU.add), reads=["d_st"], writes=["d_st"])
        p.op("dve", lambda e: e.tensor_tensor(out=col(1), in0=col(8), in1=col(9), op=ALU.subtract), reads=["d_st"], writes=["d_st"])
        p.op("dve", lambda e: e.tensor_scalar(out=col(1), in0=col(1), scalar1=2.0, scalar2=None, op0=ALU.add), reads=["d_st"], writes=["d_st"])
        p.op("dve", lambda e, NK=NK, b=b: e.tensor_tensor(out=sc[:, NK - 1024:NK], in0=sc[:, NK - 1024:NK], in1=cms[b][:], op=ALU.add),
             reads=["d_sc", cmk], writes=["d_sc"])
        for it in range(NITER):
            cit = 0.5 ** (it + 1)
            p.op("dve", lambda e, cit=cit: e.tensor_scalar(out=col(4), in0=col(1), scalar1=cit, scalar2=None, op0=ALU.mult), reads=["d_st"], writes=["d_st"])
            p.op("dve", lambda e: e.tensor_tensor(out=col(2), in0=col(0), in1=col(4), op=ALU.add), reads=["d_st"], writes=["d_st"])
            p.op("dve", lambda e, NK=NK: e.tensor_scalar(out=junk[:, 0:NK], in0=sc[:, 0:NK], scalar1=col(2), scalar2=None, op0=ALU.is_ge,
                                                         op1=ALU.add, accum_out=col(3)), reads=["d_sc", "d_st"], writes=["d_junk", "d_st"])
            p.op("dve", lambda e: e.scalar_tensor_tensor(out=col(5), in0=col(3), scalar=TOPK - 0.5, in1=col(4), op0=ALU.is_gt, op1=ALU.mult),
                 reads=["d_st"], writes=["d_st"])
            p.op("dve", lambda e: e.tensor_tensor(out=col(0), in0=col(0), in1=col(5), op=ALU.add), reads=["d_st"], writes=["d_st"])
        p.op("dve", lambda e, NK=NK: e.tensor_scalar(out=maskq[:, 0:NK], in0=sc[:, 0:NK], scalar1=col(0), scalar2=None, op0=ALU.is_ge),
             reads=["d_sc", "d_st"], writes=["d_maskq"])
        for k4 in range(nk // 4):
            for jj in range(4):
                kt = k4 * 4 + jj
                p.op("pe", lambda e, kt=kt, jj=jj: e.matmul(PTr[:, jj * 128:(jj + 1) * 128], lhsT=maskq[:, kt * 128:(kt + 1) * 128], rhs=ident[:],
                                                           start=True, stop=True), reads=["d_maskq", "d_ident"], writes=["d_ptr"], track=(jj == 3))
            p.op("act", lambda e, k4=k4: e.activation(out=maskT[:, k4 * 4:(k4 + 1) * 4, :], in_=PTr[:].rearrange("p (a t) -> p a t", a=4), func=AF.Copy),
                 reads=["d_ptr"], writes=["d_maskT"])
        for gg in range(2):
            st_ = {}

            def s_stage(kt, gg=gg, st_=st_, b=b, qk=qk):
                pss = PSS[cnt["s"] % 2]
                sk = "d_pss%d" % (cnt["s"] % 2)
                cnt["s"] += 1
                st_[kt] = (pss, sk)
                p.op("pe", lambda e: e.matmul(pss[:], lhsT=ks[:, gg, kt * 128:(kt + 1) * 128], rhs=qs[b][:, gg, :], start=True, stop=True),
                     reads=["d_k", qk], writes=[sk])

            def rest(kt, gg=gg, st_=st_, b=b, nk=nk):
                pss, sk = st_[kt]
                pt = PT[cnt["pt"] % 3]
                ptk = "d_pt%d" % (cnt["pt"] % 3)
                cnt["pt"] += 1
                p.op("act", lambda e: e.activation(out=pt[:], in_=pss[:], func=AF.Exp), reads=[sk], writes=[ptk])
                for a4 in range(4):
                    p.op("dve", lambda e, a4=a4: e.tensor_tensor(out=pt[:, a4 * 128:(a4 + 1) * 128], in0=pt[:, a4 * 128:(a4 + 1) * 128],
                                                                 in1=maskT[:, kt, :], op=ALU.mult), reads=[ptk, "d_maskT"], writes=[ptk])
                p.op("pe", lambda e: e.matmul(PSN[:], lhsT=vs[:, kt, gg * 128:(gg + 1) * 128], rhs=pt[:], start=(kt == 0), stop=(kt == nk - 1)),
                     reads=["d_v", ptk], writes=["d_psn"], track=False)
                p.op("pe", lambda e: e.matmul(PSD[:], lhsT=ones[:], rhs=pt[:], start=(kt == 0), stop=(kt == nk - 1)),
                     reads=["d_ones", ptk], writes=["d_psd"])
            s_stage(0)
            for kt in range(nk):
                if kt + 1 < nk:
                    s_stage(kt + 1)
                rest(kt)
            p.op("dve", lambda e: e.reciprocal(out=rden[:], in_=PSD[:]), reads=["d_psd"], writes=["d_rden"])
            p.op("dve", lambda e: e.tensor_tensor(out=ynum[:], in0=PSN[:], in1=rden[:], op=ALU.mult), reads=["d_psn", "d_rden"], writes=["d_ynum"])
            p.op("pool", lambda e, gg=gg, b=b: e.tensor_tensor(out=yo[b][:, gg, :], in0=ynum[:], in1=gs[b][:, gg, :], op=ALU.mult),
                 reads=["d_ynum", gk], writes=[yok])
        p.dma("pool", yT[sl], yo[b][:], yok, reads=[yok])


D = 2048
T3 = 2048


def build_p3(NG=4):
    nc = bass.Bass("TRN2", target_bir_lowering=False)
    dt = nc.dram_tensor
    TT = NG * 512
    yT = dt("yT", [NG, 128, 32, 512], BF16, kind="ExternalInput").ap()
    gT = dt("gT", [NG, 16, 128, 3, 512], BF16, kind="ExternalInput").ap()
    xT = dt("xT", [NG, 16, 128, 512], F32, kind="ExternalInput").ap()
    w_o = dt("w_o", [16, 128, 32, 128], F32, kind="ExternalInput").ap()
    w_out = dt("w_out", [16, 128, 16, 128], F32, kind="ExternalInput").ap()
    gate = dt("gate", [128, 16], F32, kind="ExternalInput").ap()
    x1T = dt("x1T", [NG, 16, 128, 512], F32, kind="ExternalOutput").ap()

    p = Prog(nc)
    ysb = p.sb("ysb", [128, 2, 32, 512], BF16)
    gsb = [p.sb("gsb%d" % i, [128, 2, 3, 512], BF16) for i in range(2)]
    msb = p.sb("msb", [128, 2, 16, 512], BF16)
    wst = [p.sb("wst%d" % i, [128, 32, 128], F32) for i in range(2)]
    wbf = [p.sb("wbf%d" % i, [128, 32, 128], BF16) for i in range(2)]
    xsb = [p.sb("xsb%d" % i, [128, 512], F32) for i in range(2)]
    osb = [p.sb("osb%d" % i, [128, 512], F32) for i in range(2)]
    t0 = [p.sb("t0_%d" % i, [128, 512], F32) for i in range(2)]
    t1 = [p.sb("t1_%d" % i, [128, 512], F32) for i in range(2)]
    gt = p.sb("gt", [128, 16], F32)
    PS = [p.ps("ps%d" % i, [128, 512]) for i in range(8)]

    p.dma("sp", gt[:], gate, "gt", writes=["gt"])
    KB = [(0, 8), (8, 8), (16, 16)]
    cnt = 0
    NH2 = 2 if NG >= 2 else 1
    for gp in range(NG // NH2):
        for hf in range(NH2):
            p.dma("sp", ysb[:, hf], yT[gp * NH2 + hf], "ysb", writes=["ysb"])
        for nn in range(16):
            wi = cnt % 2
            cnt += 1
            p.dma("sp", wst[wi][:], w_o[nn], "wst%d" % wi, writes=["wst%d" % wi])
            p.op("pool", lambda e, wi=wi: e.tensor_copy(out=wbf[wi][:], in_=wst[wi][:]),
                 reads=["wst%d" % wi], writes=["wbf%d" % wi])
            for hf in range(NH2):
                p.dma("sp", gsb[wi][:, hf], gT[gp * NH2 + hf, nn], "gsb%d" % wi, writes=["gsb%d" % wi])
            for hf in range(NH2):
                pss = [PS[hf * 3 + i] for i in range(3)]
                pkeys = ["ps%d" % (hf * 3 + i) for i in range(3)]
                for i, (k0, nk) in enumerate(KB):
                    for kk in range(nk):
                        kc = k0 + kk
                        p.op("pe", lambda e, i=i, kc=kc, kk=kk, nk=nk, wi=wi, pss=pss, hf=hf: e.matmul(
                            pss[i][:], lhsT=wbf[wi][:, kc, :], rhs=ysb[:, hf, kc, :], start=(kk == 0), stop=(kk == nk - 1)),
                             reads=["wbf%d" % wi, "ysb"], writes=[pkeys[i]], track=(kk == nk - 1))
                a = t0[hf]
                b = t1[hf]
                ak = "t0_%d" % hf
                bk = "t1_%d" % hf
                p.op("dve", lambda e, a=a, wi=wi, pss=pss, hf=hf: e.tensor_tensor(out=a[:], in0=pss[0][:], in1=gsb[wi][:, hf, 0, :], op=ALU.mult),
                     reads=[pkeys[0], "gsb%d" % wi], writes=[ak])
                p.op("dve", lambda e, b=b, wi=wi, pss=pss, hf=hf: e.tensor_tensor(out=b[:], in0=pss[1][:], in1=gsb[wi][:, hf, 1, :], op=ALU.mult),
                     reads=[pkeys[1], "gsb%d" % wi], writes=[bk])
                p.op("pool", lambda e, a=a, b=b: e.tensor_tensor(out=a[:], in0=a[:], in1=b[:], op=ALU.add),
                     reads=[ak, bk], writes=[ak])
                p.op("dve", lambda e, b=b, wi=wi, pss=pss, hf=hf: e.tensor_tensor(out=b[:], in0=pss[2][:], in1=gsb[wi][:, hf, 2, :], op=ALU.mult),
                     reads=[pkeys[2], "gsb%d" % wi], writes=[bk])
                p.op("pool", lambda e, a=a, b=b, nn=nn, hf=hf: e.tensor_tensor(out=msb[:, hf, nn, :], in0=a[:], in1=b[:], op=ALU.add),
                     reads=[ak, bk], writes=["msb"])
        for mc in range(16):
            wi = cnt % 2
            cnt += 1
            p.dma("sp", wst[wi][:, 0:16, :], w_out[mc], "wst%d" % wi, writes=["wst%d" % wi])
            p.op("pool", lambda e, wi=wi: e.tensor_copy(out=wbf[wi][:, 0:16, :], in_=wst[wi][:, 0:16, :]),
                 reads=["wst%d" % wi], writes=["wbf%d" % wi])
            for hf in range(NH2):
                g = gp * NH2 + hf
                xi = hf
                p.dma("sp", xsb[xi][:], xT[g, mc], "xsb%d" % xi, writes=["xsb%d" % xi])
                ps = PS[6 + hf]
                pk = "ps%d" % (6 + hf)
                for kc in range(16):
                    p.op("pe", lambda e, kc=kc, wi=wi, ps=ps, hf=hf: e.matmul(ps[:], lhsT=wbf[wi][:, kc, :], rhs=msb[:, hf, kc, :],
                                                                            start=(kc == 0), stop=(kc == 15)),
                         reads=["wbf%d" % wi, "msb"], writes=[pk], track=(kc == 15))
                p.op("dve", lambda e, xi=xi, ps=ps, mc=mc: e.scalar_tensor_tensor(
                    out=osb[xi][:], in0=ps[:], scalar=gt[:, mc:mc + 1], in1=xsb[xi][:], op0=ALU.mult, op1=ALU.add),
                     reads=[pk, "gt", "xsb%d" % xi], writes=["osb%d" % xi])
                p.dma("pool", x1T[g, mc], osb[xi][:], "osb%d" % xi, reads=["osb%d" % xi])
    p.finish()
    return nc


def fm(v, n):
    return np.ascontiguousarray(v.reshape(n, 128).T)

def core_cols(g):
    idx = []
    def r(f, off, n):
        s, _ = FAM[f]
        idx.extend(range(s + off, s + off + n))
    r("fq", 256 * g, 256); r("fk", 256 * g, 256); r("fv", 256 * g, 256); r("fg", 256 * g, 256)
    r("dq", 256 * g, 256); r("dk", 128 * (g % 2), 128); r("dv", 128 * (g % 2), 128)
    r("iq", 256 * g, 256); r("dg", 256 * g, 256); r("sz", 512 * g, 512); r("sxbc", 768 * g, 768)
    r("mg", 1536 * g, 1536); r("ik", 0, 64); r("iw", 0, 16); r("ff", 0, 8); r("sdt", 0, 32)
    return np.array(idx, dtype=np.int64)

def rowp_for(inp, l):
    theta = 500000.0
    if16 = (theta ** (-np.arange(16, dtype=np.float32) / 16)).astype(np.float32)
    if8 = (theta ** (-np.arange(8, dtype=np.float32) / 8)).astype(np.float32)
    rowp = np.zeros((1, NRP), np.float32)
    rowp[0, 0:128] = inp["fox_q_norm"][l]; rowp[0, 128:256] = inp["fox_k_norm"][l]
    rowp[0, 256:384] = inp["dsa_q_norm"][l]; rowp[0, 384:512] = inp["dsa_k_norm"][l]
    rowp[0, 512:520] = inp["b_fox_f"][l]; rowp[0, 520:552] = inp["dt_bias"][l]
    rowp[0, 552:568] = if16; rowp[0, 568:576] = if8; rowp[0, 576:592] = if16; rowp[0, 592:600] = if8
    return rowp

def p1_maps(xT_b, inp, l, ntg=4):
    rowp = rowp_for(inp, l)
    maps = []
    for core in range(8):
        b, g = core // 4, core % 4
        cols = core_cols(g)
        ntok = ntg * T1
        posc = inp["positions"][b, :ntok].astype(np.int32)
        maps.append(dict(
            xT=np.ascontiguousarray(xT_b[b][:, :ntok]), cvec=fm(inp["c"][b], 16), w_ada=inp["w_ada"][l],
            b_ada=fm(inp["b_ada"][l], 48), norm_w=fm(inp["norm_w"][l], 16),
            w_in=np.ascontiguousarray(inp["w_in"][l][:, cols]), rowp=rowp,
            b_merge=np.ascontiguousarray(inp["b_merge"][l].reshape(1, 3 * D)[:, g * 1536:(g + 1) * 1536]),
            pos=np.ascontiguousarray(posc.reshape(ntok // 128, 128).T)))
    return maps


def p3_maps_one(yT, gT, xT, w_o, w_out, gate):
    TT = yT.shape[1]; NG = TT // 512
    y4 = np.ascontiguousarray(yT.reshape(32, 128, NG, 512).transpose(2, 1, 0, 3)).astype(BF)
    g4 = np.ascontiguousarray(gT.reshape(3, 16, 128, NG, 512).transpose(3, 1, 2, 0, 4)).astype(BF)
    x4 = np.ascontiguousarray(xT.reshape(16, 128, NG, 512).transpose(2, 0, 1, 3)).astype(np.float32)
    return dict(yT=y4, gT=g4, xT=x4, w_o=w_o, w_out=w_out, gate=gate)

def p3_weights(inp, l):
    w_o = np.concatenate([inp["w_o_fox"][l], inp["w_o_dsa"][l], inp["w_o_ssd"][l]], 0)
    w_o4 = np.ascontiguousarray(w_o.reshape(32, 128, 16, 128).transpose(2, 1, 0, 3))
    w_out4 = np.ascontiguousarray(inp["w_out"][l].reshape(16, 128, 16, 128).transpose(2, 1, 0, 3))
    return w_o4, w_out4

def p3_unpack(x1):
    NG = x1.shape[0]
    return np.ascontiguousarray(x1.transpose(1, 2, 0, 3).reshape(2048, NG * 512))

def fox_v_layout(v_tok, NH):
    SQ = v_tok.shape[0]
    return np.ascontiguousarray(v_tok.reshape(SQ // 128, 128, NH, 128).transpose(2, 1, 0, 3))


_NC_CACHE = {}


def _get_nc(name, builder):
    if name not in _NC_CACHE:
        _NC_CACHE[name] = builder()
    return _NC_CACHE[name]


_LAUNCH_LOG = []


def _run(nc, maps, tag=""):
    res = run_bass_kernel_spmd(nc, maps, core_ids=list(range(8)))
    et = getattr(res, "exec_time_ns", None)
    _LAUNCH_LOG.append((tag, et))
    print("[kernel] launch %s exec_time_ns=%s" % (tag, et), flush=True)
    return res.results


def _f32(a):
    return np.asarray(a).astype(np.float32)


def ssd_maps(xbc_tok, conv_w, conv_b, dt_tok, a_log, d_skip, ssd_norm_g, sz_tok):
    SQ = xbc_tok.shape[0]
    xbcT = np.ascontiguousarray(xbc_tok.T.reshape(6, 128, SQ).transpose(1, 0, 2)).astype(BF)
    convw = np.ascontiguousarray(conv_w.T.reshape(6, 128, 4).transpose(1, 0, 2)).astype(np.float32)
    convb = np.ascontiguousarray(conv_b.reshape(6, 128).T).astype(np.float32)
    dtv = np.ascontiguousarray(dt_tok.reshape(SQ // 128, 128, 8).transpose(1, 0, 2)).astype(np.float32)
    rowc = np.concatenate([a_log, d_skip, ssd_norm_g]).reshape(1, -1).astype(np.float32)
    szm = np.ascontiguousarray(sz_tok.reshape(SQ // 128, 128, 512).transpose(1, 0, 2)).astype(BF)
    tri = np.triu(np.ones((128, 128), np.float32))
    cst = np.ascontiguousarray(np.stack([tri, (tri - 1) * 30000.0, np.eye(128, dtype=np.float32)], 1)).astype(np.float32)
    return dict(xbcT=xbcT, convw=convw, convb=convb, dtv=dtv, rowc=rowc, sz=szm, cst=cst)


def dsa_maps(j, NI, dq, dk, dv, iq, ik, iw, dg):
    NS = 2 * NI
    SK = NI * 1024
    qts = []
    for i in range(NI):
        qts += [8 * i + j, 8 * i + 7 - j]

    def qlay(a, qt):
        t = a[qt * 128:(qt + 1) * 128].reshape(128, 2, 4, 128)
        return np.ascontiguousarray(t.transpose(3, 1, 2, 0).reshape(128, 2, 512))
    dqT = np.stack([qlay(dq, qt) for qt in qts]).astype(BF)
    dgT = np.stack([qlay(dg, qt) for qt in qts]).astype(BF)
    dkT = np.ascontiguousarray(dk[:SK].transpose(2, 1, 0)).astype(BF)
    dvm = np.ascontiguousarray(dv[:SK].reshape(SK // 128, 128, 256).transpose(1, 0, 2)).astype(BF)

    def iqlay(qt):
        t = iq[qt * 128:(qt + 1) * 128].reshape(128, 8, 2, 64)
        return np.ascontiguousarray(t.transpose(2, 3, 1, 0).reshape(128, 8, 128))
    iqT = np.stack([iqlay(qt) for qt in qts]).astype(BF)
    ikT = ik[:SK].T
    ikT2 = np.ascontiguousarray(np.concatenate([ikT, ikT], 0)).astype(BF)
    iwm = np.ascontiguousarray(np.stack([iw[qt * 128:(qt + 1) * 128] for qt in qts], 1)).astype(np.float32)
    cm = np.zeros((NS, 128, 1024), np.float32)
    for sl, qt in enumerate(qts):
        i = sl // 2
        s = 8 * i * 128 + np.arange(1024)[None, :]
        t = qt * 128 + np.arange(128)[:, None]
        cm[sl] = np.where(s <= t, 0.0, -1e30)
    return dict(dqT=dqT, dgT=dgT, dkT=dkT, dv=dvm, iqT=iqT, ikT2=ikT2, iw=iwm, cmask=cm,
                ident=np.eye(128, dtype=np.float32).astype(BF)), qts


def run_layer(xT_b, inp, l):
    S = 8192
    nc1 = _get_nc("p1", lambda: build_p1(9, None, 4))
    rA = _run(nc1, p1_maps(xT_b, inp, l, ntg=4), "P1")
    tri_bf = np.triu(np.ones((128, 128), np.float32)).astype(BF)
    mapsB = []
    for core in range(8):
        b, g = core // 4, core % 4
        r = rA[core]
        small = _f32(rA[b * 4]["o_small"])
        q = np.asarray(r["o_fq"]).reshape(S, 2, 128)
        k = np.asarray(r["o_fk"]).reshape(S, 2, 128)
        mapsB.append(dict(
            qT=np.ascontiguousarray(q.transpose(1, 2, 0)), kT=np.ascontiguousarray(k.transpose(1, 2, 0)),
            v=fox_v_layout(np.asarray(r["o_fv"]), 2), logf=np.ascontiguousarray(small[:, 80 + 2 * g:82 + 2 * g].T),
            fgT=np.ascontiguousarray(np.asarray(r["o_fg"]).T), tri=tri_bf))
    ncB = _get_nc("fox", lambda: build_fox(2, 16))
    rB = _run(ncB, mapsB, "FoX")
    per_b = []
    for b in range(2):
        rs = [rA[b * 4 + g] for g in range(4)]
        small = _f32(rs[0]["o_small"])
        d = dict(
            dq=np.concatenate([_f32(r["o_dq"]).reshape(S, 2, 128) for r in rs], 1),
            dk=np.stack([_f32(rs[0]["o_dkv"])[:, 0:128], _f32(rs[1]["o_dkv"])[:, 0:128]], 1),
            dv=np.stack([_f32(rs[0]["o_dkv"])[:, 128:256], _f32(rs[1]["o_dkv"])[:, 128:256]], 1),
            iq=np.concatenate([_f32(r["o_iq"]).reshape(S, 4, 64) for r in rs], 1),
            dg=np.concatenate([_f32(r["o_dg"]).reshape(S, 2, 128) for r in rs], 1),
            sz=np.concatenate([_f32(r["o_sz"]) for r in rs], 1),
            xbc=np.concatenate([_f32(r["o_xbc"]) for r in rs], 1),
            mg=np.concatenate([np.asarray(r["o_mg"]) for r in rs], 1),
            ik=small[:, 0:64], iw=small[:, 64:80], dt=small[:, 88:120], gate=_f32(rs[0]["o_gate"]))
        per_b.append(d)
    mapsC = []
    for core in range(8):
        b, g = core // 4, core % 4
        d = per_b[b]
        ch = np.concatenate([np.arange(512 * g, 512 * g + 512), 2048 + np.arange(128 * g, 128 * g + 128),
                             2560 + np.arange(128 * g, 128 * g + 128)])
        mapsC.append(ssd_maps(d["xbc"][:, ch], inp["conv_w"][l][:, ch], inp["conv_b"][l][ch], d["dt"][:, 8 * g:8 * g + 8],
                              inp["a_log"][l][8 * g:8 * g + 8], inp["d_skip"][l][8 * g:8 * g + 8],
                              inp["ssd_norm"][l][512 * g:512 * g + 512], d["sz"][:, 512 * g:512 * g + 512]))
    ncC = _get_nc("ssd", lambda: build_ssd(16))
    rC = _run(ncC, mapsC, "SSD")
    mapsD, qtsD = [], []
    for core in range(8):
        b, j = core // 4, core % 4
        d = per_b[b]
        m, qts = dsa_maps(j, 8, d["dq"], d["dk"], d["dv"], d["iq"], d["ik"], d["iw"], d["dg"])
        mapsD.append(m)
        qtsD.append(qts)
    ncD = _get_nc("dsa", lambda: build_dsa(8))
    rD = _run(ncD, mapsD, "DSA")
    yT_b = []
    for b in range(2):
        yT = np.zeros((4096, S), dtype=BF)
        for g in range(4):
            yT[256 * g:256 * g + 256] = np.asarray(rB[b * 4 + g]["yT"])
            ys = np.asarray(rC[b * 4 + g]["y"]).transpose(1, 0, 2).reshape(S, 512)
            yT[2048 + 512 * g:2048 + 512 * g + 512] = ys.T
            yd = np.asarray(rD[b * 4 + g]["yT"])
            for sl, qt in enumerate(qtsD[b * 4 + g]):
                blk = yd[sl].reshape(128, 2, 4, 128).transpose(1, 2, 0, 3).reshape(1024, 128)
                yT[1024:2048, qt * 128:(qt + 1) * 128] = blk
        yT_b.append(yT)
    w_o4, w_out4 = p3_weights(inp, l)
    mapsE = []
    for core in range(8):
        b, q = core // 4, core % 4
        tsl = slice(q * 2048, (q + 1) * 2048)
        mapsE.append(p3_maps_one(np.ascontiguousarray(yT_b[b][:, tsl]), np.ascontiguousarray(per_b[b]["mg"][tsl].T),
                                 np.ascontiguousarray(xT_b[b][:, tsl]), w_o4, w_out4, per_b[b]["gate"]))
    ncE = _get_nc("p3", lambda: build_p3(4))
    rE = _run(ncE, mapsE, "P3")
    new = []
    for b in range(2):
        new.append(np.concatenate([p3_unpack(np.asarray(rE[b * 4 + q]["x1T"])) for q in range(4)], 1))
    return new


def kernel(**inputs):
    inp = {k: np.asarray(v) for k, v in inputs.items()}
    xT_b = [np.ascontiguousarray(inp["x"][b].T).astype(np.float32) for b in range(2)]
    for l in range(2):
        xT_b = run_layer(xT_b, inp, l)
    out = np.stack([np.ascontiguousarray(xT_b[b].T) for b in range(2)]).astype(np.float32)
    return out
```

```python
import math
import numpy as np
import ml_dtypes
from contextlib import ExitStack
import concourse.bass as bass
import concourse.mybir as mybir
from concourse.bass_utils import run_bass_kernel_spmd

BF = ml_dtypes.bfloat16


F32 = mybir.dt.float32
BF16 = mybir.dt.bfloat16
I32 = mybir.dt.int32
AF = mybir.ActivationFunctionType
ALU = mybir.AluOpType
AX = mybir.AxisListType

ENGS = ("pe", "act", "dve", "pool", "sp")
EPOCH = 30000


class Prog:
    def __init__(self, nc):
        self.nc = nc
        self.es = ExitStack()
        self.ops = {e: [] for e in ENGS}
        self.cnt = {e: 0 for e in ENGS}
        self.sems = {}
        self.seen = {e: {} for e in ENGS}
        self.bufs = {}
        self.dma_tot = {}
        self.nsem = 0

    def sb(self, name, shape, dt):
        return self.es.enter_context(self.nc.sbuf_tensor(name, list(shape), dt))

    def ps(self, name, shape, dt=F32):
        return self.es.enter_context(self.nc.psum_tensor(name, list(shape), dt))

    def sem(self, key):
        if key not in self.sems:
            self.nsem += 1
            self.sems[key] = self.es.enter_context(self.nc.semaphore("s%d" % self.nsem))
        return self.sems[key]

    def _need(self, waits, tok, eng):
        if tok is None:
            return
        k, v = tok
        if eng == "pe" and k[0] == "pe":
            return
        if self.seen[eng].get(k, 0) >= v:
            return
        if waits.get(k, 0) < v:
            waits[k] = v

    def _deps(self, eng, reads, writes):
        waits = {}
        for key in reads:
            st = self.bufs.get(key)
            if st is not None:
                self._need(waits, st[0], eng)
        for key in writes:
            st = self.bufs.get(key)
            if st is not None:
                self._need(waits, st[0], eng)
                for k, v in st[1].items():
                    self._need(waits, (k, v), eng)
        for k, v in waits.items():
            self.seen[eng][k] = v
        return waits

    def _record(self, tok, reads, writes):
        for key in reads:
            st = self.bufs.setdefault(key, [None, {}])
            if st[1].get(tok[0], 0) < tok[1]:
                st[1][tok[0]] = tok[1]
        for key in writes:
            self.bufs[key] = [tok, {}]

    def _tok(self, eng, n):
        ep = (n - 1) // EPOCH
        return ((eng, ep), n - ep * EPOCH)

    def op(self, eng, fn, reads=(), writes=(), track=True):
        waits = self._deps(eng, reads, writes)
        nxt = self.cnt[eng] + 1
        tok = self._tok(eng, nxt)
        if track:
            self.cnt[eng] = nxt
            inc = (tok[0], 1)
        else:
            inc = None
        self._record(tok, reads, writes)
        self.ops[eng].append((waits, fn, inc))

    def dma(self, eng, out, in_, semkey, reads=(), writes=()):
        waits = self._deps(eng, reads, writes)
        k = ("dma", semkey)
        self.dma_tot[k] = self.dma_tot.get(k, 0) + 16
        tok = (k, self.dma_tot[k])
        self._record(tok, reads, writes)
        self.ops[eng].append((waits, lambda e: e.dma_start(out=out, in_=in_), (k, 16)))

    def _emit_eng(self, name, e):
        for waits, fn, inc in self.ops[name]:
            for k, v in waits.items():
                e.wait_ge(self.sem(k), v)
            ins = fn(e)
            if inc is not None:
                ins.then_inc(self.sem(inc[0]), inc[1])

    def finish(self):
        waits = {}
        for k, v in self.dma_tot.items():
            waits[k] = v
        for e in ENGS:
            if e == "sp" or self.cnt[e] == 0:
                continue
            k, v = self._tok(e, self.cnt[e])
            waits[k] = v
        self.ops["sp"].append((waits, None, None))
        for e in ENGS:
            for waits_, fn, inc in self.ops[e]:
                for k in waits_:
                    self.sem(k)
                if inc is not None:
                    self.sem(inc[0])
        nc = self.nc
        with nc.Block() as block:
            @block.tensor
            def _(e):
                self._emit_eng("pe", e)

            @block.scalar
            def _(e):
                self._emit_eng("act", e)

            @block.vector
            def _(e):
                self._emit_eng("dve", e)

            @block.gpsimd
            def _(e):
                self._emit_eng("pool", e)

            @block.sync
            def _(e):
                for waits, fn, inc in self.ops["sp"]:
                    for k, v in waits.items():
                        e.wait_ge(self.sem(k), v)
                    if fn is not None:
                        ins = fn(e)
                        if inc is not None:
                            ins.then_inc(self.sem(inc[0]), inc[1])
        self.es.close()


D = 2048
T1 = 2048
KC = 16
NIN = 19064
NCOL = 4984
EPS = 1e-6
FAM = dict(fq=(0, 1024), fk=(1024, 1024), fv=(2048, 1024), ff=(3072, 8), fg=(3080, 1024),
           dq=(4104, 1024), dk=(5128, 256), dv=(5384, 256), iq=(5640, 1024), ik=(6664, 64),
           iw=(6728, 16), dg=(6744, 1024), sz=(7768, 2048), sxbc=(9816, 3072), sdt=(12888, 32),
           mg=(12920, 6144))
PERM_ORDER = ["fq", "fk", "fv", "fg", "dq", "dk", "dv", "iq", "dg", "sz", "sxbc", "mg", "ik", "iw", "ff", "sdt"]


def perm_cols():
    idx = []
    for f in PERM_ORDER:
        s, n = FAM[f]
        idx.extend(range(s, s + n))
    return np.array(idx, dtype=np.int64)


RP = dict(fqn=(0, 128), fkn=(128, 128), dqn=(256, 128), dkn=(384, 128), bff=(512, 8), dtb=(520, 32),
          if16=(552, 16), if8=(568, 8), if16b=(576, 16), if8b=(592, 8))
NRP = 600


def build_p1(stage=9, nchunks=None, NTG=4):
    nc = bass.Bass("TRN2", target_bir_lowering=False)
    dt = nc.dram_tensor
    TT = NTG * T1
    xT = dt("xT", [D, TT], F32, kind="ExternalInput").ap()
    cvec = dt("cvec", [128, KC], F32, kind="ExternalInput").ap()
    w_ada = dt("w_ada", [D, 3 * D], F32, kind="ExternalInput").ap()
    b_ada = dt("b_ada", [128, 48], F32, kind="ExternalInput").ap()
    norm_w = dt("norm_w", [128, KC], F32, kind="ExternalInput").ap()
    w_in = dt("w_in", [D, NCOL], F32, kind="ExternalInput").ap()
    rowp = dt("rowp", [1, NRP], F32, kind="ExternalInput").ap()
    b_merge = dt("b_merge", [1, 1536], F32, kind="ExternalInput").ap()
    pos = dt("pos", [128, 16 * NTG], I32, kind="ExternalInput").ap()

    o_fq = dt("o_fq", [TT, 256], BF16, kind="ExternalOutput").ap()
    o_fk = dt("o_fk", [TT, 256], BF16, kind="ExternalOutput").ap()
    o_fv = dt("o_fv", [TT, 256], BF16, kind="ExternalOutput").ap()
    o_fg = dt("o_fg", [TT, 256], BF16, kind="ExternalOutput").ap()
    o_dq = dt("o_dq", [TT, 256], BF16, kind="ExternalOutput").ap()
    o_dkv = dt("o_dkv", [TT, 256], BF16, kind="ExternalOutput").ap()
    o_iq = dt("o_iq", [TT, 256], BF16, kind="ExternalOutput").ap()
    o_dg = dt("o_dg", [TT, 256], BF16, kind="ExternalOutput").ap()
    o_sz = dt("o_sz", [TT, 512], BF16, kind="ExternalOutput").ap()
    o_xbc = dt("o_xbc", [TT, 768], BF16, kind="ExternalOutput").ap()
    o_mg = dt("o_mg", [TT, 1536], BF16, kind="ExternalOutput").ap()
    o_small = dt("o_small", [TT, 128], F32, kind="ExternalOutput").ap()
    o_gate = dt("o_gate", [128, KC], F32, kind="ExternalOutput").ap()

    p = Prog(nc)
    uT = p.sb("uT", [128, KC, T1], BF16)
    wst = [p.sb("wst%d" % i, [128, 4096], F32) for i in range(2)]
    wbf = [p.sb("wbf%d" % i, [128, KC, 512], BF16) for i in range(2)]
    stg = [p.sb("stg%d" % i, [128, 16, 512], BF16) for i in range(2)]
    rows = p.sb("rows", [128, NRP], F32)
    bmg = [p.sb("bmg%d" % i, [128, 512], F32) for i in range(2)]
    cs_in = wst[0][:, 0:768].rearrange("p (t c) -> p t c", t=16)
    cs_kf = wst[0][:, 768:1536].rearrange("p (t c) -> p t c", t=16)
    cs_r = wst[0][:, 1536:2304].rearrange("p (t c) -> p t c", t=16)
    cs_ki = wst[1][:, 0:768].bitcast(I32).rearrange("p (t c) -> p t c", t=16)
    cs = p.sb("cs", [128, 16, 48], F32)
    posi = p.sb("posi", [128, 16 * NTG], I32)
    posf = p.sb("posf", [128, 16 * NTG], F32)
    cv = p.sb("cv", [128, KC], F32)
    scv = p.sb("scv", [128, KC], F32)
    sig = p.sb("sig", [128, KC], F32)
    bad = p.sb("bad", [128, 48], F32)
    nw = p.sb("nw", [128, KC], F32)
    mod = p.sb("mod", [128, 48], F32)
    gvec = p.sb("gvec", [128, KC], F32)
    ones = p.sb("ones", [128, 128], BF16)
    sq = [p.sb("sq%d" % i, [128, 512], BF16) for i in range(2)]
    rstd = p.sb("rstd", [128, 512], F32)
    tmpu = [p.sb("tmpu%d" % i, [128, 512], F32) for i in range(2)]
    sqs = p.sb("sqs", [128, 512], F32)
    st4 = [p.sb("st4_%d" % i, [128, 8], F32) for i in range(2)]
    rt = [p.sb("rt%d" % i, [128, 6, 128], F32) for i in range(2)]
    nrm = [p.sb("nrm%d" % i, [128, 512], F32) for i in range(2)]
    smallo = p.sb("smallo", [128, 16, 128], F32)
    sm_t = p.sb("sm_t", [128, 64], F32)
    PS = [p.ps("ps%d" % i, [128, 512]) for i in range(8)]

    p.op("pool", lambda e: e.memset(ones[:], 1.0), writes=["ones"])
    p.dma("sp", rows[:], rowp.partition_broadcast(128), "rows", writes=["rows"])
    p.dma("sp", posi[:], pos, "posi", writes=["posi"])
    p.dma("sp", cv[:], cvec, "cv", writes=["cv"])
    p.dma("sp", bad[:], b_ada, "bad", writes=["bad"])
    p.dma("sp", nw[:], norm_w, "nw", writes=["nw"])
    p.op("pool", lambda e: e.memset(smallo[:], 0.0), writes=["smallo"])

    p.op("dve", lambda e: e.tensor_copy(out=posf[:], in_=posi[:]), reads=["posi"], writes=["posf"])

    def rope_tables(tg):
        for tt in range(16):
            p.op("dve", lambda e, tt=tt: e.tensor_scalar(out=cs_in[:, tt, :], in0=rows[:, 552:600],
                                                          scalar1=posf[:, tg * 16 + tt:tg * 16 + tt + 1], scalar2=None, op0=ALU.mult),
                 reads=["rows", "posf"], writes=["wst0"])
        p.op("dve", lambda e: e.tensor_scalar(out=cs_in[:, :, 24:48], in0=cs_in[:, :, 24:48], scalar1=math.pi / 2,
                                              scalar2=None, op0=ALU.add), reads=["wst0"], writes=["wst0"])
        p.op("dve", lambda e: e.tensor_scalar(out=cs_ki, in0=cs_in, scalar1=1.0 / (2 * math.pi), scalar2=None,
                                              op0=ALU.mult), reads=["wst0"], writes=["wst1"])
        p.op("dve", lambda e: e.tensor_copy(out=cs_kf, in_=cs_ki), reads=["wst1"], writes=["wst0"])
        p.op("dve", lambda e: e.scalar_tensor_tensor(out=cs_r, in0=cs_kf, scalar=-2 * math.pi, in1=cs_in,
                                                     op0=ALU.mult, op1=ALU.add), reads=["wst0", "wst0"], writes=["wst0"])
        p.op("dve", lambda e: e.tensor_scalar(out=cs_kf, in0=cs_r, scalar1=math.pi, scalar2=2 * math.pi,
                                              op0=ALU.is_gt, op1=ALU.mult), reads=["wst0"], writes=["wst0"])
        p.op("dve", lambda e: e.tensor_tensor(out=cs_r, in0=cs_r, in1=cs_kf, op=ALU.subtract),
             reads=["wst0", "wst0"], writes=["wst0"])
        p.op("dve", lambda e: e.tensor_scalar(out=cs_kf, in0=cs_r, scalar1=-math.pi, scalar2=2 * math.pi,
                                              op0=ALU.is_lt, op1=ALU.mult), reads=["wst0"], writes=["wst0"])
        p.op("dve", lambda e: e.tensor_tensor(out=cs_r, in0=cs_r, in1=cs_kf, op=ALU.add),
             reads=["wst0", "wst0"], writes=["wst0"])
        p.op("act", lambda e: e.activation(out=cs[:], in_=cs_r, func=AF.Sin), reads=["wst0"], writes=["cs"])


    p.op("act", lambda e: e.activation(out=scv[:], in_=cv[:], func=AF.Silu), reads=["cv"], writes=["scv"])
    w_ada_v = w_ada.rearrange("(kc p) n -> p kc n", p=128)
    mps = PS[0]
    for blk in range(24):
        buf = wst[blk % 2]
        key = "wst%d" % (blk % 2)
        bv = buf[:].rearrange("p (k n) -> p k n", k=16)
        p.dma("sp", bv, w_ada_v[:, :, blk * 256:(blk + 1) * 256], key, writes=[key])
        for jj in range(2):
            j = blk * 2 + jj
            for kc in range(KC):
                p.op("pe", lambda e, bv=bv, jj=jj, j=j, kc=kc: e.matmul(
                    mps[:, j:j + 1], lhsT=bv[:, kc, jj * 128:(jj + 1) * 128], rhs=scv[:, kc:kc + 1],
                    start=(kc == 0), stop=(kc == KC - 1)),
                     reads=[key, "scv"], writes=["ps0"], track=(kc == KC - 1))
    p.op("dve", lambda e: e.tensor_tensor(out=mod[:], in0=mps[:, 0:48], in1=bad[:], op=ALU.add),
         reads=["ps0", "bad"], writes=["mod"])
    p.op("dve", lambda e: e.scalar_tensor_tensor(out=gvec[:], in0=mod[:, 16:32], scalar=1.0, in1=nw[:],
                                                 op0=ALU.add, op1=ALU.mult), reads=["mod", "nw"], writes=["gvec"])
    p.dma("pool", o_gate, mod[:, 32:48], "o_gate", reads=["mod"])

    xT_v = xT.rearrange("(kc p) t -> p kc t", p=128)
    w_in_v = w_in.rearrange("(kc p) n -> p kc n", p=128)
    chunks = []
    c0 = 0

    def add(n, kind, oap, oc):
        nonlocal c0
        chunks.append((c0, n, kind, oap, oc))
        c0 += n
    add(256, "fq", o_fq, 0)
    add(256, "fk", o_fk, 0)
    add(256, "cast", o_fv, 0)
    add(256, "silu", o_fg, 0)
    add(256, "dq", o_dq, 0)
    add(256, "dkv", o_dkv, 0)
    add(256, "iq", o_iq, 0)
    add(256, "silu", o_dg, 0)
    add(512, "silu", o_sz, 0)
    add(512, "cast", o_xbc, 0)
    add(256, "cast", o_xbc, 512)
    for i in range(3): add(512, "mg", o_mg, i * 512)
    add(120, "small", o_small, 0)
    assert c0 == NCOL
    if nchunks is not None:
        chunks = chunks[:nchunks]
    state = dict(psi=3, st4i=0, rti=0, nrmi=0, ld=0)

    def compute_u(tg):
        for g in range(4):
            tsl = slice(g * 512, (g + 1) * 512)
            gsl = slice(tg * T1 + g * 512, tg * T1 + (g + 1) * 512)
            for hh in range(2):
                p.dma("sp", wst[hh][:].rearrange("p (k t) -> p k t", k=8), xT_v[:, hh * 8:(hh + 1) * 8, gsl],
                      "wst%d" % hh, writes=["wst%d" % hh])
            ssp = PS[1 + (g % 2)]
            sskey = "ps%d" % (1 + (g % 2))
            for kc in range(KC):
                xv = wst[kc // 8][:, (kc % 8) * 512:(kc % 8 + 1) * 512]
                xkey = "wst%d" % (kc // 8)
                s_ = sq[kc % 2]
                skey = "sq%d" % (kc % 2)
                p.op("act", lambda e, s_=s_, xv=xv: e.activation(out=s_[:], in_=xv, func=AF.Square),
                     reads=[xkey], writes=[skey])
                p.op("pe", lambda e, s_=s_, kc=kc, ssp=ssp: e.matmul(ssp[:], lhsT=ones[:], rhs=s_[:], start=(kc == 0),
                                                                    stop=(kc == KC - 1)),
                     reads=[skey, "ones"], writes=[sskey])
            p.op("dve", lambda e, ssp=ssp: e.tensor_scalar(out=rstd[:], in0=ssp[:], scalar1=1.0 / D, scalar2=EPS,
                                                           op0=ALU.mult, op1=ALU.add), reads=[sskey], writes=["rstd"])
            p.op("act", lambda e: e.activation(out=rstd[:], in_=rstd[:], func=AF.Ln), reads=["rstd"], writes=["rstd"])
            p.op("act", lambda e: e.activation(out=rstd[:], in_=rstd[:], func=AF.Exp, scale=-0.5), reads=["rstd"], writes=["rstd"])
            for kc in range(KC):
                xv = wst[kc // 8][:, (kc % 8) * 512:(kc % 8 + 1) * 512]
                xkey = "wst%d" % (kc // 8)
                tm = tmpu[kc % 2]
                tkey = "tmpu%d" % (kc % 2)
                p.op("dve", lambda e, tm=tm, xv=xv, kc=kc: e.scalar_tensor_tensor(
                    out=tm[:], in0=xv, scalar=gvec[:, kc:kc + 1], in1=rstd[:], op0=ALU.mult, op1=ALU.mult),
                     reads=[xkey, "gvec", "rstd"], writes=[tkey])
                p.op("act", lambda e, tm=tm, kc=kc, tsl=tsl: e.activation(
                    out=uT[:, kc, tsl], in_=tm[:], func=AF.Identity, bias=mod[:, kc:kc + 1], scale=1.0),
                     reads=[tkey, "mod"], writes=["uT"])

    def load_chunk(ci):
        col0, n, kind, oap, oc = chunks[ci]
        li = state["ld"]
        state["ld"] += 1
        wb = wbf[li % 2]
        wkey = "wbf%d" % (li % 2)
        for hh in range(2):
            skey = "wst%d" % hh
            sv = wst[hh][:, 0:8 * n].rearrange("p (k n) -> p k n", k=8)
            p.dma("sp", sv, w_in_v[:, hh * 8:(hh + 1) * 8, col0:col0 + n], skey, writes=[skey])
            eng = "pool" if hh == 0 else "dve"
            p.op(eng, lambda e, wb=wb, sv=sv, hh=hh, n=n: e.tensor_copy(out=wb[:, hh * 8:(hh + 1) * 8, 0:n], in_=sv),
                 reads=[skey], writes=[wkey + "h%d" % hh])
        if kind == "mg":
            bm = bmg[li % 2]
            bkey = "bmg%d" % (li % 2)
            p.dma("sp", bm[:], b_merge[:, oc:oc + 512].partition_broadcast(128), bkey, writes=[bkey])
        return li

    def rms_heads(ps, pkey, nh, hd, wcol, scale, dst, dkey):
        s4 = st4[state["st4i"] % 2]
        s4key = "st4_%d" % (state["st4i"] % 2)
        state["st4i"] += 1
        p.op("act", lambda e: e.activation(out=sqs[:, 0:nh * hd], in_=ps[:, 0:nh * hd], func=AF.Square),
             reads=[pkey], writes=["sqs"])
        p.op("dve", lambda e: e.tensor_reduce(out=s4[:, 0:nh], in_=sqs[:, 0:nh * hd].rearrange("p (h d) -> p h d", h=nh),
                                              axis=AX.X, op=ALU.add), reads=["sqs"], writes=[s4key])
        p.op("dve", lambda e: e.tensor_scalar(out=s4[:, 0:nh], in0=s4[:, 0:nh], scalar1=1.0 / hd, scalar2=EPS,
                                              op0=ALU.mult, op1=ALU.add), reads=[s4key], writes=[s4key])
        p.op("act", lambda e: e.activation(out=s4[:, 0:nh], in_=s4[:, 0:nh], func=AF.Ln), reads=[s4key], writes=[s4key])
        p.op("act", lambda e: e.activation(out=s4[:, 0:nh], in_=s4[:, 0:nh], func=AF.Exp, scale=-0.5,
                                           bias=math.log(scale)), reads=[s4key], writes=[s4key])
        for h in range(nh):
            p.op("dve", lambda e, h=h: e.scalar_tensor_tensor(
                out=dst[:, h * hd:(h + 1) * hd], in0=ps[:, h * hd:(h + 1) * hd], scalar=s4[:, h:h + 1],
                in1=rows[:, wcol:wcol + hd], op0=ALU.mult, op1=ALU.mult),
                 reads=[pkey, s4key, "rows"], writes=[dkey])

    def rope(src, skey, nh, hd, half, tt, dst, dkey, sin_off, cos_off):
        r = rt[state["rti"] % 2]
        rkey = "rt%d" % (state["rti"] % 2)
        state["rti"] += 1
        sv = src.rearrange("p (h d) -> p h d", h=nh)
        dv = dst.rearrange("p (h d) -> p h d", h=nh)
        x1 = sv[:, :, 0:half]
        x2 = sv[:, :, half:2 * half]
        sn = cs[:, tt, sin_off:sin_off + half].unsqueeze(1).broadcast_to([128, nh, half])
        cn = cs[:, tt, cos_off:cos_off + half].unsqueeze(1).broadcast_to([128, nh, half])
        W = nh * half

        def rv(i):
            return r[:, i, 0:W].rearrange("p (h d) -> p h d", h=nh)
        p.op("act", lambda e: e.activation(out=dst, in_=src, func=AF.Copy), reads=[skey], writes=[dkey])
        p.op("dve", lambda e: e.tensor_tensor(out=rv(0), in0=x1, in1=cn, op=ALU.mult), reads=[skey, "cs"], writes=[rkey])
        p.op("dve", lambda e: e.tensor_tensor(out=rv(1), in0=x2, in1=sn, op=ALU.mult), reads=[skey, "cs"], writes=[rkey])
        p.op("dve", lambda e: e.tensor_tensor(out=rv(2), in0=x2, in1=cn, op=ALU.mult), reads=[skey, "cs"], writes=[rkey])
        p.op("dve", lambda e: e.tensor_tensor(out=rv(3), in0=x1, in1=sn, op=ALU.mult), reads=[skey, "cs"], writes=[rkey])
        p.op("dve", lambda e: e.tensor_tensor(out=dv[:, :, 0:half], in0=rv(0), in1=rv(1), op=ALU.subtract),
             reads=[rkey, dkey], writes=[dkey])
        p.op("dve", lambda e: e.tensor_tensor(out=dv[:, :, half:2 * half], in0=rv(2), in1=rv(3), op=ALU.add),
             reads=[rkey, dkey], writes=[dkey])

    for tg in range(NTG):
        rope_tables(tg)
        if stage < 1:
            break
        compute_u(tg)
        if stage < 2:
            continue
        nxt = load_chunk(0)
        for ci in range(len(chunks)):
            col0, n, kind, oap, oc = chunks[ci]
            li = nxt
            if ci + 1 < len(chunks):
                nxt = load_chunk(ci + 1)
            wb = wbf[li % 2]
            wkey = "wbf%d" % (li % 2)
            sg = stg[li % 2]
            sgkey = "stg%d" % (li % 2)
            for tt in range(16):
                ps = PS[3 + state["psi"] % 5]
                pkey = "ps%d" % (3 + state["psi"] % 5)
                state["psi"] += 1
                for kc in range(KC):
                    p.op("pe", lambda e, ps=ps, kc=kc, tt=tt, wb=wb, n=n: e.matmul(
                        ps[:, 0:n], lhsT=uT[:, kc, tt * 128:(tt + 1) * 128], rhs=wb[:, kc, 0:n],
                        start=(kc == 0), stop=(kc == KC - 1)),
                         reads=["uT", wkey + "h%d" % (kc // 8)], writes=[pkey], track=(kc == KC - 1))
                dst = sg[:, tt, 0:n] if kind != "small" else None
                if kind == "cast":
                    p.op("act", lambda e, dst=dst, ps=ps, n=n: e.activation(out=dst, in_=ps[:, 0:n], func=AF.Copy),
                         reads=[pkey], writes=[sgkey])
                elif kind == "silu":
                    p.op("act", lambda e, dst=dst, ps=ps, n=n: e.activation(out=dst, in_=ps[:, 0:n], func=AF.Silu),
                         reads=[pkey], writes=[sgkey])
                elif kind == "mg":
                    bm = bmg[li % 2]
                    bkey = "bmg%d" % (li % 2)
                    tm = tmpu[tt % 2]
                    tkey = "tmpu%d" % (tt % 2)
                    p.op("dve", lambda e, tm=tm, ps=ps, bm=bm: e.tensor_tensor(out=tm[:], in0=ps[:], in1=bm[:], op=ALU.add),
                         reads=[pkey, bkey], writes=[tkey])
                    p.op("act", lambda e, dst=dst, tm=tm: e.activation(out=dst, in_=tm[:], func=AF.Sigmoid),
                         reads=[tkey], writes=[sgkey])
                elif kind == "fq":
                    rms_heads(ps, pkey, 2, 128, 0, 128 ** -0.5, dst, sgkey)
                elif kind == "fk":
                    rms_heads(ps, pkey, 2, 128, 128, 1.0, dst, sgkey)
                elif kind == "dq":
                    nm = nrm[state["nrmi"] % 2]
                    nkey = "nrm%d" % (state["nrmi"] % 2)
                    state["nrmi"] += 1
                    rms_heads(ps, pkey, 2, 128, 256, 128 ** -0.5, nm[:, 0:256], nkey)
                    rope(nm[:, 0:256], nkey, 2, 128, 16, tt, dst, sgkey, 0, 24)
                elif kind == "dkv":
                    nm = nrm[state["nrmi"] % 2]
                    nkey = "nrm%d" % (state["nrmi"] % 2)
                    state["nrmi"] += 1
                    rms_heads(ps, pkey, 1, 128, 384, 1.0, nm[:, 0:128], nkey)
                    rope(nm[:, 0:128], nkey, 1, 128, 16, tt, dst[:, 0:128], sgkey, 0, 24)
                    p.op("act", lambda e, dst=dst, ps=ps: e.activation(out=dst[:, 128:256], in_=ps[:, 128:256], func=AF.Copy),
                         reads=[pkey], writes=[sgkey])
                elif kind == "iq":
                    nm = nrm[state["nrmi"] % 2]
                    nkey = "nrm%d" % (state["nrmi"] % 2)
                    state["nrmi"] += 1
                    p.op("act", lambda e, nm=nm, ps=ps: e.activation(out=nm[:, 0:256], in_=ps[:, 0:256], func=AF.Copy),
                         reads=[pkey], writes=[nkey])
                    rope(nm[:, 0:256], nkey, 4, 64, 8, tt, dst, sgkey, 16, 40)
                elif kind == "small":
                    so = smallo[:, tt, :]
                    nm = nrm[state["nrmi"] % 2]
                    nkey = "nrm%d" % (state["nrmi"] % 2)
                    state["nrmi"] += 1
                    p.op("act", lambda e, nm=nm, ps=ps: e.activation(out=nm[:, 0:64], in_=ps[:, 0:64], func=AF.Copy),
                         reads=[pkey], writes=[nkey])
                    rope(nm[:, 0:64], nkey, 1, 64, 8, tt, so[:, 0:64], "smallo", 16, 40)
                    p.op("act", lambda e, so=so, ps=ps: e.activation(out=so[:, 64:80], in_=ps[:, 64:80], func=AF.Copy),
                         reads=[pkey], writes=["smallo"])
                    p.op("dve", lambda e, ps=ps: e.tensor_tensor(out=sm_t[:, 0:8], in0=ps[:, 80:88], in1=rows[:, 512:520], op=ALU.add),
                         reads=[pkey, "rows"], writes=["sm_t"])
                    p.op("act", lambda e: e.activation(out=sm_t[:, 8:16], in_=sm_t[:, 0:8], func=AF.Exp, scale=-1.0),
                         reads=["sm_t"], writes=["sm_t"])
                    p.op("act", lambda e: e.activation(out=sm_t[:, 16:24], in_=sm_t[:, 8:16], func=AF.Ln, bias=1.0, scale=1.0),
                         reads=["sm_t"], writes=["sm_t"])
                    p.op("dve", lambda e, so=so: e.tensor_scalar(out=so[:, 80:88], in0=sm_t[:, 16:24], scalar1=-1.0, scalar2=None, op0=ALU.mult),
                         reads=["sm_t"], writes=["smallo"])
                    p.op("dve", lambda e, ps=ps: e.tensor_tensor(out=sm_t[:, 24:56], in0=ps[:, 88:120], in1=rows[:, 520:552], op=ALU.add),
                         reads=[pkey, "rows"], writes=["sm_t"])
                    p.op("act", lambda e: e.activation(out=sm_t[:, 24:56], in_=sm_t[:, 24:56], func=AF.Exp),
                         reads=["sm_t"], writes=["sm_t"])
                    p.op("act", lambda e, so=so: e.activation(out=so[:, 88:120], in_=sm_t[:, 24:56], func=AF.Ln, bias=1.0, scale=1.0),
                         reads=["sm_t"], writes=["smallo"])
            rsl = slice(tg * T1, (tg + 1) * T1)
            if kind == "small":
                p.dma("pool", o_small[rsl, :].rearrange("(t p) c -> p t c", p=128), smallo[:], "smallo", reads=["smallo"])
            else:
                p.dma("pool", oap[rsl, :].rearrange("(t p) c -> p t c", p=128)[:, :, oc:oc + n], sg[:, :, 0:n], sgkey, reads=[sgkey])
    p.finish()
    return nc


S = 8192


def build_fox(NH=2, NQC=16, dbg=0):
    nc = bass.Bass("TRN2", target_bir_lowering=False)
    dt = nc.dram_tensor
    SQ = NQC * 512
    qT = dt("qT", [NH, 128, SQ], BF16, kind="ExternalInput").ap()
    kT = dt("kT", [NH, 128, SQ], BF16, kind="ExternalInput").ap()
    v = dt("v", [NH, 128, SQ // 128, 128], BF16, kind="ExternalInput").ap()
    logf = dt("logf", [NH, SQ], F32, kind="ExternalInput").ap()
    fgT = dt("fgT", [NH * 128, SQ], BF16, kind="ExternalInput").ap()
    tri = dt("tri", [128, 128], BF16, kind="ExternalInput").ap()
    yT = dt("yT", [NH * 128, SQ], BF16, kind="ExternalOutput").ap()

    p = Prog(nc)
    emit_fox(p, NH, NQC, qT, kT, v, logf, fgT, tri, yT, dbg)
    p.finish()
    return nc


def emit_fox(p, NH, NQC, qT, kT, v, logf, fgT, tri, yT, dbg=0):
    SQ = NQC * 512
    NKT = SQ // 128
    FC = min(2048, SQ)
    qsb = p.sb("f_q", [128, SQ], BF16)
    ksb = p.sb("f_k", [128, SQ], BF16)
    vsb = p.sb("f_v", [128, NKT, 128], BF16)
    Fp = p.sb("f_Fp", [96, SQ], BF16)
    F3 = p.sb("f_F3", [65, FC], F32)
    Ft = p.sb("f_Ft", [65, FC], BF16)
    Fr = p.sb("f_Fr", [65, FC], F32)
    lf = p.sb("f_lf", [65, FC], F32)
    one3 = p.sb("f_one3", [65, FC], F32)
    carry = p.sb("f_carry", [65, 1], F32)
    trisb = p.sb("f_tri", [128, 128], BF16)
    ones_bf = p.sb("f_ones", [128, 512], BF16)
    nones_bf = p.sb("f_nones", [128, 512], BF16)
    PT = [p.sb("f_pt%d" % i, [128, 512], BF16) for i in range(3)]
    tmpd = [p.sb("f_tmpd%d" % i, [128, 512], F32) for i in range(2)]
    rden = p.sb("f_rden", [128, 512], F32)
    ynum = p.sb("f_ynum", [128, 512], F32)
    fgs = [p.sb("f_fg%d" % i, [128, 512], BF16) for i in range(2)]
    yo = [p.sb("f_yo%d" % i, [128, 512], BF16) for i in range(2)]
    PSS = [p.ps("f_pss%d" % i, [128, 512]) for i in range(3)]
    PSN = [p.ps("f_psn%d" % i, [128, 512]) for i in range(2)]
    PSD = [p.ps("f_psd%d" % i, [128, 512]) for i in range(2)]

    p.dma("sp", trisb[:], tri, "f_tri", writes=["f_tri"])
    p.op("pool", lambda e: e.memset(ones_bf[:], 1.0), writes=["f_ones"])
    p.op("pool", lambda e: e.memset(nones_bf[:], -1.0), writes=["f_nones"])
    p.op("pool", lambda e: e.memset(one3[:], 1.0), writes=["f_one3"])
    p.op("pool", lambda e: e.memset(lf[:], 0.0), writes=["f_lf"])
    cnt = dict(s=0, pt=0, q=0)
    for h in range(NH):
        p.dma("sp", qsb[:], qT[h], "f_q", writes=["f_q"])
        p.dma("sp", ksb[:], kT[h], "f_k", writes=["f_k"])
        p.dma("sp", vsb[:], v[h], "f_v", writes=["f_v"])
        p.op("pool", lambda e: e.memset(Fp[:], 0.0), writes=["f_Fp"])
        p.op("pool", lambda e: e.memset(carry[:], 0.0), writes=["f_carry"])
        for c in range(SQ // FC if dbg != 2 else 0):
            csl = slice(c * FC, (c + 1) * FC)
            for r in (0, 32, 64):
                p.dma("sp", lf[r:r + 1, :], logf[h:h + 1, csl], "f_lf", writes=["f_lf"])
            p.op("dve", lambda e: e.tensor_tensor_scan(out=F3[:], data0=one3[:], data1=lf[:], initial=carry[:],
                                                       op0=ALU.mult, op1=ALU.add),
                 reads=["f_one3", "f_lf", "f_carry"], writes=["f_F3"])
            p.op("dve", lambda e: e.tensor_copy(out=carry[:], in_=F3[:, FC - 1:FC]), reads=["f_F3"], writes=["f_carry"])
            p.op("dve", lambda e: e.tensor_copy(out=Ft[:], in_=F3[:]), reads=["f_F3"], writes=["f_Ft"])
            p.op("dve", lambda e, csl=csl: e.tensor_copy(out=Fp[0:1, csl], in_=Ft[0:1, :]), reads=["f_Ft"], writes=["f_Fp"])
            p.op("dve", lambda e: e.tensor_tensor(out=Fr[:], in0=F3[:], in1=Ft[:], op=ALU.subtract),
                 reads=["f_F3", "f_Ft"], writes=["f_Fr"])
            p.op("dve", lambda e: e.tensor_copy(out=Ft[:], in_=Fr[:]), reads=["f_Fr"], writes=["f_Ft"])
            p.op("dve", lambda e, csl=csl: e.tensor_copy(out=Fp[32:33, csl], in_=Ft[32:33, :]), reads=["f_Ft"], writes=["f_Fp"])
            p.op("dve", lambda e: e.tensor_tensor(out=Fr[:], in0=Fr[:], in1=Ft[:], op=ALU.subtract),
                 reads=["f_Fr", "f_Ft"], writes=["f_Fr"])
            p.op("dve", lambda e, csl=csl: e.tensor_copy(out=Fp[64:65, csl], in_=Fr[64:65, :]), reads=["f_Fr"], writes=["f_Fp"])
        tiles = []
        for qc in range(NQC if dbg != 1 else 0):
            for kt in range(4 * qc + 4):
                tiles.append((qc, kt))
        info = {}

        def s_stage(i):
            qc, kt = tiles[i]
            q0 = qc * 512
            j = kt - 4 * qc
            c0 = 128 * j if j > 0 else 0
            ncol = 512 - c0
            if kt == 0:
                qi = cnt["q"]
                cnt["q"] += 1
                info[("q", qc)] = qi
                fg = fgs[qi % 2]
                p.dma("sp", fg[:], fgT[h * 128:(h + 1) * 128, q0:q0 + 512], "f_fg%d" % (qi % 2), writes=["f_fg%d" % (qi % 2)])
            pss = PSS[cnt["s"] % 3]
            skey = "f_pss%d" % (cnt["s"] % 3)
            cnt["s"] += 1
            info[i] = (pss, skey, c0, ncol, j)
            ksl = slice(kt * 128, (kt + 1) * 128)
            qsl = slice(q0 + c0, q0 + 512)
            p.op("pe", lambda e: e.matmul(pss[:, 0:ncol], lhsT=ksb[:, ksl], rhs=qsb[:, qsl], start=True, stop=False),
                 reads=["f_k", "f_q"], writes=[skey], track=False)
            p.op("pe", lambda e: e.matmul(pss[:, 0:ncol], lhsT=ones_bf[0:96, 0:128], rhs=Fp[:, qsl], start=False, stop=False),
                 reads=["f_ones", "f_Fp"], writes=[skey], track=False)
            p.op("pe", lambda e: e.matmul(pss[:, 0:ncol], lhsT=Fp[:, ksl], rhs=nones_bf[0:96, 0:ncol], start=False, stop=True),
                 reads=["f_nones", "f_Fp"], writes=[skey])

        def e_stage(i):
            qc, kt = tiles[i]
            pss, skey, c0, ncol, j = info[i]
            pt = PT[cnt["pt"] % 3]
            ptkey = "f_pt%d" % (cnt["pt"] % 3)
            cnt["pt"] += 1
            info[("pt", i)] = (pt, ptkey)
            if j >= 0:
                td = tmpd[kt % 2]
                tdkey = "f_tmpd%d" % (kt % 2)
                p.op("dve", lambda e: e.tensor_scalar(out=td[:, 0:ncol], in0=pss[:, 0:ncol], scalar1=30.0, scalar2=None, op0=ALU.min),
                     reads=[skey], writes=[tdkey])
                p.op("act", lambda e: e.activation(out=pt[:, 0:ncol], in_=td[:, 0:ncol], func=AF.Exp), reads=[tdkey], writes=[ptkey])
                p.op("dve", lambda e: e.tensor_tensor(out=pt[:, 0:128], in0=pt[:, 0:128], in1=trisb[:], op=ALU.mult),
                     reads=[ptkey, "f_tri"], writes=[ptkey])
            else:
                p.op("act", lambda e: e.activation(out=pt[:], in_=pss[:], func=AF.Exp), reads=[skey], writes=[ptkey])

        def pv_stage(i):
            qc, kt = tiles[i]
            q0 = qc * 512
            nkt = 4 * qc + 4
            pss, skey, c0, ncol, j = info[i]
            pt, ptkey = info[("pt", i)]
            qi = info[("q", qc)]
            psn, psd = PSN[qi % 2], PSD[qi % 2]
            nkey, dkey = "f_psn%d" % (qi % 2), "f_psd%d" % (qi % 2)
            p.op("pe", lambda e: e.matmul(psn[:, c0:512], lhsT=vsb[:, kt, :], rhs=pt[:, 0:ncol], start=(kt == 0), stop=(kt == nkt - 1)),
                 reads=["f_v", ptkey], writes=[nkey], track=False)
            p.op("pe", lambda e: e.matmul(psd[:, c0:512], lhsT=ones_bf[:, 0:128], rhs=pt[:, 0:ncol], start=(kt == 0), stop=(kt == nkt - 1)),
                 reads=["f_ones", ptkey], writes=[dkey])
            if kt == nkt - 1:
                fg, fgkey = fgs[qi % 2], "f_fg%d" % (qi % 2)
                yob, yokey = yo[qi % 2], "f_yo%d" % (qi % 2)
                p.op("dve", lambda e: e.reciprocal(out=rden[:], in_=psd[:]), reads=[dkey], writes=["f_rden"])
                p.op("dve", lambda e: e.tensor_tensor(out=ynum[:], in0=psn[:], in1=rden[:], op=ALU.mult),
                     reads=[nkey, "f_rden"], writes=["f_ynum"])
                p.op("pool", lambda e: e.tensor_tensor(out=yob[:], in0=ynum[:], in1=fg[:], op=ALU.mult),
                     reads=["f_ynum", fgkey], writes=[yokey])
                p.dma("pool", yT[h * 128:(h + 1) * 128, q0:q0 + 512], yob[:], yokey, reads=[yokey])

        if tiles:
            s_stage(0)
        for i in range(len(tiles)):
            if i + 1 < len(tiles):
                s_stage(i + 1)
            e_stage(i)
            pv_stage(i)


EPS = 1e-6


def build_ssd(NBLK=16):
    nc = bass.Bass("TRN2", target_bir_lowering=False)
    dt = nc.dram_tensor
    SQ = NBLK * 512
    xbcT = dt("xbcT", [128, 6, SQ], BF16, kind="ExternalInput").ap()
    convw = dt("convw", [128, 6, 4], F32, kind="ExternalInput").ap()
    convb = dt("convb", [128, 6], F32, kind="ExternalInput").ap()
    dtv = dt("dtv", [128, SQ // 128, 8], F32, kind="ExternalInput").ap()
    rowc = dt("rowc", [1, 16 + 512], F32, kind="ExternalInput").ap()
    sz = dt("sz", [128, SQ // 128, 512], BF16, kind="ExternalInput").ap()
    cst = dt("cst", [128, 3, 128], F32, kind="ExternalInput").ap()
    y = dt("y", [128, SQ // 128, 512], BF16, kind="ExternalOutput").ap()
    p = Prog(nc)
    emit_ssd(p, NBLK, xbcT, convw, convb, dtv, rowc, sz, cst, y)
    p.finish()
    return nc


def emit_ssd(p, NBLK, xbcT, convw, convb, dtv, rowc, sz, cst, y):
    SQ = NBLK * 512
    xr = [p.sb("s_xr%d" % i, [128, 6, 515], BF16) for i in range(2)]
    cv = p.sb("s_cv", [128, 6, 512], BF16)
    acc = [p.sb("s_acc%d" % i, [128, 512], F32) for i in range(2)]
    cw = p.sb("s_cw", [128, 6, 4], F32)
    cb = p.sb("s_cb", [128, 6], F32)
    dts = p.sb("s_dt", [128, SQ // 128, 8], F32)
    rows = p.sb("s_rows", [128, 528], F32)
    Arow = p.sb("s_A", [128, 8], F32)
    csts = p.sb("s_cst", [128, 3, 128], F32)
    ident = p.sb("s_ident", [128, 128], BF16)
    ones = p.sb("s_ones", [128, 128], F32)
    mnb = p.sb("s_mnb", [128, 8, 128], F32)
    szs = [p.sb("s_sz%d" % i, [128, 4, 512], BF16) for i in range(2)]
    xs = p.sb("s_xs", [128, 512], F32)
    xd = p.sb("s_xd", [128, 512], BF16)
    xdw = p.sb("s_xdw", [128, 512], BF16)
    Btok = p.sb("s_Btok", [128, 128], BF16)
    cbT = p.sb("s_cbT", [128, 128], F32)
    da = p.sb("s_da", [128, 8], F32)
    acs = p.sb("s_acs", [128, 8], F32)
    tot = p.sb("s_tot", [128, 8], F32)
    wl = p.sb("s_wl", [128, 8], F32)
    eacs = p.sb("s_eacs", [128, 8], F32)
    cd = p.sb("s_cd", [128, 8], F32)
    X = p.sb("s_X", [128, 8, 128], F32)
    dif = p.sb("s_dif", [128, 8, 128], F32)
    dec = p.sb("s_dec", [128, 8, 128], F32)
    MT = p.sb("s_MT", [128, 8, 128], BF16)
    Sst = p.sb("s_S", [128, 512], F32)
    Sbf = p.sb("s_Sbf", [128, 512], BF16)
    t1 = p.sb("s_t1", [128, 512], F32)
    t2 = p.sb("s_t2", [128, 512], F32)
    t3 = p.sb("s_t3", [128, 512], F32)
    ssq = p.sb("s_ssq", [128, 2], F32)
    junk = p.sb("s_junk", [128, 512], F32)
    yst = [p.sb("s_yst%d" % i, [128, 4, 512], BF16) for i in range(2)]
    P_xs = p.ps("s_pxs", [128, 512])
    P_b = p.ps("s_pb", [128, 512])
    P_a = p.ps("s_pa", [128, 512])
    P_d = [p.ps("s_pd%d" % i, [128, 512]) for i in range(2)]
    P_y = p.ps("s_py", [128, 512])
    P_o = p.ps("s_po", [128, 512])
    P_s = p.ps("s_psb", [128, 512])

    p.dma("sp", cw[:], convw, "s_cw", writes=["s_cw"])
    p.dma("sp", cb[:], convb, "s_cb", writes=["s_cb"])
    p.dma("sp", dts[:], dtv, "s_dt", writes=["s_dt"])
    p.dma("sp", rows[:], rowc.partition_broadcast(128), "s_rows", writes=["s_rows"])
    p.dma("sp", csts[:], cst, "s_cst", writes=["s_cst"])
    p.op("act", lambda e: e.activation(out=Arow[:], in_=rows[:, 0:8], func=AF.Exp), reads=["s_rows"], writes=["s_A"])
    p.op("dve", lambda e: e.tensor_scalar(out=Arow[:], in0=Arow[:], scalar1=-1.0, scalar2=None, op0=ALU.mult),
         reads=["s_A"], writes=["s_A"])
    p.op("dve", lambda e: e.tensor_copy(out=ident[:], in_=csts[:, 2, :]), reads=["s_cst"], writes=["s_ident"])
    p.op("pool", lambda e: e.memset(ones[:], 1.0), writes=["s_ones"])
    p.op("pool", lambda e: e.memset(Sst[:], 0.0), writes=["s_S"])
    p.op("pool", lambda e: e.memset(Sbf[:], 0.0), writes=["s_Sbf"])
    p.op("dve", lambda e: e.tensor_copy(out=mnb[:], in_=csts[:, 1, :].unsqueeze(1).broadcast_to([128, 8, 128])),
         reads=["s_cst"], writes=["s_mnb"])
    tri = csts[:, 0, :]
    xbc_v = xbcT
    sz_v = sz
    y_v = y
    Db = rows[:, 8:16].unsqueeze(2).broadcast_to([128, 8, 64])

    def v3(t):
        return t.rearrange("p (h d) -> p h d", h=8)

    for blk in range(NBLK):
        bi = blk % 2
        xk = "s_xr%d" % bi
        if blk == 0:
            p.op("pool", lambda e: e.memset(xr[0][:, :, 0:3], 0.0), writes=[xk])
            p.dma("sp", xr[0][:, :, 3:515], xbc_v[:, :, 0:512], xk, writes=[xk])
        else:
            p.dma("sp", xr[bi][:], xbc_v[:, :, blk * 512 - 3:blk * 512 + 512], xk, writes=[xk])
        szk = "s_sz%d" % bi
        p.dma("sp", szs[bi][:], sz_v[:, blk * 4:(blk + 1) * 4, :], szk, writes=[szk])
        for cc in range(6):
            a = acc[cc % 2]
            ak = "s_acc%d" % (cc % 2)
            p.op("dve", lambda e, a=a, cc=cc, bi=bi: e.tensor_scalar(out=a[:], in0=xr[bi][:, cc, 0:512], scalar1=cw[:, cc, 0:1],
                                                                  scalar2=None, op0=ALU.mult), reads=[xk, "s_cw"], writes=[ak])
            for k in range(1, 4):
                p.op("dve", lambda e, a=a, cc=cc, k=k, bi=bi: e.scalar_tensor_tensor(
                    out=a[:], in0=xr[bi][:, cc, k:k + 512], scalar=cw[:, cc, k:k + 1], in1=a[:], op0=ALU.mult, op1=ALU.add),
                     reads=[xk, "s_cw", ak], writes=[ak])
            p.op("act", lambda e, a=a, cc=cc: e.activation(out=cv[:, cc, :], in_=a[:], func=AF.Silu, bias=cb[:, cc:cc + 1], scale=1.0),
                 reads=[ak, "s_cb"], writes=["s_cv"])
        yk = "s_yst%d" % bi
        for j in range(4):
            c = blk * 4 + j
            jsl = slice(j * 128, (j + 1) * 128)
            for cc in range(4):
                p.op("pe", lambda e, cc=cc, jsl=jsl: e.matmul(P_xs[:, cc * 128:(cc + 1) * 128], lhsT=cv[:, cc, jsl], rhs=ident[:],
                                                             start=True, stop=True), reads=["s_cv", "s_ident"], writes=["s_pxs"],
                     track=(cc == 3))
            p.op("pe", lambda e, jsl=jsl: e.matmul(P_b[:, 0:128], lhsT=cv[:, 4, jsl], rhs=ident[:], start=True, stop=True),
                 reads=["s_cv", "s_ident"], writes=["s_pb"], track=False)
            p.op("pe", lambda e, jsl=jsl: e.matmul(P_b[:, 128:256], lhsT=cv[:, 4, jsl], rhs=cv[:, 5, jsl], start=True, stop=True),
                 reads=["s_cv"], writes=["s_pb"])
            p.op("act", lambda e: e.activation(out=xs[:], in_=P_xs[:], func=AF.Copy), reads=["s_pxs"], writes=["s_xs"])
            p.op("act", lambda e: e.activation(out=Btok[:], in_=P_b[:, 0:128], func=AF.Copy), reads=["s_pb"], writes=["s_Btok"])
            p.op("act", lambda e: e.activation(out=cbT[:], in_=P_b[:, 128:256], func=AF.Copy), reads=["s_pb"], writes=["s_cbT"])
            p.op("dve", lambda e, c=c: e.tensor_tensor(out=da[:], in0=dts[:, c, :], in1=Arow[:], op=ALU.mult),
                 reads=["s_dt", "s_A"], writes=["s_da"])
            p.op("pe", lambda e: e.matmul(P_a[:, 0:8], lhsT=tri, rhs=da[:], start=True, stop=True),
                 reads=["s_cst", "s_da"], writes=["s_pa"], track=False)
            p.op("pe", lambda e: e.matmul(P_a[:, 8:16], lhsT=ones[:], rhs=da[:], start=True, stop=True),
                 reads=["s_ones", "s_da"], writes=["s_pa"])
            p.op("dve", lambda e: e.tensor_copy(out=acs[:], in_=P_a[:, 0:8]), reads=["s_pa"], writes=["s_acs"])
            p.op("dve", lambda e: e.tensor_copy(out=tot[:], in_=P_a[:, 8:16]), reads=["s_pa"], writes=["s_tot"])
            p.op("dve", lambda e: e.tensor_tensor(out=wl[:], in0=tot[:], in1=acs[:], op=ALU.subtract),
                 reads=["s_tot", "s_acs"], writes=["s_wl"])
            p.op("act", lambda e: e.activation(out=wl[:], in_=wl[:], func=AF.Exp), reads=["s_wl"], writes=["s_wl"])
            p.op("act", lambda e: e.activation(out=eacs[:], in_=acs[:], func=AF.Exp), reads=["s_acs"], writes=["s_eacs"])
            p.op("act", lambda e: e.activation(out=cd[:], in_=tot[:], func=AF.Exp), reads=["s_tot"], writes=["s_cd"])
            p.op("dve", lambda e: e.tensor_tensor(out=X[:], in0=tri.unsqueeze(1).broadcast_to([128, 8, 128]),
                                                  in1=da[:].unsqueeze(2).broadcast_to([128, 8, 128]), op=ALU.mult),
                 reads=["s_cst", "s_da"], writes=["s_X"])
            for hh in range(2):
                pk = "s_pd%d" % hh
                p.op("pe", lambda e, hh=hh: e.matmul(P_d[hh][:], lhsT=ones[:], rhs=X[:, hh * 4:(hh + 1) * 4, :], start=True, stop=False),
                     reads=["s_ones", "s_X"], writes=[pk], track=False)
                p.op("pe", lambda e, hh=hh: e.matmul(P_d[hh][:], lhsT=csts[:, 2, :], rhs=mnb[:, hh * 4:(hh + 1) * 4, :], start=False, stop=True),
                     reads=["s_cst", "s_mnb"], writes=[pk])
                p.op("dve", lambda e, hh=hh: e.tensor_tensor(
                    out=dif[:, hh * 4:(hh + 1) * 4, :], in0=P_d[hh][:].rearrange("p (h l) -> p h l", h=4),
                    in1=acs[:, hh * 4:(hh + 1) * 4].unsqueeze(2).broadcast_to([128, 4, 128]), op=ALU.subtract),
                     reads=[pk, "s_acs"], writes=["s_dif"])
            p.op("act", lambda e: e.activation(out=dec[:], in_=dif[:], func=AF.Exp), reads=["s_dif"], writes=["s_dec"])
            p.op("dve", lambda e: e.tensor_tensor(out=MT[:], in0=dec[:], in1=cbT[:].unsqueeze(1).broadcast_to([128, 8, 128]), op=ALU.mult),
                 reads=["s_dec", "s_cbT"], writes=["s_MT"])
            p.op("pool", lambda e, c=c: e.tensor_tensor(out=v3(xd[:]), in0=v3(xs[:]),
                                                       in1=dts[:, c, :].unsqueeze(2).broadcast_to([128, 8, 64]), op=ALU.mult),
                 reads=["s_xs", "s_dt"], writes=["s_xd"])
            p.op("pool", lambda e: e.tensor_tensor(out=v3(xdw[:]), in0=v3(xd[:]), in1=wl[:].unsqueeze(2).broadcast_to([128, 8, 64]), op=ALU.mult),
                 reads=["s_xd", "s_wl"], writes=["s_xdw"])
            for h in range(8):
                p.op("pe", lambda e, h=h: e.matmul(P_y[:, h * 64:(h + 1) * 64], lhsT=MT[:, h, :], rhs=xd[:, h * 64:(h + 1) * 64],
                                                   start=True, stop=True), reads=["s_MT", "s_xd"], writes=["s_py"], track=(h == 7))
            p.op("pe", lambda e, jsl=jsl: e.matmul(P_o[:], lhsT=cv[:, 5, jsl], rhs=Sbf[:], start=True, stop=True),
                 reads=["s_cv", "s_Sbf"], writes=["s_po"])
            p.op("pe", lambda e: e.matmul(P_s[:], lhsT=Btok[:], rhs=xdw[:], start=True, stop=True),
                 reads=["s_Btok", "s_xdw"], writes=["s_psb"])
            p.op("dve", lambda e: e.tensor_tensor(out=v3(t1[:]), in0=v3(P_o[:]), in1=eacs[:].unsqueeze(2).broadcast_to([128, 8, 64]), op=ALU.mult),
                 reads=["s_po", "s_eacs"], writes=["s_t1"])
            p.op("dve", lambda e: e.tensor_tensor(out=t2[:], in0=P_y[:], in1=t1[:], op=ALU.add), reads=["s_py", "s_t1"], writes=["s_t2"])
            p.op("pool", lambda e: e.tensor_tensor(out=v3(t3[:]), in0=v3(xs[:]), in1=Db, op=ALU.mult), reads=["s_xs", "s_rows"], writes=["s_t3"])
            p.op("pool", lambda e: e.tensor_tensor(out=t2[:], in0=t2[:], in1=t3[:], op=ALU.add), reads=["s_t2", "s_t3"], writes=["s_t2"])
            p.op("dve", lambda e: e.tensor_tensor(out=v3(Sst[:]), in0=v3(Sst[:]), in1=cd[:].unsqueeze(2).broadcast_to([128, 8, 64]), op=ALU.mult),
                 reads=["s_S", "s_cd"], writes=["s_S"])
            p.op("dve", lambda e: e.tensor_tensor(out=Sst[:], in0=P_s[:], in1=Sst[:], op=ALU.add), reads=["s_psb", "s_S"], writes=["s_S"])
            p.op("act", lambda e: e.activation(out=Sbf[:], in_=Sst[:], func=AF.Copy), reads=["s_S"], writes=["s_Sbf"])
            p.op("dve", lambda e, j=j, bi=bi: e.tensor_tensor(out=t2[:], in0=t2[:], in1=szs[bi][:, j, :], op=ALU.mult),
                 reads=["s_t2", szk], writes=["s_t2"])
            p.op("pool", lambda e: e.memset(ssq[:], 0.0), writes=["s_ssq"])
            p.op("act", lambda e: e.activation(out=junk[:], in_=t2[:], func=AF.Square, accum_out=ssq[:, 0:1]),
                 reads=["s_t2"], writes=["s_junk", "s_ssq"])
            p.op("dve", lambda e: e.tensor_scalar(out=ssq[:, 1:2], in0=ssq[:, 0:1], scalar1=1.0 / 512, scalar2=EPS, op0=ALU.mult, op1=ALU.add),
                 reads=["s_ssq"], writes=["s_ssq"])
            p.op("act", lambda e: e.activation(out=ssq[:, 1:2], in_=ssq[:, 1:2], func=AF.Ln), reads=["s_ssq"], writes=["s_ssq"])
            p.op("act", lambda e: e.activation(out=ssq[:, 1:2], in_=ssq[:, 1:2], func=AF.Exp, scale=-0.5), reads=["s_ssq"], writes=["s_ssq"])
            p.op("dve", lambda e, j=j, bi=bi: e.scalar_tensor_tensor(out=yst[bi][:, j, :], in0=t2[:], scalar=ssq[:, 1:2], in1=rows[:, 16:528],
                                                                    op0=ALU.mult, op1=ALU.mult),
                 reads=["s_t2", "s_ssq", "s_rows"], writes=[yk])
        p.dma("pool", y_v[:, blk * 4:(blk + 1) * 4, :], yst[bi][:], yk, reads=[yk])


NITER = 15
TOPK = 256


def build_dsa(NI=8):
    nc = bass.Bass("TRN2", target_bir_lowering=False)
    dt = nc.dram_tensor
    NS = 2 * NI
    SK = NI * 1024
    dqT = dt("dqT", [NS, 128, 2, 512], BF16, kind="ExternalInput").ap()
    dgT = dt("dgT", [NS, 128, 2, 512], BF16, kind="ExternalInput").ap()
    dkT = dt("dkT", [128, 2, SK], BF16, kind="ExternalInput").ap()
    dv = dt("dv", [128, SK // 128, 256], BF16, kind="ExternalInput").ap()
    iqT = dt("iqT", [NS, 128, 8, 128], BF16, kind="ExternalInput").ap()
    ikT2 = dt("ikT2", [128, SK], BF16, kind="ExternalInput").ap()
    iw = dt("iw", [128, NS, 16], F32, kind="ExternalInput").ap()
    cmask = dt("cmask", [NS, 128, 1024], F32, kind="ExternalInput").ap()
    identd = dt("ident", [128, 128], BF16, kind="ExternalInput").ap()
    yT = dt("yT", [NS, 128, 2, 512], BF16, kind="ExternalOutput").ap()
    p = Prog(nc)
    emit_dsa(p, NI, dqT, dgT, dkT, dv, iqT, ikT2, iw, cmask, identd, yT)
    p.finish()
    return nc


def emit_dsa(p, NI, dqT, dgT, dkT, dv, iqT, ikT2, iw, cmask, identd, yT):
    NS = 2 * NI
    SK = NI * 1024
    ks = p.sb("d_k", [128, 2, SK], BF16)
    vs = p.sb("d_v", [128, SK // 128, 256], BF16)
    iks = p.sb("d_ik", [128, SK], BF16)
    iws = p.sb("d_iw", [128, NS, 16], F32)
    ident = p.sb("d_ident", [128, 128], BF16)
    ones = p.sb("d_ones", [128, 128], BF16)
    qs = [p.sb("d_q%d" % i, [128, 2, 512], BF16) for i in range(2)]
    gs = [p.sb("d_g%d" % i, [128, 2, 512], BF16) for i in range(2)]
    iqs = [p.sb("d_iq%d" % i, [128, 8, 128], BF16) for i in range(2)]
    cms = [p.sb("d_cm%d" % i, [128, 1024], F32) for i in range(2)]
    sc = p.sb("d_sc", [128, SK], F32)
    junk = p.sb("d_junk", [128, SK], BF16)
    maskq = p.sb("d_maskq", [128, SK], BF16)
    maskT = p.sb("d_maskT", [128, SK // 128, 128], BF16)
    R = [p.sb("d_R%d" % i, [128, 512], F32) for i in range(3)]
    st = p.sb("d_st", [128, 16], F32)
    PT = [p.sb("d_pt%d" % i, [128, 512], BF16) for i in range(3)]
    rden = p.sb("d_rden", [128, 512], F32)
    ynum = p.sb("d_ynum", [128, 512], F32)
    yo = [p.sb("d_yo%d" % i, [128, 2, 512], BF16) for i in range(2)]
    PI = [p.ps("d_pi%d" % i, [128, 512]) for i in range(2)]
    PTr = p.ps("d_ptr", [128, 512])
    PSS = [p.ps("d_pss%d" % i, [128, 512]) for i in range(2)]
    PSN = p.ps("d_psn", [128, 512])
    PSD = p.ps("d_psd", [128, 512])

    p.dma("sp", ks[:], dkT, "d_k", writes=["d_k"])
    p.dma("sp", vs[:], dv, "d_v", writes=["d_v"])
    p.dma("sp", iks[:], ikT2, "d_ik", writes=["d_ik"])
    p.dma("sp", iws[:], iw, "d_iw", writes=["d_iw"])
    p.dma("sp", ident[:], identd, "d_ident", writes=["d_ident"])
    p.op("pool", lambda e: e.memset(ones[:], 1.0), writes=["d_ones"])
    cnt = dict(r=0, pi=0, s=0, pt=0)

    def col(i):
        return st[:, i:i + 1]

    for sl in range(NS):
        i = sl // 2
        nk = 8 * (i + 1)
        NK = nk * 128
        b = sl % 2
        qk, gk, iqk, cmk, yok = "d_q%d" % b, "d_g%d" % b, "d_iq%d" % b, "d_cm%d" % b, "d_yo%d" % b
        p.dma("sp", qs[b][:], dqT[sl], qk, writes=[qk])
        p.dma("sp", gs[b][:], dgT[sl], gk, writes=[gk])
        p.dma("sp", iqs[b][:], iqT[sl], iqk, writes=[iqk])
        p.dma("sp", cms[b][:], cmask[sl], cmk, writes=[cmk])
        for kc in range(nk // 4):
            ksl = slice(kc * 512, (kc + 1) * 512)
            for h in range(16):
                pr, hf = h // 2, h % 2
                pi = PI[cnt["pi"] % 2]
                pik = "d_pi%d" % (cnt["pi"] % 2)
                cnt["pi"] += 1
                r = R[cnt["r"] % 3]
                rk = "d_R%d" % (cnt["r"] % 3)
                cnt["r"] += 1
                p.op("pe", lambda e, pi=pi, pr=pr, hf=hf, ksl=ksl, b=b: e.matmul(
                    pi[:], lhsT=iqs[b][hf * 64:(hf + 1) * 64, pr, :], rhs=iks[hf * 64:(hf + 1) * 64, ksl], start=True, stop=True),
                     reads=[iqk, "d_ik"], writes=[pik])
                p.op("act", lambda e, pi=pi, r=r: e.activation(out=r[:], in_=pi[:], func=AF.Relu), reads=[pik], writes=[rk])
                if h == 0:
                    p.op("dve", lambda e, r=r, ksl=ksl, sl=sl: e.tensor_scalar(out=sc[:, ksl], in0=r[:], scalar1=iws[:, sl, 0:1], scalar2=None,
                                                                            op0=ALU.mult), reads=[rk, "d_iw"], writes=["d_sc"])
                else:
                    p.op("dve", lambda e, r=r, ksl=ksl, sl=sl, h=h: e.scalar_tensor_tensor(
                        out=sc[:, ksl], in0=r[:], scalar=iws[:, sl, h:h + 1], in1=sc[:, ksl], op0=ALU.mult, op1=ALU.add),
                         reads=[rk, "d_iw", "d_sc"], writes=["d_sc"])
        p.op("dve", lambda e, NK=NK: e.tensor_reduce(out=col(8), in_=sc[:, 0:NK], axis=AX.X, op=ALU.max), reads=["d_sc"], writes=["d_st"])
        p.op("dve", lambda e, NK=NK: e.tensor_reduce(out=col(9), in_=sc[:, 0:NK], axis=AX.X, op=ALU.min), reads=["d_sc"], writes=["d_st"])
        p.op("dve", lambda e: e.tensor_scalar(out=col(0), in0=col(9), scalar1=-1.0, scalar2=None, op0=ALU.add), reads=["d_st"], writes=["d_st"])
        p.op("dve", lambda e: e.tensor_tensor(out=col(1), in0=col(8), in1=col(9), op=ALU.subtract), reads=["d_st"], writes=["d_st"])
        p.op("dve", lambda e: e.tensor_scalar(out=col(1), in0=col(1), scalar1=2.0, scalar2=None, op0=ALU.add), reads=["d_st"], writes=["d_st"])
        p.op("dve", lambda e, NK=NK, b=b: e.tensor_tensor(out=sc[:, NK - 1024:NK], in0=sc[:, NK - 1024:NK], in1=cms[b][:], op=ALU.add),
             reads=["d_sc", cmk], writes=["d_sc"])
        for it in range(NITER):
            cit = 0.5 ** (it + 1)
            p.op("dve", lambda e, cit=cit: e.tensor_scalar(out=col(4), in0=col(1), scalar1=cit, scalar2=None, op0=ALU.mult), reads=["d_st"], writes=["d_st"])
            p.op("dve", lambda e: e.tensor_tensor(out=col(2), in0=col(0), in1=col(4), op=ALU.add), reads=["d_st"], writes=["d_st"])
            p.op("dve", lambda e, NK=NK: e.tensor_scalar(out=junk[:, 0:NK], in0=sc[:, 0:NK], scalar1=col(2), scalar2=None, op0=ALU.is_ge,
                                                         op1=ALU.add, accum_out=col(3)), reads=["d_sc", "d_st"], writes=["d_junk", "d_st"])
            p.op("dve", lambda e: e.scalar_tensor_tensor(out=col(5), in0=col(3), scalar=TOPK - 0.5, in1=col(4), op0=ALU.is_gt, op1=ALU.mult),
                 reads=["d_st"], writes=["d_st"])
            p.op("dve", lambda e: e.tensor_tensor(out=col(0), in0=col(0), in1=col(5), op=ALU.add), reads=["d_st"], writes=["d_st"])
        p.op("dve", lambda e, NK=NK: e.tensor_scalar(out=maskq[:, 0:NK], in0=sc[:, 0:NK], scalar1=col(0), scalar2=None, op0=ALU.is_ge),
             reads=["d_sc", "d_st"], writes=["d_maskq"])
        for k4 in range(nk // 4):
            for jj in range(4):
                kt = k4 * 4 + jj
                p.op("pe", lambda e, kt=kt, jj=jj: e.matmul(PTr[:, jj * 128:(jj + 1) * 128], lhsT=maskq[:, kt * 128:(kt + 1) * 128], rhs=ident[:],
                                                           start=True, stop=True), reads=["d_maskq", "d_ident"], writes=["d_ptr"], track=(jj == 3))
            p.op("act", lambda e, k4=k4: e.activation(out=maskT[:, k4 * 4:(k4 + 1) * 4, :], in_=PTr[:].rearrange("p (a t) -> p a t", a=4), func=AF.Copy),
                 reads=["d_ptr"], writes=["d_maskT"])
        for gg in range(2):
            st_ = {}

            def s_stage(kt, gg=gg, st_=st_, b=b, qk=qk):
                pss = PSS[cnt["s"] % 2]
                sk = "d_pss%d" % (cnt["s"] % 2)
                cnt["s"] += 1
                st_[kt] = (pss, sk)
                p.op("pe", lambda e: e.matmul(pss[:], lhsT=ks[:, gg, kt * 128:(kt + 1) * 128], rhs=qs[b][:, gg, :], start=True, stop=True),
                     reads=["d_k", qk], writes=[sk])

            def rest(kt, gg=gg, st_=st_, b=b, nk=nk):
                pss, sk = st_[kt]
                pt = PT[cnt["pt"] % 3]
                ptk = "d_pt%d" % (cnt["pt"] % 3)
                cnt["pt"] += 1
                p.op("act", lambda e: e.activation(out=pt[:], in_=pss[:], func=AF.Exp), reads=[sk], writes=[ptk + "_%d" % a for a in range(4)])
                for a4 in range(4):
                    eng = "dve"
                    p.op(eng, lambda e, a4=a4: e.tensor_tensor(out=pt[:, a4 * 128:(a4 + 1) * 128], in0=pt[:, a4 * 128:(a4 + 1) * 128],
                                                               in1=maskT[:, kt, :], op=ALU.mult), reads=[ptk + "_%d" % a4, "d_maskT"],
                         writes=[ptk + "_%d" % a4])
                p.op("pe", lambda e: e.matmul(PSN[:], lhsT=vs[:, kt, gg * 128:(gg + 1) * 128], rhs=pt[:], start=(kt == 0), stop=(kt == nk - 1)),
                     reads=["d_v"] + [ptk + "_%d" % a for a in range(4)], writes=["d_psn"], track=False)
                p.op("pe", lambda e: e.matmul(PSD[:], lhsT=ones[:], rhs=pt[:], start=(kt == 0), stop=(kt == nk - 1)),
                     reads=["d_ones"] + [ptk + "_%d" % a for a in range(4)], writes=["d_psd"])
            s_stage(0)
            for kt in range(nk):
                if kt + 1 < nk:
                    s_stage(kt + 1)
                rest(kt)
            p.op("dve", lambda e: e.reciprocal(out=rden[:], in_=PSD[:]), reads=["d_psd"], writes=["d_rden"])
            p.op("dve", lambda e: e.tensor_tensor(out=ynum[:], in0=PSN[:], in1=rden[:], op=ALU.mult), reads=["d_psn", "d_rden"], writes=["d_ynum"])
            p.op("pool", lambda e, gg=gg, b=b: e.tensor_tensor(out=yo[b][:, gg, :], in0=ynum[:], in1=gs[b][:, gg, :], op=ALU.mult),
                 reads=["d_ynum", gk], writes=[yok])
        p.dma("pool", yT[sl], yo[b][:], yok, reads=[yok])


D = 2048
T3 = 2048


def build_p3(NG=4):
    nc = bass.Bass("TRN2", target_bir_lowering=False)
    dt = nc.dram_tensor
    TT = NG * 512
    yT = dt("yT", [NG, 128, 32, 512], BF16, kind="ExternalInput").ap()
    gT = dt("gT", [NG, 16, 128, 3, 512], BF16, kind="ExternalInput").ap()
    xT = dt("xT", [NG, 16, 128, 512], F32, kind="ExternalInput").ap()
    w_o = dt("w_o", [16, 128, 32, 128], F32, kind="ExternalInput").ap()
    w_out = dt("w_out", [16, 128, 16, 128], F32, kind="ExternalInput").ap()
    gate = dt("gate", [128, 16], F32, kind="ExternalInput").ap()
    x1T = dt("x1T", [NG, 16, 128, 512], F32, kind="ExternalOutput").ap()

    p = Prog(nc)
    ysb = p.sb("ysb", [128, 2, 32, 512], BF16)
    gsb = [p.sb("gsb%d" % i, [128, 2, 3, 512], BF16) for i in range(2)]
    msb = p.sb("msb", [128, 2, 16, 512], BF16)
    wst = [p.sb("wst%d" % i, [128, 32, 128], F32) for i in range(2)]
    wbf = [p.sb("wbf%d" % i, [128, 32, 128], BF16) for i in range(2)]
    xsb = [p.sb("xsb%d" % i, [128, 512], F32) for i in range(2)]
    osb = [p.sb("osb%d" % i, [128, 512], F32) for i in range(2)]
    t0 = [p.sb("t0_%d" % i, [128, 512], F32) for i in range(2)]
    t1 = [p.sb("t1_%d" % i, [128, 512], F32) for i in range(2)]
    gt = p.sb("gt", [128, 16], F32)
    PS = [p.ps("ps%d" % i, [128, 512]) for i in range(8)]

    p.dma("sp", gt[:], gate, "gt", writes=["gt"])
    KB = [(0, 8), (8, 8), (16, 16)]
    cnt = 0
    NH2 = 2 if NG >= 2 else 1
    for gp in range(NG // NH2):
        for hf in range(NH2):
            p.dma("sp", ysb[:, hf], yT[gp * NH2 + hf], "ysb", writes=["ysb"])
        for nn in range(16):
            wi = cnt % 2
            cnt += 1
            p.dma("sp", wst[wi][:], w_o[nn], "wst%d" % wi, writes=["wst%d" % wi])
            p.op("pool", lambda e, wi=wi: e.tensor_copy(out=wbf[wi][:], in_=wst[wi][:]),
                 reads=["wst%d" % wi], writes=["wbf%d" % wi])
            for hf in range(NH2):
                p.dma("sp", gsb[wi][:, hf], gT[gp * NH2 + hf, nn], "gsb%d" % wi, writes=["gsb%d" % wi])
            for hf in range(NH2):
                pss = [PS[hf * 3 + i] for i in range(3)]
                pkeys = ["ps%d" % (hf * 3 + i) for i in range(3)]
                for i, (k0, nk) in enumerate(KB):
                    for kk in range(nk):
                        kc = k0 + kk
                        p.op("pe", lambda e, i=i, kc=kc, kk=kk, nk=nk, wi=wi, pss=pss, hf=hf: e.matmul(
                            pss[i][:], lhsT=wbf[wi][:, kc, :], rhs=ysb[:, hf, kc, :], start=(kk == 0), stop=(kk == nk - 1)),
                             reads=["wbf%d" % wi, "ysb"], writes=[pkeys[i]], track=(kk == nk - 1))
                a = t0[hf]
                b = t1[hf]
                ak = "t0_%d" % hf
                bk = "t1_%d" % hf
                p.op("dve", lambda e, a=a, wi=wi, pss=pss, hf=hf: e.tensor_tensor(out=a[:], in0=pss[0][:], in1=gsb[wi][:, hf, 0, :], op=ALU.mult),
                     reads=[pkeys[0], "gsb%d" % wi], writes=[ak])
                p.op("dve", lambda e, b=b, wi=wi, pss=pss, hf=hf: e.tensor_tensor(out=b[:], in0=pss[1][:], in1=gsb[wi][:, hf, 1, :], op=ALU.mult),
                     reads=[pkeys[1], "gsb%d" % wi], writes=[bk])
                p.op("pool", lambda e, a=a, b=b: e.tensor_tensor(out=a[:], in0=a[:], in1=b[:], op=ALU.add),
                     reads=[ak, bk], writes=[ak])
                p.op("dve", lambda e, b=b, wi=wi, pss=pss, hf=hf: e.tensor_tensor(out=b[:], in0=pss[2][:], in1=gsb[wi][:, hf, 2, :], op=ALU.mult),
                     reads=[pkeys[2], "gsb%d" % wi], writes=[bk])
                p.op("pool", lambda e, a=a, b=b, nn=nn, hf=hf: e.tensor_tensor(out=msb[:, hf, nn, :], in0=a[:], in1=b[:], op=ALU.add),
                     reads=[ak, bk], writes=["msb"])
        for mc in range(16):
            wi = cnt % 2
            cnt += 1
            p.dma("sp", wst[wi][:, 0:16, :], w_out[mc], "wst%d" % wi, writes=["wst%d" % wi])
            p.op("pool", lambda e, wi=wi: e.tensor_copy(out=wbf[wi][:, 0:16, :], in_=wst[wi][:, 0:16, :]),
                 reads=["wst%d" % wi], writes=["wbf%d" % wi])
            for hf in range(NH2):
                g = gp * NH2 + hf
                xi = hf
                p.dma("sp", xsb[xi][:], xT[g, mc], "xsb%d" % xi, writes=["xsb%d" % xi])
                ps = PS[6 + hf]
                pk = "ps%d" % (6 + hf)
                for kc in range(16):
                    p.op("pe", lambda e, kc=kc, wi=wi, ps=ps, hf=hf: e.matmul(ps[:], lhsT=wbf[wi][:, kc, :], rhs=msb[:, hf, kc, :],
                                                                            start=(kc == 0), stop=(kc == 15)),
                         reads=["wbf%d" % wi, "msb"], writes=[pk], track=(kc == 15))
                p.op("dve", lambda e, xi=xi, ps=ps, mc=mc: e.scalar_tensor_tensor(
                    out=osb[xi][:], in0=ps[:], scalar=gt[:, mc:mc + 1], in1=xsb[xi][:], op0=ALU.mult, op1=ALU.add),
                     reads=[pk, "gt", "xsb%d" % xi], writes=["osb%d" % xi])
                p.dma("pool", x1T[g, mc], osb[xi][:], "osb%d" % xi, reads=["osb%d" % xi])
    p.finish()
    return nc


def fm(v, n):
    return np.ascontiguousarray(v.reshape(n, 128).T)

def core_cols(g):
    idx = []
    def r(f, off, n):
        s, _ = FAM[f]
        idx.extend(range(s + off, s + off + n))
    r("fq", 256 * g, 256); r("fk", 256 * g, 256); r("fv", 256 * g, 256); r("fg", 256 * g, 256)
    r("dq", 256 * g, 256); r("dk", 128 * (g % 2), 128); r("dv", 128 * (g % 2), 128)
    r("iq", 256 * g, 256); r("dg", 256 * g, 256); r("sz", 512 * g, 512); r("sxbc", 768 * g, 768)
    r("mg", 1536 * g, 1536); r("ik", 0, 64); r("iw", 0, 16); r("ff", 0, 8); r("sdt", 0, 32)
    return np.array(idx, dtype=np.int64)

def rowp_for(inp, l):
    theta = 500000.0
    if16 = (theta ** (-np.arange(16, dtype=np.float32) / 16)).astype(np.float32)
    if8 = (theta ** (-np.arange(8, dtype=np.float32) / 8)).astype(np.float32)
    rowp = np.zeros((1, NRP), np.float32)
    rowp[0, 0:128] = inp["fox_q_norm"][l]; rowp[0, 128:256] = inp["fox_k_norm"][l]
    rowp[0, 256:384] = inp["dsa_q_norm"][l]; rowp[0, 384:512] = inp["dsa_k_norm"][l]
    rowp[0, 512:520] = inp["b_fox_f"][l]; rowp[0, 520:552] = inp["dt_bias"][l]
    rowp[0, 552:568] = if16; rowp[0, 568:576] = if8; rowp[0, 576:592] = if16; rowp[0, 592:600] = if8
    return rowp

def p1_maps(xT_b, inp, l, ntg=4):
    rowp = rowp_for(inp, l)
    maps = []
    for core in range(8):
        b, g = core // 4, core % 4
        cols = core_cols(g)
        ntok = ntg * T1
        posc = inp["positions"][b, :ntok].astype(np.int32)
        maps.append(dict(
            xT=np.ascontiguousarray(xT_b[b][:, :ntok]), cvec=fm(inp["c"][b], 16), w_ada=inp["w_ada"][l],
            b_ada=fm(inp["b_ada"][l], 48), norm_w=fm(inp["norm_w"][l], 16),
            w_in=np.ascontiguousarray(inp["w_in"][l][:, cols]), rowp=rowp,
            b_merge=np.ascontiguousarray(inp["b_merge"][l].reshape(1, 3 * D)[:, g * 1536:(g + 1) * 1536]),
            pos=np.ascontiguousarray(posc.reshape(ntok // 128, 128).T)))
    return maps


def p3_maps_one(yT, gT, xT, w_o, w_out, gate):
    TT = yT.shape[1]; NG = TT // 512
    y4 = np.ascontiguousarray(yT.reshape(32, 128, NG, 512).transpose(2, 1, 0, 3)).astype(BF)
    g4 = np.ascontiguousarray(gT.reshape(3, 16, 128, NG, 512).transpose(3, 1, 2, 0, 4)).astype(BF)
    x4 = np.ascontiguousarray(xT.reshape(16, 128, NG, 512).transpose(2, 0, 1, 3)).astype(np.float32)
    return dict(yT=y4, gT=g4, xT=x4, w_o=w_o, w_out=w_out, gate=gate)

def p3_weights(inp, l):
    w_o = np.concatenate([inp["w_o_fox"][l], inp["w_o_dsa"][l], inp["w_o_ssd"][l]], 0)
    w_o4 = np.ascontiguousarray(w_o.reshape(32, 128, 16, 128).transpose(2, 1, 0, 3))
    w_out4 = np.ascontiguousarray(inp["w_out"][l].reshape(16, 128, 16, 128).transpose(2, 1, 0, 3))
    return w_o4, w_out4

def p3_unpack(x1):
    NG = x1.shape[0]
    return np.ascontiguousarray(x1.transpose(1, 2, 0, 3).reshape(2048, NG * 512))

def fox_v_layout(v_tok, NH):
    SQ = v_tok.shape[0]
    return np.ascontiguousarray(v_tok.reshape(SQ // 128, 128, NH, 128).transpose(2, 1, 0, 3))


_NC_CACHE = {}


def _get_nc(name, builder):
    if name not in _NC_CACHE:
        _NC_CACHE[name] = builder()
    return _NC_CACHE[name]


_LAUNCH_LOG = []


def _run(nc, maps, tag=""):
    res = run_bass_kernel_spmd(nc, maps, core_ids=list(range(8)))
    et = getattr(res, "exec_time_ns", None)
    _LAUNCH_LOG.append((tag, et))
    print("[kernel] launch %s exec_time_ns=%s" % (tag, et), flush=True)
    return res.results


def _f32(a):
    return np.asarray(a).astype(np.float32)


def ssd_maps(xbc_tok, conv_w, conv_b, dt_tok, a_log, d_skip, ssd_norm_g, sz_tok):
    SQ = xbc_tok.shape[0]
    xbcT = np.ascontiguousarray(xbc_tok.T.reshape(6, 128, SQ).transpose(1, 0, 2)).astype(BF)
    convw = np.ascontiguousarray(conv_w.T.reshape(6, 128, 4).transpose(1, 0, 2)).astype(np.float32)
    convb = np.ascontiguousarray(conv_b.reshape(6, 128).T).astype(np.float32)
    dtv = np.ascontiguousarray(dt_tok.reshape(SQ // 128, 128, 8).transpose(1, 0, 2)).astype(np.float32)
    rowc = np.concatenate([a_log, d_skip, ssd_norm_g]).reshape(1, -1).astype(np.float32)
    szm = np.ascontiguousarray(sz_tok.reshape(SQ // 128, 128, 512).transpose(1, 0, 2)).astype(BF)
    tri = np.triu(np.ones((128, 128), np.float32))
    cst = np.ascontiguousarray(np.stack([tri, (tri - 1) * 30000.0, np.eye(128, dtype=np.float32)], 1)).astype(np.float32)
    return dict(xbcT=xbcT, convw=convw, convb=convb, dtv=dtv, rowc=rowc, sz=szm, cst=cst)


def dsa_maps(j, NI, dq, dk, dv, iq, ik, iw, dg):
    NS = 2 * NI
    SK = NI * 1024
    qts = []
    for i in range(NI):
        qts += [8 * i + j, 8 * i + 7 - j]

    def qlay(a, qt):
        t = a[qt * 128:(qt + 1) * 128].reshape(128, 2, 4, 128)
        return np.ascontiguousarray(t.transpose(3, 1, 2, 0).reshape(128, 2, 512))
    dqT = np.stack([qlay(dq, qt) for qt in qts]).astype(BF)
    dgT = np.stack([qlay(dg, qt) for qt in qts]).astype(BF)
    dkT = np.ascontiguousarray(dk[:SK].transpose(2, 1, 0)).astype(BF)
    dvm = np.ascontiguousarray(dv[:SK].reshape(SK // 128, 128, 256).transpose(1, 0, 2)).astype(BF)

    def iqlay(qt):
        t = iq[qt * 128:(qt + 1) * 128].reshape(128, 8, 2, 64)
        return np.ascontiguousarray(t.transpose(2, 3, 1, 0).reshape(128, 8, 128))
    iqT = np.stack([iqlay(qt) for qt in qts]).astype(BF)
    ikT = ik[:SK].T
    ikT2 = np.ascontiguousarray(np.concatenate([ikT, ikT], 0)).astype(BF)
    iwm = np.ascontiguousarray(np.stack([iw[qt * 128:(qt + 1) * 128] for qt in qts], 1)).astype(np.float32)
    cm = np.zeros((NS, 128, 1024), np.float32)
    for sl, qt in enumerate(qts):
        i = sl // 2
        s = 8 * i * 128 + np.arange(1024)[None, :]
        t = qt * 128 + np.arange(128)[:, None]
        cm[sl] = np.where(s <= t, 0.0, -1e30)
    return dict(dqT=dqT, dgT=dgT, dkT=dkT, dv=dvm, iqT=iqT, ikT2=ikT2, iw=iwm, cmask=cm,
                ident=np.eye(128, dtype=np.float32).astype(BF)), qts


def run_layer(xT_b, inp, l):
    S = 8192
    nc1 = _get_nc("p1", lambda: build_p1(9, None, 4))
    rA = _run(nc1, p1_maps(xT_b, inp, l, ntg=4), "P1")
    tri_bf = np.triu(np.ones((128, 128), np.float32)).astype(BF)
    mapsB = []
    for core in range(8):
        b, g = core // 4, core % 4
        r = rA[core]
        small = _f32(rA[b * 4]["o_small"])
        q = np.asarray(r["o_fq"]).reshape(S, 2, 128)
        k = np.asarray(r["o_fk"]).reshape(S, 2, 128)
        mapsB.append(dict(
            qT=np.ascontiguousarray(q.transpose(1, 2, 0)), kT=np.ascontiguousarray(k.transpose(1, 2, 0)),
            v=fox_v_layout(np.asarray(r["o_fv"]), 2), logf=np.ascontiguousarray(small[:, 80 + 2 * g:82 + 2 * g].T),
            fgT=np.ascontiguousarray(np.asarray(r["o_fg"]).T), tri=tri_bf))
    ncB = _get_nc("fox", lambda: build_fox(2, 16))
    rB = _run(ncB, mapsB, "FoX")
    per_b = []
    for b in range(2):
        rs = [rA[b * 4 + g] for g in range(4)]
        small = _f32(rs[0]["o_small"])
        d = dict(
            dq=np.concatenate([_f32(r["o_dq"]).reshape(S, 2, 128) for r in rs], 1),
            dk=np.stack([_f32(rs[0]["o_dkv"])[:, 0:128], _f32(rs[1]["o_dkv"])[:, 0:128]], 1),
            dv=np.stack([_f32(rs[0]["o_dkv"])[:, 128:256], _f32(rs[1]["o_dkv"])[:, 128:256]], 1),
            iq=np.concatenate([_f32(r["o_iq"]).reshape(S, 4, 64) for r in rs], 1),
            dg=np.concatenate([_f32(r["o_dg"]).reshape(S, 2, 128) for r in rs], 1),
            sz=np.concatenate([_f32(r["o_sz"]) for r in rs], 1),
            xbc=np.concatenate([_f32(r["o_xbc"]) for r in rs], 1),
            mg=np.concatenate([np.asarray(r["o_mg"]) for r in rs], 1),
            ik=small[:, 0:64], iw=small[:, 64:80], dt=small[:, 88:120], gate=_f32(rs[0]["o_gate"]))
        per_b.append(d)
    mapsC = []
    for core in range(8):
        b, g = core // 4, core % 4
        d = per_b[b]
        ch = np.concatenate([np.arange(512 * g, 512 * g + 512), 2048 + np.arange(128 * g, 128 * g + 128),
                             2560 + np.arange(128 * g, 128 * g + 128)])
        mapsC.append(ssd_maps(d["xbc"][:, ch], inp["conv_w"][l][:, ch], inp["conv_b"][l][ch], d["dt"][:, 8 * g:8 * g + 8],
                              inp["a_log"][l][8 * g:8 * g + 8], inp["d_skip"][l][8 * g:8 * g + 8],
                              inp["ssd_norm"][l][512 * g:512 * g + 512], d["sz"][:, 512 * g:512 * g + 512]))
    ncC = _get_nc("ssd", lambda: build_ssd(16))
    rC = _run(ncC, mapsC, "SSD")
    mapsD, qtsD = [], []
    for core in range(8):
        b, j = core // 4, core % 4
        d = per_b[b]
        m, qts = dsa_maps(j, 8, d["dq"], d["dk"], d["dv"], d["iq"], d["ik"], d["iw"], d["dg"])
        mapsD.append(m)
        qtsD.append(qts)
    ncD = _get_nc("dsa", lambda: build_dsa(8))
    rD = _run(ncD, mapsD, "DSA")
    yT_b = []
    for b in range(2):
        yT = np.zeros((4096, S), dtype=BF)
        for g in range(4):
            yT[256 * g:256 * g + 256] = np.asarray(rB[b * 4 + g]["yT"])
            ys = np.asarray(rC[b * 4 + g]["y"]).transpose(1, 0, 2).reshape(S, 512)
            yT[2048 + 512 * g:2048 + 512 * g + 512] = ys.T
            yd = np.asarray(rD[b * 4 + g]["yT"])
            for sl, qt in enumerate(qtsD[b * 4 + g]):
                blk = yd[sl].reshape(128, 2, 4, 128).transpose(1, 2, 0, 3).reshape(1024, 128)
                yT[1024:2048, qt * 128:(qt + 1) * 128] = blk
        yT_b.append(yT)
    w_o4, w_out4 = p3_weights(inp, l)
    mapsE = []
    for core in range(8):
        b, q = core // 4, core % 4
        tsl = slice(q * 2048, (q + 1) * 2048)
        mapsE.append(p3_maps_one(np.ascontiguousarray(yT_b[b][:, tsl]), np.ascontiguousarray(per_b[b]["mg"][tsl].T),
                                 np.ascontiguousarray(xT_b[b][:, tsl]), w_o4, w_out4, per_b[b]["gate"]))
    ncE = _get_nc("p3", lambda: build_p3(4))
    rE = _run(ncE, mapsE, "P3")
    new = []
    for b in range(2):
        new.append(np.concatenate([p3_unpack(np.asarray(rE[b * 4 + q]["x1T"])) for q in range(4)], 1))
    return new


def kernel(**inputs):
    inp = {k: np.asarray(v) for k, v in inputs.items()}
    xT_b = [np.ascontiguousarray(inp["x"][b].T).astype(np.float32) for b in range(2)]
    for l in range(2):
        xT_b = run_layer(xT_b, inp, l)
    out = np.stack([np.ascontiguousarray(xT_b[b].T) for b in range(2)]).astype(np.float32)
    return out
```
